# Optimizing a Trainium2 kernel written in Bass

```python
import jax
import jax.numpy as jnp
from jax import lax
import numpy as np


D_MODEL = 1024
BATCH = 4
SEQ = 8192
DEPTH = 1

N_META = 16
ATT_HEADS = 8
ATT_KV_HEADS = 2
HEAD_DIM = 64
ATT_GROUP = ATT_HEADS // ATT_KV_HEADS
WINDOW = 128
BLOCK = 128
ROPE_THETA = 500000.0
ROT_DIM = HEAD_DIM // 4
ATT_Q = ATT_HEADS * HEAD_DIM
ATT_KV = ATT_KV_HEADS * HEAD_DIM
MASK_VALUE = -1e30
RWKV_HEADS = 8
RWKV_HEAD = 64
RWKV_WIDTH = RWKV_HEADS * RWKV_HEAD
DECAY_LORA = 64
AAA_LORA = 64
GATE_LORA = 128
GN_EPS = 64e-5
RWKV_IN = 3 * RWKV_WIDTH + DECAY_LORA + AAA_LORA + GATE_LORA
IN_COLS = ATT_Q + 2 * ATT_KV + RWKV_IN + 2 * D_MODEL
D_FF = 2816
NORM_EPS = 1e-6

kernel_name = 'hybrid_swa_sink_rwkv7_macaron_meta'


def rms_norm(x, gain):
    xf = x.astype(jnp.float32)
    y = xf * lax.rsqrt(jnp.mean(xf * xf, axis=-1, keepdims=True) + NORM_EPS)
    return (y * gain.astype(jnp.float32)).astype(x.dtype)


def swiglu(x, w_gate_up, w_down):
    gate, up = jnp.split(x @ w_gate_up, 2, axis=-1)
    return (jax.nn.silu(gate) * up) @ w_down


def partial_rope(x, pos):
    half = ROT_DIM // 2
    inv_freq = 1.0 / (ROPE_THETA ** (jnp.arange(half, dtype=jnp.float32) * (2.0 / ROT_DIM)))
    ang = pos.astype(jnp.float32)[:, None] * inv_freq[None, :]
    cos = jnp.cos(ang)[None, :, None, :]
    sin = jnp.sin(ang)[None, :, None, :]
    xr = x[..., :ROT_DIM].astype(jnp.float32)
    x1, x2 = xr[..., :half], xr[..., half:]
    rot = jnp.concatenate([x1 * cos - x2 * sin, x2 * cos + x1 * sin], axis=-1).astype(x.dtype)
    return jnp.concatenate([rot, x[..., ROT_DIM:]], axis=-1)


def sliding_window_gqa_sinks(q, k, v, sinks):
    B, T = q.shape[0], q.shape[1]
    pad = (-T) % BLOCK
    Tp = T + pad
    nb = Tp // BLOCK
    padw = ((0, 0), (pad, 0), (0, 0), (0, 0))
    qb = jnp.pad(q, padw).reshape(B, nb, BLOCK, ATT_KV_HEADS, ATT_GROUP, HEAD_DIM)
    kb = jnp.pad(k, padw).reshape(B, nb, BLOCK, ATT_KV_HEADS, HEAD_DIM)
    vb = jnp.pad(v, padw).reshape(B, nb, BLOCK, ATT_KV_HEADS, HEAD_DIM)
    kw = jnp.concatenate([jnp.concatenate([jnp.zeros_like(kb[:, :1]), kb[:, :-1]], axis=1), kb], axis=2)
    vw = jnp.concatenate([jnp.concatenate([jnp.zeros_like(vb[:, :1]), vb[:, :-1]], axis=1), vb], axis=2)
    s = jnp.einsum('bnqhgd,bnkhd->bnhgqk', qb, kw, preferred_element_type=jnp.float32)
    s = s * (HEAD_DIM ** -0.5)
    blk = jnp.arange(nb)[:, None, None]
    q_idx = blk * BLOCK + jnp.arange(BLOCK)[None, :, None]
    k_idx = (blk - 1) * BLOCK + jnp.arange(2 * BLOCK)[None, None, :]
    rel = q_idx - k_idx
    mask = (rel >= 0) & (rel < WINDOW) & (k_idx >= pad)
    s = jnp.where(mask[None, :, None, None], s, MASK_VALUE)
    sink = jnp.broadcast_to(
        sinks.astype(jnp.float32).reshape(ATT_KV_HEADS, ATT_GROUP)[None, None, :, :, None, None],
        s.shape[:-1] + (1,))
    p = jax.nn.softmax(jnp.concatenate([s, sink], axis=-1), axis=-1)[..., :-1]
    o = jnp.einsum('bnhgqk,bnkhd->bnqhgd', p.astype(v.dtype), vw)
    return o.reshape(B, Tp, ATT_Q)[:, pad:]


def token_shift(z, mu):
    z_prev = jnp.pad(z, ((0, 0), (1, 0), (0, 0)))[:, :-1]
    return z + (z_prev - z) * mu


def wkv7_scan(r, w, k, v, a, b):
    B, T, H, N = r.shape

    def step(S, inp):
        r_t, w_t, k_t, v_t, a_t, b_t = inp
        sa = jnp.einsum('bhvk,bhk->bhv', S, a_t)
        S = S * w_t[:, :, None, :] + sa[..., None] * b_t[:, :, None, :] + v_t[..., None] * k_t[:, :, None, :]
        return S, jnp.einsum('bhvk,bhk->bhv', S, r_t)

    xs = (jnp.moveaxis(r, 1, 0), jnp.moveaxis(w, 1, 0), jnp.moveaxis(k, 1, 0),
          jnp.moveaxis(v, 1, 0), jnp.moveaxis(a, 1, 0), jnp.moveaxis(b, 1, 0))
    S0 = jnp.zeros((B, H, N, N), jnp.float32)
    _, ys = lax.scan(step, S0, xs)
    return jnp.moveaxis(ys, 0, 1)


def rwkv7_time_mix(z, mu, w0, w2, a0, a2, g2, k_k, k_a, r_k, ln_w, ln_b):
    B, T = z.shape[0], z.shape[1]
    f32 = jnp.float32
    z = token_shift(z, mu).astype(f32)
    o1 = RWKV_WIDTH
    o4 = 3 * RWKV_WIDTH + DECAY_LORA
    r, k, v, xw, xa, xg = jnp.split(z, [o1, 2 * o1, 3 * o1, o4, o4 + AAA_LORA], axis=-1)
    w = -jax.nn.softplus(-(w0.astype(f32) + jnp.tanh(xw) @ w2.astype(f32))) - 0.5
    a = jax.nn.sigmoid(a0.astype(f32) + xa @ a2.astype(f32))
    g = jax.nn.sigmoid(xg) @ g2.astype(f32)
    hs = (B, T, RWKV_HEADS, RWKV_HEAD)
    kk = (k * k_k.astype(f32)).reshape(hs)
    kk = kk / jnp.maximum(jnp.sqrt(jnp.sum(kk * kk, axis=-1, keepdims=True)), 1e-12)
    k = k * (1.0 + (a - 1.0) * k_a.astype(f32))
    decay = jnp.exp(-jnp.exp(w))
    r_h, k_h, v_h, a_h = r.reshape(hs), k.reshape(hs), v.reshape(hs), a.reshape(hs)
    y = wkv7_scan(r_h, decay.reshape(hs), k_h, v_h, -kk, kk * a_h)
    mean = jnp.mean(y, axis=-1, keepdims=True)
    var = jnp.mean(jnp.square(y - mean), axis=-1, keepdims=True)
    y = ((y - mean) * lax.rsqrt(var + GN_EPS)).reshape(B, T, RWKV_WIDTH)
    y = y * ln_w.astype(f32) + ln_b.astype(f32)
    bonus = jnp.sum(r_h * k_h * r_k.astype(f32), axis=-1, keepdims=True) * v_h
    return ((y + bonus.reshape(B, T, RWKV_WIDTH)) * g).astype(mu.dtype)


def setup_inputs(seed: int = 0) -> dict:
    key = jax.random.key(seed)
    ks = jax.random.split(key, 32)
    f32 = jnp.float32

    def nrm(k, shape, scale):
        return jax.random.normal(k, shape, f32) * scale

    def gain(k, n):
        return 1.0 + 0.1 * jax.random.normal(k, (DEPTH, n), f32)

    return {
        'x': nrm(ks[0], (BATCH, SEQ, D_MODEL), 1.0),
        'meta_tokens': nrm(ks[1], (N_META, D_MODEL), 1.0),
        'ffn1_norm_pre': gain(ks[2], D_MODEL),
        'ffn1_w_gate_up': nrm(ks[3], (DEPTH, D_MODEL, 2 * D_FF), D_MODEL ** -0.5),
        'ffn1_w_down': nrm(ks[4], (DEPTH, D_FF, D_MODEL), D_FF ** -0.5),
        'ffn1_norm_post': gain(ks[5], D_MODEL),
        'mix_norm_pre': gain(ks[6], D_MODEL),
        'w_in': nrm(ks[7], (DEPTH, D_MODEL, IN_COLS), D_MODEL ** -0.5),
        'att_sinks': nrm(ks[8], (DEPTH, ATT_HEADS), 0.5),
        'rwkv_mu': jax.random.uniform(ks[9], (DEPTH, RWKV_IN), f32, 0.0, 1.0),
        'rwkv_w0': jax.random.uniform(ks[10], (DEPTH, RWKV_WIDTH), f32, -4.0, 1.0),
        'rwkv_w2': nrm(ks[11], (DEPTH, DECAY_LORA, RWKV_WIDTH), DECAY_LORA ** -0.5),
        'rwkv_a0': nrm(ks[12], (DEPTH, RWKV_WIDTH), 0.1),
        'rwkv_a2': nrm(ks[13], (DEPTH, AAA_LORA, RWKV_WIDTH), AAA_LORA ** -0.5),
        'rwkv_g2': nrm(ks[14], (DEPTH, GATE_LORA, RWKV_WIDTH), GATE_LORA ** -0.5),
        'rwkv_k_k': 0.85 + nrm(ks[15], (DEPTH, RWKV_WIDTH), 0.05),
        'rwkv_k_a': 1.0 + nrm(ks[16], (DEPTH, RWKV_WIDTH), 0.05),
        'rwkv_r_k': nrm(ks[17], (DEPTH, RWKV_HEADS, RWKV_HEAD), 0.1),
        'rwkv_ln_w': gain(ks[18], RWKV_WIDTH),
        'rwkv_ln_b': nrm(ks[19], (DEPTH, RWKV_WIDTH), 0.02),
        'w_att_branch': nrm(ks[20], (DEPTH, ATT_Q, D_MODEL), ATT_Q ** -0.5),
        'w_rwkv_branch': nrm(ks[21], (DEPTH, RWKV_WIDTH, D_MODEL), RWKV_WIDTH ** -0.5),
        'w_mix_out': nrm(ks[22], (DEPTH, D_MODEL, D_MODEL), D_MODEL ** -0.5),
        'mix_norm_post': gain(ks[23], D_MODEL),
        'ffn2_norm_pre': gain(ks[24], D_MODEL),
        'ffn2_w_gate_up': nrm(ks[25], (DEPTH, D_MODEL, 2 * D_FF), D_MODEL ** -0.5),
        'ffn2_w_down': nrm(ks[26], (DEPTH, D_FF, D_MODEL), D_FF ** -0.5),
        'ffn2_norm_post': gain(ks[27], D_MODEL),
    }


def reference(x, meta_tokens, ffn1_norm_pre, ffn1_w_gate_up, ffn1_w_down, ffn1_norm_post,
              mix_norm_pre, w_in, att_sinks, rwkv_mu, rwkv_w0, rwkv_w2, rwkv_a0, rwkv_a2,
              rwkv_g2, rwkv_k_k, rwkv_k_a, rwkv_r_k, rwkv_ln_w, rwkv_ln_b, w_att_branch,
              w_rwkv_branch, w_mix_out, mix_norm_post, ffn2_norm_pre, ffn2_w_gate_up,
              ffn2_w_down, ffn2_norm_post):
    B = x.shape[0]
    meta = jnp.broadcast_to(meta_tokens.astype(x.dtype)[None], (B, N_META, D_MODEL))
    h = jnp.concatenate([meta, x], axis=1)
    T = h.shape[1]
    pos = jnp.arange(T, dtype=jnp.int32)
    c_q = ATT_Q
    c_k = c_q + ATT_KV
    c_v = c_k + ATT_KV
    c_r = c_v + RWKV_IN
    c_ga = c_r + D_MODEL
    for l in range(DEPTH):
        f = swiglu(rms_norm(h, ffn1_norm_pre[l]), ffn1_w_gate_up[l], ffn1_w_down[l])
        h = h + 0.5 * rms_norm(f, ffn1_norm_post[l])
        u = rms_norm(h, mix_norm_pre[l])
        z = u @ w_in[l]
        q, k, v, zr, ga, gr = jnp.split(z, [c_q, c_k, c_v, c_r, c_ga], axis=-1)
        q = partial_rope(q.reshape(B, T, ATT_HEADS, HEAD_DIM), pos)
        k = partial_rope(k.reshape(B, T, ATT_KV_HEADS, HEAD_DIM), pos)
        v = v.reshape(B, T, ATT_KV_HEADS, HEAD_DIM)
        y_att = sliding_window_gqa_sinks(q, k, v, att_sinks[l]) @ w_att_branch[l]
        y_rwkv = rwkv7_time_mix(zr, rwkv_mu[l], rwkv_w0[l], rwkv_w2[l], rwkv_a0[l], rwkv_a2[l],
                                rwkv_g2[l], rwkv_k_k[l], rwkv_k_a[l], rwkv_r_k[l],
                                rwkv_ln_w[l], rwkv_ln_b[l]) @ w_rwkv_branch[l]
        merged = jax.nn.sigmoid(ga) * y_att + jax.nn.sigmoid(gr) * y_rwkv
        h = h + rms_norm(merged @ w_mix_out[l], mix_norm_post[l])
        f = swiglu(rms_norm(h, ffn2_norm_pre[l]), ffn2_w_gate_up[l], ffn2_w_down[l])
        h = h + 0.5 * rms_norm(f, ffn2_norm_post[l])
    return h[:, N_META:]
```

```python
import os
import numpy as np
import concourse.bass as bass
import concourse.mybir as mybir
from concourse.bass_utils import run_bass_kernel_spmd

F32 = mybir.dt.float32
BF16 = mybir.dt.bfloat16
AF = mybir.ActivationFunctionType
ALU = mybir.AluOpType
AX = mybir.AxisListType

D = 1024
DFF = 2816
NKC = 8
NHC = 22
SEQ = 8192
NMETA = 16
SUBS = 2
NT = 128 * SUBS
FRONT = 240
NSUP_FULL = 33
NPRE = 16
NMAIN = 17
EPS = 1e-6
GN_EPS = 64e-5
CDEC = float(np.exp(-0.5))
WSLOT = 22 * 128
NWSLOT = 4

IQ, IQP, IKA, IKPA, IKB, IKPB, IV, IRW, IGA, IGR = 0, 4, 8, 9, 10, 11, 12, 13, 27, 35
NIN = 43


class Buf:
    __slots__ = ("name", "w", "r")

    def __init__(self, name):
        self.name = name
        self.w = None
        self.r = []


class Prog:
    def __init__(self, nc, ctx):
        self.nc = nc
        self.ctx = ctx
        self.engs = {"pe": nc.tensor, "act": nc.scalar, "dve": nc.vector, "pool": nc.gpsimd, "sp": nc.sync}
        self.sems = {}
        self.cnt = {}
        self.recs = {k: [] for k in self.engs}
        self.seen = {k: {} for k in self.engs}
        for k in self.engs:
            self.sems[k] = ctx.enter_context(nc.semaphore("s_" + k))
            self.cnt[k] = 0
        self.ndma = 0

    def dma_sem(self, name):
        key = "dma_" + name
        if key not in self.sems:
            self.sems[key] = self.ctx.enter_context(self.nc.semaphore(key))
            self.cnt[key] = 0
        return key

    def _deps(self, eng, reads, writes):
        need = {}
        for b in reads:
            if b.w is not None:
                k, v = b.w
                need[k] = max(need.get(k, 0), v)
        for b in writes:
            if b.w is not None:
                k, v = b.w
                need[k] = max(need.get(k, 0), v)
            for (k, v) in b.r:
                need[k] = max(need.get(k, 0), v)
        waits = []
        seen = self.seen[eng]
        for k, v in need.items():
            if k == eng and eng == "pe":
                continue
            if seen.get(k, 0) < v:
                seen[k] = v
                waits.append((k, v))
        return waits

    def op(self, eng, fn, reads=(), writes=()):
        waits = self._deps(eng, reads, writes)
        self.cnt[eng] += 1
        v = self.cnt[eng]
        self.recs[eng].append((waits, fn, eng, 1))
        for b in reads:
            b.r.append((eng, v))
        for b in writes:
            b.w = (eng, v)
            b.r = []

    def dma(self, eng, semname, fn, reads=(), writes=()):
        key = self.dma_sem(semname)
        waits = self._deps(eng, reads, writes)
        self.cnt[key] += 16
        v = self.cnt[key]
        self.recs[eng].append((waits, fn, key, 16))
        for b in reads:
            b.r.append((key, v))
        for b in writes:
            b.w = (key, v)
            b.r = []

    def final_wait(self, eng, bufs):
        waits = self._deps(eng, bufs, bufs)
        self.recs[eng].append((waits, None, None, 0))

    def emit(self, block):
        prog = self

        def run(name, e):
            for waits, fn, key, inc in prog.recs[name]:
                for (k, v) in waits:
                    e.wait_ge(prog.sems[k], v)
                if fn is not None:
                    fn(e).then_inc(prog.sems[key], inc)

        @block.tensor
        def _(e):
            run("pe", e)

        @block.scalar
        def _(e):
            run("act", e)

        @block.vector
        def _(e):
            run("dve", e)

        @block.gpsimd
        def _(e):
            run("pool", e)

        @block.sync
        def _(e):
            run("sp", e)


class K:
    def __init__(self, nc, ctx, nsup, stage, dbg):
        self.nc = nc
        self.ctx = ctx
        self.P = Prog(nc, ctx)
        self.nsup = nsup
        self.stage = stage
        self.dbg = dbg
        self.bufs = {}
        self.wslot_i = 0
        self.ps_i = 0
        self.ps_ring = [0, 1, 2, 3, 4, 5]

    def sb(self, name, shape, dt):
        return self.ctx.enter_context(self.nc.sbuf_tensor(name, shape, dt))

    def B(self, name):
        if name not in self.bufs:
            self.bufs[name] = Buf(name)
        return self.bufs[name]


def build_program(npre=NPRE, nmain=NMAIN, stage=99, dbg=False):
    from contextlib import ExitStack
    nc = bass.Bass("TRN2", target_bir_lowering=False)
    nsup = npre + nmain
    ntok = nsup * NT

    def din(name, shape, dt=F32):
        return nc.dram_tensor(name, shape, dt, kind="ExternalInput").ap()

    xT = din("xT", [128, NKC, ntok])
    outT = nc.dram_tensor("outT", [128, NKC, nmain * NT], F32, kind="ExternalOutput").ap()
    wsrc = {
        "gu1": din("gu1", [NHC, 128, 2 * NKC * 128]),
        "d1": din("d1", [NKC, 128, NHC * 128]),
        "gu2": din("gu2", [NHC, 128, 2 * NKC * 128]),
        "d2": din("d2", [NKC, 128, NHC * 128]),
        "win": din("win", [NIN, 128, NKC * 128]),
        "watt": din("watt", [NKC, 128, 4 * 128]),
        "wrw": din("wrw", [NKC, 128, 4 * 128]),
        "wmix": din("wmix", [NKC, 128, NKC * 128]),
    }
    gains = din("gains", [128, 6, NKC])
    rope = din("rope", [nmain, 128, 2, NT])
    masks = din("masks", [128, 6, 512])
    ident_in = din("ident", [128, 128])
    sinks_in = din("sinks", [128, 8])
    rwp_in = din("rwp", [128, 8, 4])
    mu_in = din("mu", [128, 14])
    lora_in_d = din("loraw", [128, 3, 512])
    bones_in = din("bones", [128, 128])
    wscr = {}
    for k, ap in wsrc.items():
        shp = list(ap.shape)
        wscr[k] = nc.dram_tensor("scr_" + k, shp, BF16, kind="Internal").ap()

    with ExitStack() as ctx:
        kb = K(nc, ctx, nsup, stage, dbg)
        kb.npre = npre
        P = kb.P
        B = kb.B
        sb = kb.sb
        block = ctx.enter_context(nc.Block())

        hT = sb("hT", [128, NKC, NT], F32)
        hT2 = sb("hT2", [128, NKC, NT], F32)
        xn = sb("xn", [128, NKC, NT], BF16)
        hid = sb("hid", [128, NHC, NT], BF16)
        fT = sb("fT", [128, NKC, NT], F32)
        sq = sb("sq", [128, NKC, NT], BF16)
        rstd = sb("rstd", [128, NT], F32)
        sgt = sb("sgt", [128, 2, NT], F32)
        gn = sb("gn", [128, 6, NKC], F32)
        gnh = sb("gnh", [128, 6, NKC], F32)
        ones_bf = sb("ones_bf", [128, 128], BF16)
        wring = sb("wring", [128, NWSLOT, WSLOT], BF16)
        psum = [ctx.enter_context(nc.psum_tensor("ps%d" % i, [128, 512], F32)) for i in range(8)]

        def ps_next():
            ring = kb.ps_ring
            i = ring[kb.ps_i % len(ring)]
            kb.ps_i += 1
            return psum[i], B("ps%d" % i)

        for k in wsrc:
            s, d = wsrc[k], wscr[k]
            n0 = s.shape[0]
            for i in range(n0):
                P.dma("pool", "cv_" + k, (lambda e, s=s, d=d, i=i: e.dma_start(out=d[i], in_=s[i])),
                      writes=[B("scr_" + k)] if i == n0 - 1 else [])
            B("scr_" + k).w = ("dma_cv_" + k, P.cnt["dma_cv_" + k])

        P.dma("sp", "c0", lambda e: e.dma_start(out=gn[:], in_=gains[:, :, :]), writes=[B("gn")])
        P.op("pool", lambda e: e.memset(ones_bf[:], 1.0), writes=[B("ones")])
        P.op("dve", lambda e: e.tensor_scalar(out=gnh[:], in0=gn[:], scalar1=0.5, scalar2=None, op0=ALU.mult),
             reads=[B("gn")], writes=[B("gnh")])

        def wload(name, idx, nel):
            slot = kb.wslot_i % NWSLOT
            kb.wslot_i += 1
            bslot = B("wslot%d" % slot)
            src = wscr[name]
            P.dma("sp", "w%d" % slot,
                  (lambda e, slot=slot, src=src, idx=idx, nel=nel:
                   e.dma_start(out=wring[:, slot, 0:nel], in_=src[idx])),
                  reads=[B("scr_" + name)], writes=[bslot])
            return wring[:, slot, :], bslot

        def rmsnorm_stats(src_tile, src_bufs, from_psum=None):
            for c in range(NKC):
                P.op("act", (lambda e, c=c: e.activation(out=sq[:, c, :], in_=src_tile[:, c, :], func=AF.Square)),
                     reads=[src_bufs[c]], writes=[B("sq%d" % c)])
            pt, pb = ps_next()
            for c in range(NKC):
                P.op("pe", (lambda e, c=c, pt=pt: e.matmul(pt[:, 0:NT], lhsT=ones_bf[:], rhs=sq[:, c, :],
                                                           start=(c == 0), stop=(c == NKC - 1))),
                     reads=[B("ones"), B("sq%d" % c)], writes=[pb])
            P.op("act", (lambda e, pt=pt: e.activation(out=rstd[:], in_=pt[:, 0:NT], func=AF.Ln,
                                                       scale=1.0 / D, bias=epsb[:, 0:1])),
                 reads=[pb, B("epsb")], writes=[B("rstd")])
            P.op("act", lambda e: e.activation(out=rstd[:], in_=rstd[:], func=AF.Exp, scale=-0.5),
                 reads=[B("rstd")], writes=[B("rstd")])

        epsb = sb("epsb", [128, 2], F32)
        P.op("pool", lambda e: e.memset(epsb[:, 0:1], EPS), writes=[B("epsb")])
        P.op("pool", lambda e: e.memset(epsb[:, 1:2], GN_EPS), writes=[B("epsb")])

        hb = [B("hT%d" % c) for c in range(NKC)]
        hb2 = [B("hT2_%d" % c) for c in range(NKC)]
        HT = [(hT, hb), (hT2, hb2)]
        fb = [B("fT%d" % c) for c in range(NKC)]
        xb = [B("xn%d" % c) for c in range(NKC)]

        def prenorm(gi, hsel=0):
            hT, hb = HT[hsel]
            rmsnorm_stats(hT, hb)
            for c in range(NKC):
                P.op("dve", (lambda e, c=c, hT=hT: e.scalar_tensor_tensor(
                    out=xn[:, c, :], in0=hT[:, c, :], scalar=gn[:, gi, c:c + 1], op0=ALU.mult,
                    in1=rstd[:], op1=ALU.mult)),
                    reads=[hb[c], B("gn"), B("rstd")], writes=[xb[c]])

        def postnorm_add(gi, half, hsel=0):
            hT, hb = HT[hsel]
            rmsnorm_stats(fT, fb)
            g = gnh if half else gn
            for c in range(NKC):
                P.op("dve", (lambda e, c=c: e.scalar_tensor_tensor(
                    out=fT[:, c, :], in0=fT[:, c, :], scalar=g[:, gi, c:c + 1], op0=ALU.mult,
                    in1=rstd[:], op1=ALU.mult)),
                    reads=[fb[c], B("gnh"), B("gn"), B("rstd")], writes=[fb[c]])
                P.op("pool", (lambda e, c=c, hT=hT: e.tensor_tensor(out=hT[:, c, :], in0=hT[:, c, :], in1=fT[:, c, :],
                                                             op=ALU.add)),
                     reads=[fb[c], hb[c]], writes=[hb[c]])

        def ffn(wgu, wd):
            hidb = [B("hid%d" % j) for j in range(NHC)]
            for j in range(NHC):
                w, wb = wload(wgu, j, 2 * NKC * 128)
                pg, pgb = ps_next()
                pu, pub = ps_next()
                for g, (pt, pbuf) in enumerate(((pg, pgb), (pu, pub))):
                    for kc in range(NKC):
                        off = (g * NKC + kc) * 128
                        P.op("pe", (lambda e, pt=pt, w=w, off=off, kc=kc: e.matmul(
                            pt[:, 0:NT], lhsT=w[:, off:off + 128], rhs=xn[:, kc, :],
                            start=(kc == 0), stop=(kc == NKC - 1))),
                            reads=[wb, xb[kc]], writes=[pbuf])
                s = j % 2
                P.op("act", (lambda e, pg=pg, s=s: e.activation(out=sgt[:, s, :], in_=pg[:, 0:NT], func=AF.Silu)),
                     reads=[pgb], writes=[B("sgt%d" % s)])
                P.op("dve", (lambda e, pu=pu, s=s, j=j: e.tensor_tensor(out=hid[:, j, :], in0=sgt[:, s, :],
                                                                       in1=pu[:, 0:NT], op=ALU.mult)),
                     reads=[B("sgt%d" % s), pub], writes=[hidb[j]])
                yield
            yield "UPDONE"
            for c in range(NKC):
                w, wb = wload(wd, c, NHC * 128)
                pt, pbuf = ps_next()
                for kc in range(NHC):
                    P.op("pe", (lambda e, pt=pt, w=w, kc=kc: e.matmul(
                        pt[:, 0:NT], lhsT=w[:, kc * 128:(kc + 1) * 128], rhs=hid[:, kc, :],
                        start=(kc == 0), stop=(kc == NHC - 1))),
                        reads=[wb, hidb[kc]], writes=[pbuf])
                P.op("act", (lambda e, pt=pt, c=c: e.activation(out=fT[:, c, :], in_=pt[:, 0:NT], func=AF.Copy)),
                     reads=[pbuf], writes=[fb[c]])
                yield

        T1s = sb("T1", [128, 19, NT], F32)
        ropet = sb("ropet", [128, 2, NT], F32)
        rtmp = sb("rtmp", [128, 2, 2, NT], F32)
        qrot = sb("qrot", [128, 4, NT], BF16)
        kT = [sb("kTa", [128, (1 + SUBS) * 128], BF16), sb("kTb", [128, (1 + SUBS) * 128], BF16)]
        vtok = sb("vtok", [128, 1 + SUBS, 2, 65], BF16)
        zr = sb("zr", [128, 14, NT + 1], F32)
        gates = sb("gates", [128, 16, NT], BF16)
        PT = sb("PT", [128, 16, 128], BF16)
        maskb = sb("maskb", [128, 6, 128], BF16)
        esink = sb("esink", [128, 8], F32)
        den = sb("den", [128, 2, 8], F32)
        yatt = sb("yatt", [128, 512], F32)
        yattT = sb("yattT", [128, 4, NT], BF16)
        ident = sb("ident_sb", [128, 128], F32)

        stg = T1s[:, 0:4, :].rearrange("p (a b) c -> p a (b c)", a=2)
        kb.stg_i = 0

        def load_cast(dst_ap, src_ap, dst_buf):
            k = kb.stg_i % 2
            kb.stg_i += 1
            P.dma("sp", "stg%d" % k, (lambda e: e.dma_start(out=stg[:, k, :], in_=src_ap)), writes=[B("stg%d" % k)])
            P.op("dve", (lambda e: e.tensor_copy(out=dst_ap, in_=stg[:, k, :])), reads=[B("stg%d" % k)],
                 writes=[dst_buf])

        for mv in range(6):
            k_ = kb.stg_i % 2
            kb.stg_i += 1
            P.dma("sp", "stg%d" % k_, (lambda e, k_=k_, mv=mv: e.dma_start(out=stg[:, k_, 0:128], in_=masks[:, mv, 0:128])),
                  writes=[B("stg%d" % k_)])
            P.op("dve", (lambda e, k_=k_, mv=mv: e.tensor_copy(out=maskb[:, mv, :], in_=stg[:, k_, 0:128])),
                 reads=[B("stg%d" % k_)], writes=[B("maskb")])
        P.dma("sp", "c2", lambda e: e.dma_start(out=ident[:], in_=ident_in[:, :]), writes=[B("ident")])
        P.dma("sp", "c3", lambda e: e.dma_start(out=esink[:], in_=sinks_in[:, :]), writes=[B("esink")])
        P.op("act", lambda e: e.activation(out=esink[:], in_=esink[:], func=AF.Exp), reads=[B("esink")],
             writes=[B("esink")])
        kslot = [[B("kT%d_%d" % (a, sl)) for sl in range(1 + SUBS)] for a in range(2)]
        vslot = [B("v_%d" % sl) for sl in range(1 + SUBS)]
        P.op("pool", lambda e: e.memset(kT[0][:], 0.0), writes=kslot[0])
        P.op("pool", lambda e: e.memset(kT[1][:], 0.0), writes=kslot[1])
        P.op("pool", lambda e: e.memset(vtok[:], 0.0), writes=vslot)
        P.op("pool", lambda e: e.memset(vtok[:, :, :, 64:65], 1.0), writes=vslot)
        P.op("pool", lambda e: e.memset(zr[:, :, 0:1], 0.0), writes=[B("zrp%d" % c) for c in range(14)])
        qb = [B("qrot%d" % c) for c in range(4)]
        zb = [B("zr%d" % c) for c in range(14)]
        gb = [B("gate%d" % c) for c in range(16)]

        def inproj_chunk(ci):
            w, wb = wload("win", ci, NKC * 128)
            pt, pbuf = ps_next()
            for kc in range(NKC):
                P.op("pe", (lambda e, pt=pt, w=w, kc=kc: e.matmul(
                    pt[:, 0:NT], lhsT=w[:, kc * 128:(kc + 1) * 128], rhs=xn[:, kc, :],
                    start=(kc == 0), stop=(kc == NKC - 1))),
                    reads=[wb, xb[kc]], writes=[pbuf])
            return pt, pbuf

        def rope_pair(ci, cpi, out_ap, out_bufs, par):
            p1, b1 = inproj_chunk(ci)
            p2, b2 = inproj_chunk(cpi)
            P.op("dve", (lambda e, p1=p1, par=par: e.tensor_tensor(out=rtmp[:, par, 0, :], in0=p1[:, 0:NT],
                                                                   in1=ropet[:, 0, :], op=ALU.mult)),
                 reads=[b1, B("ropet")], writes=[B("rtmp%d0" % par)])
            P.op("dve", (lambda e, p2=p2, par=par: e.tensor_tensor(out=rtmp[:, par, 1, :], in0=p2[:, 0:NT],
                                                                   in1=ropet[:, 1, :], op=ALU.mult)),
                 reads=[b2, B("ropet")], writes=[B("rtmp%d1" % par)])
            P.op("pool", (lambda e, par=par: e.tensor_tensor(out=out_ap, in0=rtmp[:, par, 0, :],
                                                             in1=rtmp[:, par, 1, :], op=ALU.add)),
                 reads=[B("rtmp%d0" % par), B("rtmp%d1" % par)], writes=out_bufs)

        def mask_ids(gt):
            if gt == 1:
                return 2, 4
            if gt == 2:
                return 0, 3
            return 0, 1

        def mixer_inproj(st, full=True):
            prenorm(2, st % 2)
            yield
            if not full:
                for c in range(13):
                    pt, pbuf = inproj_chunk(IRW + c)
                    P.op("act", (lambda e, pt=pt, c=c: e.activation(out=zr[:, c, 1:NT + 1], in_=pt[:, 0:NT], func=AF.Copy)),
                         reads=[pbuf], writes=[zb[c]])
                    yield
                return
            ml = st - kb.npre
            P.dma("sp", "rope", (lambda e: e.dma_start(out=ropet[:], in_=rope[ml])), writes=[B("ropet")])
            for a in range(2):
                P.op("pool", (lambda e, a=a: e.tensor_copy(out=kT[a][:, 0:128], in_=kT[a][:, SUBS * 128:(SUBS + 1) * 128])),
                     reads=[kslot[a][SUBS]], writes=[kslot[a][0]])
            P.op("pool", lambda e: e.tensor_copy(out=vtok[:, 0, :, 0:64], in_=vtok[:, SUBS, :, 0:64]),
                 reads=[vslot[SUBS]], writes=[vslot[0]])
            for c in range(14):
                pt, pbuf = inproj_chunk(IRW + c)
                P.op("act", (lambda e, pt=pt, c=c: e.activation(out=zr[:, c, 1:NT + 1], in_=pt[:, 0:NT], func=AF.Copy)),
                     reads=[pbuf], writes=[zb[c]])
                yield
            yield "RWKV_IN_DONE"
            for c in range(4):
                rope_pair(IQ + c, IQP + c, qrot[:, c, :], [qb[c]], c % 2)
                yield
            rope_pair(IKA, IKPA, kT[0][:, 128:(1 + SUBS) * 128], kslot[0][1:], 0)
            yield
            rope_pair(IKB, IKPB, kT[1][:, 128:(1 + SUBS) * 128], kslot[1][1:], 1)
            yield
            wv, wvb = wload("win", IV, NKC * 128)
            for i in range(SUBS):
                pt, pbuf = ps_next()
                for kc in range(NKC):
                    P.op("pe", (lambda e, pt=pt, kc=kc, i=i: e.matmul(
                        pt[:, 0:128], lhsT=xn[:, kc, i * 128:(i + 1) * 128], rhs=wv[:, kc * 128:(kc + 1) * 128],
                        start=(kc == 0), stop=(kc == NKC - 1))),
                        reads=[wvb, xb[kc]], writes=[pbuf])
                P.op("act", (lambda e, pt=pt, i=i: e.activation(
                    out=vtok[:, 1 + i, :, 0:64], in_=pt[:, 0:128].rearrange("p (a b) -> p a b", a=2), func=AF.Copy)),
                    reads=[pbuf], writes=[vslot[1 + i]])
            yield
            for c in range(16):
                pt, pbuf = inproj_chunk(IGA + c)
                P.op("act", (lambda e, pt=pt, c=c: e.activation(out=gates[:, c, :], in_=pt[:, 0:NT], func=AF.Sigmoid)),
                     reads=[pbuf], writes=[gb[c]])
                yield

        def attention(st):
            CUT = 99
            for i in range(SUBS):
                mc, mp = mask_ids((st - kb.npre) * SUBS + i)
                banks = [ps_next() for _ in range(4)]
                for cp in range(2):
                    slot = 1 + i - cp
                    for h in range(8):
                        pbs = (h % 2) * 64
                        g = h // 4
                        a = 0 if g == (h % 2) else 1
                        pt, pbuf = banks[cp * 2 + h % 2]
                        hh = h // 2
                        P.op("pe", (lambda e, pt=pt, a=a, pbs=pbs, slot=slot, h=h, hh=hh, i=i: e.matmul(
                            pt[:, hh * 128:(hh + 1) * 128],
                            lhsT=kT[a][pbs:pbs + 64, slot * 128:(slot + 1) * 128],
                            rhs=qrot[pbs:pbs + 64, h // 2, i * 128:(i + 1) * 128], start=True, stop=True)),
                            reads=[kslot[a][slot], qb[h // 2]], writes=[pbuf])
                KSUB = 99
                for bi in range(4):
                    if KSUB <= 0:
                        break
                    pt, pbuf = banks[bi]
                    mi = mc if bi < 2 else mp
                    P.op("act", (lambda e, pt=pt, bi=bi: e.activation(
                        out=PT[:, bi * 4:(bi + 1) * 4, :], in_=pt[:, :].rearrange("p (a b) -> p a b", a=4),
                        func=AF.Exp, scale=0.125)),
                        reads=[pbuf], writes=[B("PT%d" % bi)])
                    if KSUB <= 1:
                        continue
                    P.op("pool", (lambda e, bi=bi, mi=mi: e.tensor_tensor(
                        out=PT[:, bi * 4:(bi + 1) * 4, :], in0=PT[:, bi * 4:(bi + 1) * 4, :],
                        in1=maskb[:, mi:mi + 1, :].broadcast_to([128, 4, 128]), op=ALU.mult)),
                        reads=[B("PT%d" % bi), B("maskb")], writes=[B("PT%d" % bi)])
                if CUT <= 2:
                    continue
                yield
                obanks = [ps_next() for _ in range(2)]
                for h in range(8):
                    g = h // 4
                    pt, pbuf = obanks[h // 4]
                    hh = h % 4
                    for cp in range(2):
                        slot = 1 + i - cp
                        P.op("pe", (lambda e, pt=pt, hh=hh, cp=cp, h=h, slot=slot, g=g: e.matmul(
                            pt[:, hh * 65:(hh + 1) * 65], lhsT=PT[:, cp * 8 + (h % 2) * 4 + h // 2, :],
                            rhs=vtok[:, slot, g, :], start=(cp == 0), stop=(cp == 1))),
                            reads=[B("PT%d" % (cp * 2 + h % 2)), vslot[slot]], writes=[pbuf])
                for hb2 in range(2):
                    pt, pbuf = obanks[hb2]
                    o3 = pt[:, 0:260].rearrange("p (a b) -> p a b", b=65)
                    P.op("dve", (lambda e, o3=o3, hb2=hb2: e.tensor_tensor(
                        out=den[:, 0, hb2 * 4:(hb2 + 1) * 4].unsqueeze(2), in0=o3[:, :, 64:65],
                        in1=esink[:, hb2 * 4:(hb2 + 1) * 4].unsqueeze(2), op=ALU.add)),
                        reads=[pbuf, B("esink")], writes=[B("den%d" % hb2)])
                    P.op("dve", (lambda e, hb2=hb2: e.reciprocal(out=den[:, 1, hb2 * 4:(hb2 + 1) * 4],
                                                                in_=den[:, 0, hb2 * 4:(hb2 + 1) * 4])),
                         reads=[B("den%d" % hb2)], writes=[B("den%d" % hb2)])
                    P.op("dve", (lambda e, o3=o3, hb2=hb2: e.tensor_tensor(
                        out=yatt[:, hb2 * 256:(hb2 + 1) * 256].rearrange("p (a b) -> p a b", b=64),
                        in0=o3[:, :, 0:64],
                        in1=den[:, 1, hb2 * 4:(hb2 + 1) * 4].unsqueeze(2).broadcast_to([128, 4, 64]), op=ALU.mult)),
                        reads=[pbuf, B("den%d" % hb2)], writes=[B("yatt%d" % hb2)])
                if CUT <= 3:
                    continue
                pt, pbuf = ps_next()
                for kc in range(4):
                    P.op("pe", (lambda e, pt=pt, kc=kc: e.transpose(pt[:, kc * 128:(kc + 1) * 128],
                                                                    yatt[:, kc * 128:(kc + 1) * 128], ident[:])),
                         reads=[B("yatt%d" % (kc // 2)), B("ident")], writes=[pbuf])
                P.op("act", (lambda e, pt=pt, i=i: e.activation(
                    out=yattT[:, :, i * 128:(i + 1) * 128], in_=pt[:, :].rearrange("p (a b) -> p a b", a=4),
                    func=AF.Copy)),
                    reads=[pbuf], writes=[B("yattT")])
                yield

        rwp = sb("rwp_sb", [128, 9, 4], F32)
        mus = sb("mus", [128, 14], F32)
        loraw = sb("loraw_sb", [128, 3, 512], BF16)
        bones = sb("bones_sb", [128, 128], BF16)
        identb = sb("identb", [128, 128], BF16)
        ones_f = sb("ones_f", [128, 128], F32)
        mX = sb("mX", [128, 384], BF16)
        mY = sb("mY", [128, 256], BF16)
        lin = sb("lin", [128, NT], BF16)
        sgx = sb("sgx", [128, NT], BF16)
        dz = sgt
        T1 = T1s
        gT = sb("gT", [128, 4, NT], BF16)
        ksq = sb("ksq", [128, NT], BF16)
        gg = sb("gg", [128, 2, 2, SUBS, 129], F32)
        sc = sb("sc", [128, 4, 4, SUBS], F32)
        opb = sb("opb", [128, 5, 4, NT], BF16)
        rt0 = sb("rt0", [128, 4, NT], F32)
        prod = sb("prod", [128, 4, NT], BF16)
        tokm = sb("tokm", [128, SUBS, 4, 512], BF16)
        AX8 = sb("AX8", [128, 8, 384], BF16)
        AY8 = sb("AY8", [128, 8, 256], BF16)
        SQ8 = sb("SQ8", [128, 2, 8, 2, 128], BF16)
        Xbf = sb("Xbf", [128, 2, 8, 64], BF16)
        Pblk = sb("Pblk", [128, 4, 128], F32)
        Qs = sb("Qs", [128, 4, 64], F32)
        G0T = sb("G0T", [128, 4, 128], F32)
        Hst = sb("Hst", [128, 4, 64], F32)
        Ysb = sb("Ysb", [128, 512], F32)
        yn = sb("yn", [128, 512], F32)
        sqY = yn
        gst = sb("gst", [128, 6, 8], F32)
        ynT = sb("ynT", [128, 4, NT], F32)
        yrT = sb("yrT", [128, 4, NT], BF16)

        P.dma("sp", "c4", lambda e: e.dma_start(out=rwp[:, 0:8, :], in_=rwp_in[:, :, :]), writes=[B("rwp")])
        P.dma("sp", "c5", lambda e: e.dma_start(out=mus[:], in_=mu_in[:, :]), writes=[B("mus")])
        for q3 in range(3):
            load_cast(loraw[:, q3, :], lora_in_d[:, q3, :], B("loraw"))
        P.dma("sp", "stgb", lambda e: e.dma_start(out=ones_f[:], in_=bones_in[:, :]), writes=[B("ones_f")])
        P.op("dve", lambda e: e.tensor_copy(out=bones[:], in_=ones_f[:]), reads=[B("ones_f")], writes=[B("bones")])
        P.op("pool", lambda e: e.memset(ones_f[:], 1.0), reads=[B("bones")], writes=[B("ones_f")])
        P.op("dve", lambda e: e.tensor_copy(out=identb[:], in_=ident[:]), reads=[B("ident")], writes=[B("identb")])
        P.op("dve", lambda e: e.tensor_scalar(out=rwp[:, 7, :], in0=rwp[:, 3, :], scalar1=-1.0, scalar2=1.0,
                                              op0=ALU.mult, op1=ALU.add), reads=[B("rwp")], writes=[B("rwp")])
        P.op("dve", lambda e: e.tensor_scalar(out=rwp[:, 8, :], in0=rwp[:, 2, :], scalar1=-1.0, scalar2=None, op0=ALU.mult),
             reads=[B("rwp")], writes=[B("rwp")])
        P.op("pool", lambda e: e.tensor_copy(out=mX[:, 0:128], in_=maskb[:, 5, 0:128]), reads=[B("maskb")], writes=[B("mX")])
        P.op("pool", lambda e: e.tensor_copy(out=mX[:, 128:256], in_=maskb[:, 1, 0:128]), reads=[B("maskb")], writes=[B("mX")])
        P.op("pool", lambda e: e.tensor_copy(out=mX[:, 256:384], in_=maskb[:, 5, 0:128]), reads=[B("maskb")], writes=[B("mX")])
        P.op("pool", lambda e: e.tensor_copy(out=mY[:, 0:128], in_=maskb[:, 0, 0:128]), reads=[B("maskb")], writes=[B("mY")])
        P.op("pool", lambda e: e.tensor_copy(out=mY[:, 128:256], in_=maskb[:, 0, 0:128]), reads=[B("maskb")], writes=[B("mY")])
        P.op("pool", lambda e: e.memset(Hst[:], 0.0), writes=[B("H%d" % j) for j in range(4)])
        P.op("pool", lambda e: e.memset(Pblk[:], 0.0), writes=[B("Pblk%d" % j) for j in range(4)])
        P.op("pool", lambda e: e.memset(gg[:], 1.0), writes=[B("gg0"), B("gg1")])
        YB = [(psum[6], B("ps6")), (psum[7], B("ps7"))]
        tb = [B("T1_%d" % q) for q in range(19)]
        P.op("pool", lambda e: e.memset(T1[:, 18, 0:1], 0.0), writes=[B("stg0"), B("stg1")] + tb)
        opbuf = [[B("opb%d_%d" % (k_, j)) for j in range(4)] for k_ in range(5)]

        def rwkv(st, full=True):
            zin = lambda c: zr[:, c, 1:NT + 1]
            for c in range(14 if full else 13):
                par = c % 2
                P.op("pool", (lambda e, c=c, par=par: e.tensor_tensor(out=dz[:, par, :], in0=zr[:, c, 0:NT],
                                                                      in1=zr[:, c, 1:NT + 1], op=ALU.subtract)),
                     reads=[zb[c], B("zrp%d" % c)], writes=[B("sgt%d" % par)])
                P.op("pool", (lambda e, c=c: e.tensor_copy(out=zr[:, c, 0:1], in_=zr[:, c, NT:NT + 1])),
                     reads=[zb[c], B("sgt%d" % par)], writes=[B("zrp%d" % c)])
                P.op("dve", (lambda e, c=c, par=par: e.scalar_tensor_tensor(
                    out=zr[:, c, 1:NT + 1], in0=dz[:, par, :], scalar=mus[:, c:c + 1], op0=ALU.mult,
                    in1=zr[:, c, 1:NT + 1], op1=ALU.add)),
                    reads=[B("sgt%d" % par), B("mus"), zb[c]], writes=[zb[c]])
                if c % 2 == 1:
                    yield
            P.op("act", lambda e: e.activation(out=lin[0:64, :], in_=zr[0:64, 12, 1:NT + 1], func=AF.Tanh),
                 reads=[zb[12]], writes=[B("lin0")])
            P.op("act", lambda e: e.activation(out=lin[64:128, :], in_=zr[64:128, 12, 1:NT + 1], func=AF.Copy),
                 reads=[zb[12]], writes=[B("lin1")])
            if full:
                P.op("act", lambda e: e.activation(out=sgx[:], in_=zr[:, 13, 1:NT + 1], func=AF.Sigmoid),
                     reads=[zb[13]], writes=[B("sgx")])
            def prep_j(j, T1, tb):
                pw, pwb = ps_next()
                P.op("pe", (lambda e, pw=pw, j=j: e.matmul(pw[:, 0:NT], lhsT=loraw[0:64, 0, j * 128:(j + 1) * 128],
                                                           rhs=lin[0:64, :], start=True, stop=True)),
                     reads=[B("loraw"), B("lin0")], writes=[pwb])
                pa, pab = ps_next()
                P.op("pe", (lambda e, pa=pa, j=j: e.matmul(pa[:, 0:NT], lhsT=loraw[64:128, 1, j * 128:(j + 1) * 128],
                                                           rhs=lin[64:128, :], start=True, stop=True)),
                     reads=[B("loraw"), B("lin1")], writes=[pab])
                if full:
                    pg, pgb = ps_next()
                    P.op("pe", (lambda e, pg=pg, j=j: e.matmul(pg[:, 0:NT], lhsT=loraw[:, 2, j * 128:(j + 1) * 128],
                                                               rhs=sgx[:], start=True, stop=True)),
                         reads=[B("loraw"), B("sgx")], writes=[pgb])
                P.op("act", (lambda e, pw=pw, j=j: e.activation(out=T1[:, 0, :], in_=pw[:, 0:NT], func=AF.Sigmoid,
                                                                bias=rwp[:, 0, j:j + 1])),
                     reads=[pwb, B("rwp")], writes=[tb[0]])
                P.op("act", (lambda e, pa=pa, j=j: e.activation(out=T1[:, 1, :], in_=pa[:, 0:NT], func=AF.Sigmoid,
                                                                bias=rwp[:, 1, j:j + 1])),
                     reads=[pab, B("rwp")], writes=[tb[1]])
                if full:
                    P.op("act", (lambda e, pg=pg, j=j: e.activation(out=gT[:, j, :], in_=pg[:, 0:NT], func=AF.Copy)),
                         reads=[pgb], writes=[B("gT%d" % j)])
                yield "P1"
                P.op("act", (lambda e, j=j: e.activation(out=ksq[:], in_=zin(4 + j), func=AF.Square,
                                                         scale=rwp[:, 2, j:j + 1])),
                     reads=[zb[4 + j], B("rwp")], writes=[B("ksq")])
                pss, pssb = ps_next()
                P.op("pe", (lambda e, pss=pss: e.matmul(pss[:, 0:NT], lhsT=bones[:], rhs=ksq[:], start=True, stop=True)),
                     reads=[B("bones"), B("ksq")], writes=[pssb])
                P.op("dve", (lambda e, pss=pss: e.tensor_scalar(out=T1[:, 2, :], in0=pss[:, 0:NT], scalar1=1e-24,
                                                                scalar2=None, op0=ALU.max)),
                     reads=[pssb], writes=[tb[2]])
                P.op("act", lambda e: e.activation(out=T1[:, 2, :], in_=T1[:, 2, :], func=AF.Ln), reads=[tb[2]], writes=[tb[2]])
                P.op("act", lambda e: e.activation(out=T1[:, 2, :], in_=T1[:, 2, :], func=AF.Exp, scale=-0.5),
                     reads=[tb[2]], writes=[tb[2]])
                P.op("dve", (lambda e, j=j: e.scalar_tensor_tensor(out=T1[:, 3, :], in0=zin(4 + j),
                                                                   scalar=rwp[:, 8, j:j + 1], op0=ALU.mult,
                                                                   in1=T1[:, 2, :], op1=ALU.mult)),
                     reads=[zb[4 + j], B("rwp"), tb[2]], writes=[tb[3]])
                P.op("dve", (lambda e, j=j: e.tensor_scalar(out=T1[:, 4, :], in0=T1[:, 1, :], scalar1=rwp[:, 3, j:j + 1],
                                                            scalar2=rwp[:, 7, j:j + 1], op0=ALU.mult, op1=ALU.add)),
                     reads=[tb[1], B("rwp")], writes=[tb[4]])
                P.op("pool", (lambda e, j=j: e.tensor_tensor(out=T1[:, 5, :], in0=zin(4 + j), in1=T1[:, 4, :], op=ALU.mult)),
                     reads=[zb[4 + j], tb[4]], writes=[tb[5]])
                P.op("dve", lambda e: e.scalar_tensor_tensor(out=T1[:, 6, :], in0=T1[:, 3, :], scalar=-1.0, op0=ALU.mult,
                                                             in1=T1[:, 1, :], op1=ALU.mult),
                     reads=[tb[3], tb[1]], writes=[tb[6]])
                if full:
                    P.op("dve", (lambda e, j=j: e.scalar_tensor_tensor(out=prod[:, j, :], in0=zin(j), scalar=rwp[:, 6, j:j + 1],
                                                                       op0=ALU.mult, in1=T1[:, 5, :], op1=ALU.mult)),
                         reads=[zb[j], B("rwp"), tb[5]], writes=[B("prod%d" % j)])
                yield
                gsel = j % 2
                ggb = B("gg%d" % gsel)
                scb = B("sc%d" % j)
                P.op("act", lambda e: e.activation(out=T1[:, 7, :], in_=T1[:, 0, :], func=AF.Exp, scale=-CDEC),
                     reads=[tb[0]], writes=[tb[7]])
                P.op("act", lambda e: e.activation(out=T1[:, 8, :], in_=T1[:, 0, :], func=AF.Exp, scale=CDEC),
                     reads=[tb[0]], writes=[tb[8]])
                for i in range(SUBS):
                    for q_, slot in ((0, 7), (1, 8)):
                        P.op("dve", (lambda e, i=i, q_=q_, slot=slot: e.tensor_tensor_scan(
                            out=gg[:, gsel, q_, i, 1:129], data0=T1[:, slot, i * 128:(i + 1) * 128], data1=ones_f[:, 0:128],
                            initial=1.0, op0=ALU.mult, op1=ALU.mult)),
                            reads=[tb[slot], B("ones_f")], writes=[ggb])
                P.op("dve", (lambda e: e.tensor_tensor(out=sc[:, 2, j, :], in0=gg[:, gsel, 0, :, 128], in1=gg[:, gsel, 1, :, 64],
                                                       op=ALU.mult)), reads=[ggb], writes=[scb])
                P.op("dve", (lambda e: e.tensor_copy(out=sc[:, 3, j, :], in_=gg[:, gsel, 0, :, 64])), reads=[ggb], writes=[scb])
                yield
                for i in range(SUBS):
                    isl = slice(i * 128, (i + 1) * 128)
                    gmid = gg[:, gsel, 0, i, 64:65]
                    imid = gg[:, gsel, 1, i, 64:65]
                    gprev = gg[:, gsel, 0, i, 0:128]
                    gcur = gg[:, gsel, 0, i, 1:129]
                    icur = gg[:, gsel, 1, i, 1:129]
                    P.op("dve", (lambda e, isl=isl, imid=imid, gprev=gprev: e.scalar_tensor_tensor(
                        out=opb[:, 0, j, isl], in0=T1[:, 3, isl], scalar=imid, op0=ALU.mult, in1=gprev, op1=ALU.mult)),
                        reads=[tb[3], ggb], writes=[opbuf[0][j]])
                    P.op("pool", (lambda e, isl=isl, gprev=gprev: e.tensor_tensor(out=opb[:, 1, j, isl], in0=T1[:, 3, isl],
                                                                                  in1=gprev, op=ALU.mult)),
                         reads=[tb[3], ggb], writes=[opbuf[1][j]])
                    P.op("dve", (lambda e, isl=isl, gmid=gmid, icur=icur: e.scalar_tensor_tensor(
                        out=opb[:, 2, j, isl], in0=T1[:, 6, isl], scalar=gmid, op0=ALU.mult, in1=icur, op1=ALU.mult)),
                        reads=[tb[6], ggb], writes=[opbuf[2][j]])
                    P.op("dve", (lambda e, isl=isl, gmid=gmid, icur=icur: e.scalar_tensor_tensor(
                        out=opb[:, 3, j, isl], in0=T1[:, 5, isl], scalar=gmid, op0=ALU.mult, in1=icur, op1=ALU.mult)),
                        reads=[tb[5], ggb], writes=[opbuf[3][j]])
                    yield
                    if not full:
                        continue
                    P.op("dve", (lambda e, isl=isl, i=i, imid=imid, gcur=gcur: e.scalar_tensor_tensor(
                        out=opb[:, 4, j, isl], in0=zr[:, j, 1 + i * 128:1 + (i + 1) * 128], scalar=imid, op0=ALU.mult,
                        in1=gcur, op1=ALU.mult)), reads=[zb[j], ggb], writes=[opbuf[4][j]])
                    P.op("pool", (lambda e, isl=isl, i=i, gcur=gcur: e.tensor_tensor(
                        out=rt0[:, j, isl], in0=zr[:, j, 1 + i * 128:1 + (i + 1) * 128], in1=gcur, op=ALU.mult)),
                        reads=[zb[j], ggb], writes=[B("rt0_%d" % j)])
            for jp in (0, 2):
                gens = [prep_j(jp + q_, T1s[:, 9 * q_:9 * q_ + 9, :], tb[9 * q_:9 * q_ + 9]) for q_ in range(2)]
                for g_ in gens:
                    for v in g_:
                        if v == "P1":
                            break
                        yield
                    yield
                for g_ in gens:
                    yield from g_
            for i in range(SUBS):
                isl = slice(i * 128, (i + 1) * 128)
                pA, pAb = ps_next()
                pAv = pA[:, :].bitcast(BF16)
                for j in range(4):
                    P.op("pe", (lambda e, pAv=pAv, j=j, isl=isl: e.transpose(pAv[:, j * 128:(j + 1) * 128],
                                                                             opb[:, 1, j, isl], identb[:])),
                         reads=[opbuf[1][j], B("identb")], writes=[pAb])
                    P.op("pe", (lambda e, pAv=pAv, j=j, isl=isl: e.transpose(pAv[:, 512 + j * 128:512 + (j + 1) * 128],
                                                                             opb[:, 2, j, isl], identb[:])),
                         reads=[opbuf[2][j], B("identb")], writes=[pAb])
                P.op("act", (lambda e, pAv=pAv, i=i: e.activation(out=tokm[:, i, 0:2, :],
                                                                  in_=pAv.rearrange("p (a b) -> p a b", a=2), func=AF.Copy)),
                     reads=[pAb], writes=[B("tokm%d_0" % i), B("tokm%d_1" % i)])
                yield
                pB_, pBb = ps_next()
                pBv = pB_[:, :].bitcast(BF16)
                for j in range(4):
                    P.op("pe", (lambda e, pBv=pBv, j=j, isl=isl: e.transpose(pBv[:, j * 128:(j + 1) * 128],
                                                                             opb[:, 3, j, isl], identb[:])),
                         reads=[opbuf[3][j], B("identb")], writes=[pBb])
                P.op("act", (lambda e, pBv=pBv, i=i: e.activation(out=tokm[:, i, 2, :], in_=pBv[:, 0:512], func=AF.Copy)),
                     reads=[pBb], writes=[B("tokm%d_2" % i)])
                pV, pVb = ps_next()
                for j in range(4):
                    P.op("pe", (lambda e, pV=pV, j=j, i=i: e.transpose(pV[:, j * 128:(j + 1) * 128],
                                                                       zr[:, 8 + j, 1 + i * 128:1 + (i + 1) * 128], ident[:])),
                         reads=[zb[8 + j], B("ident")], writes=[pVb])
                P.op("act", (lambda e, pV=pV, i=i: e.activation(out=tokm[:, i, 3, :], in_=pV[:, :], func=AF.Copy)),
                     reads=[pVb], writes=[B("tokm%d_3" % i)])
                yield
            yield "PREP_DONE"
            for i in range(SUBS):
                isl = slice(i * 128, (i + 1) * 128)
                ystart = [True, True]
                axb = [B("AX8_%d" % h) for h in range(8)]
                ayb = [B("AY8_%d" % h) for h in range(8)]
                x32g = [B("X32g%d" % g) for g in range(2)]
                xbfg = [B("Xbfg%d" % g) for g in range(2)]
                for h in range(8):
                    j, e_ = h // 2, h % 2
                    pbs = e_ * 64
                    at_ = opb[pbs:pbs + 64, 0, j, isl]
                    bt_ = opb[pbs:pbs + 64, 2, j, isl]
                    kt_ = opb[pbs:pbs + 64, 3, j, isl]
                    rt_ = opb[pbs:pbs + 64, 4, j, isl]
                    rds = [opbuf[k_][j] for k_ in (0, 2, 3, 4)]
                    bx, bxb = ps_next()
                    if full:
                        by, byb = ps_next()
                        rds = [opbuf[k_][j] for k_ in (0, 2, 3, 4)]
                    else:
                        rds = [opbuf[k_][j] for k_ in (0, 2, 3)]
                    for (dst, l_, r_) in ((bx[:, 0:128], bt_, at_), (bx[:, 128:256], at_, bt_), (bx[:, 256:384], kt_, at_)):
                        P.op("pe", (lambda e, dst=dst, l_=l_, r_=r_: e.matmul(dst, lhsT=l_, rhs=r_, start=True, stop=True)),
                             reads=rds, writes=[bxb])
                    P.op("dve", (lambda e, bx=bx, h=h: e.tensor_tensor(out=AX8[:, h, :], in0=bx[:, 0:384], in1=mX[:],
                                                                       op=ALU.mult)), reads=[bxb, B("mX")], writes=[axb[h]])
                    if full:
                        for (dst, l_, r_) in ((by[:, 0:128], bt_, rt_), (by[:, 128:256], kt_, rt_)):
                            P.op("pe", (lambda e, dst=dst, l_=l_, r_=r_: e.matmul(dst, lhsT=l_, rhs=r_, start=True, stop=True)),
                                 reads=rds, writes=[byb])
                        P.op("dve", (lambda e, by=by, h=h: e.tensor_tensor(out=AY8[:, h, :], in0=by[:, 0:256], in1=mY[:],
                                                                           op=ALU.mult)), reads=[byb, B("mY")], writes=[ayb[h]])
                    yield
                kb.ps_ring = [0, 1, 2, 3]
                XB = [(psum[4], B("ps4")), (psum[5], B("ps5"))]
                casteng = ["act", "dve"]

                def xcast(g):
                    xb_, xbb = XB[g]
                    src = xb_[:, :].rearrange("p (h a d) -> p a h d", h=4, a=2)
                    dst = Xbf[:, :, g * 4:(g + 1) * 4, :]
                    if casteng[g] == "act":
                        P.op("act", (lambda e: e.activation(out=dst, in_=src, func=AF.Copy)), reads=[xbb], writes=[xbfg[g]])
                    else:
                        P.op("dve", (lambda e: e.tensor_copy(out=dst, in_=src)), reads=[xbb], writes=[xbfg[g]])

                for g in range(2):
                    xb_, xbb = XB[g]
                    for hh in range(4):
                        h = g * 4 + hh
                        P.op("pe", (lambda e, xb_=xb_, hh=hh, h=h, i=i, st_=(hh == 0): e.matmul(
                            xb_[:, hh * 128:hh * 128 + 64], lhsT=identb[:], rhs=tokm[:, i, 0, h * 64:(h + 1) * 64],
                            start=st_, stop=False, skip_group_check=True)),
                            reads=[B("identb"), B("tokm%d_0" % i)], writes=[xbb])
                        P.op("pe", (lambda e, xb_=xb_, hh=hh, h=h, i=i: e.matmul(
                            xb_[:, hh * 128 + 64:hh * 128 + 128], lhsT=AX8[:, h, 256:384],
                            rhs=tokm[:, i, 3, h * 64:(h + 1) * 64], start=False, stop=False, skip_group_check=True)),
                            reads=[axb[h], B("tokm%d_3" % i)], writes=[xbb])
                    xcast(g)
                    yield
                curA = [AX8[:, h, 128:256] for h in range(8)]
                curAT = [AX8[:, h, 0:128] for h in range(8)]
                curb = [[axb[h]] for h in range(8)]
                pp = 0
                for lvl in range(7):
                    for g in range(2):
                        xb_, xbb = XB[g]
                        for hh in range(4):
                            h = g * 4 + hh
                            P.op("pe", (lambda e, xb_=xb_, hh=hh, h=h, lt=curAT[h]: e.matmul(
                                xb_[:, hh * 128:(hh + 1) * 128].rearrange("p (a d) -> p a d", a=2), lhsT=lt,
                                rhs=Xbf[:, :, h, :], start=False, stop=(lvl == 6), skip_group_check=True)),
                                reads=curb[h] + [xbfg[g]], writes=[xbb])
                        xcast(g)
                        yield
                    if lvl < 6:
                        for hp in range(4):
                            pq, pqb = ps_next()
                            sqb = B("SQ8_%d_%d" % (pp, hp))
                            for q_ in range(2):
                                h = hp * 2 + q_
                                P.op("pe", (lambda e, pq=pq, q_=q_, la=curAT[h], ra=curA[h]: e.matmul(
                                    pq[:, q_ * 256:q_ * 256 + 128], lhsT=la, rhs=ra, start=True, stop=True)),
                                    reads=curb[h], writes=[pqb])
                                P.op("pe", (lambda e, pq=pq, q_=q_, la=curA[h], ra=curAT[h]: e.matmul(
                                    pq[:, q_ * 256 + 128:q_ * 256 + 256], lhsT=la, rhs=ra, start=True, stop=True)),
                                    reads=curb[h], writes=[pqb])
                            eng = "act" if hp % 2 == 0 else "dve"
                            if eng == "act":
                                P.op("act", (lambda e, pq=pq, hp=hp, pp=pp: e.activation(
                                    out=SQ8[:, pp, hp * 2:hp * 2 + 2, :, :],
                                    in_=pq[:, :].rearrange("p (q a c) -> p q a c", q=2, a=2), func=AF.Copy)),
                                    reads=[pqb], writes=[sqb])
                            else:
                                P.op("dve", (lambda e, pq=pq, hp=hp, pp=pp: e.tensor_copy(
                                    out=SQ8[:, pp, hp * 2:hp * 2 + 2, :, :],
                                    in_=pq[:, :].rearrange("p (q a c) -> p q a c", q=2, a=2))),
                                    reads=[pqb], writes=[sqb])
                            for q_ in range(2):
                                h = hp * 2 + q_
                                curA[h] = SQ8[:, pp, h, 0, :]
                                curAT[h] = SQ8[:, pp, h, 1, :]
                                curb[h] = [sqb]
                            if hp % 2 == 1:
                                yield
                        pp ^= 1
                kb.ps_ring = [0, 1, 2, 3, 4, 5]
                pP, pPb = ps_next()
                pQ, pQb = ps_next()
                pG = [ps_next(), ps_next()] if full else [None, None]
                for j in range(4):
                    g = j // 2
                    jsl = slice(j * 128, (j + 1) * 128)
                    W0p = Xbf[:, 0, 2 * j:2 * j + 2, :]
                    P.op("pe", (lambda e, W0p=W0p, i=i, jsl=jsl, pP=pP: e.matmul(pP[:, jsl], lhsT=W0p, rhs=tokm[:, i, 1, jsl],
                                                                        start=True, stop=True)),
                         reads=[xbfg[g], B("tokm%d_1" % i)], writes=[pPb])
                for j in range(4):
                    g = j // 2
                    jsl = slice(j * 128, (j + 1) * 128)
                    Uvp = Xbf[:, 1, 2 * j:2 * j + 2, :]
                    P.op("pe", (lambda e, Uvp=Uvp, i=i, jsl=jsl, pQ=pQ: e.matmul(pQ[:, jsl], lhsT=tokm[:, i, 1, jsl], rhs=Uvp,
                                                                        start=True, stop=False)),
                         reads=[xbfg[g], B("tokm%d_1" % i)], writes=[pQb])
                    P.op("pe", (lambda e, i=i, jsl=jsl, pQ=pQ: e.matmul(pQ[:, jsl], lhsT=tokm[:, i, 2, jsl], rhs=tokm[:, i, 3, jsl],
                                                               start=False, stop=True)),
                         reads=[B("tokm%d_2" % i), B("tokm%d_3" % i)], writes=[pQb])
                for j in range(4 if full else 0):
                    g = j // 2
                    W0p = Xbf[:, 0, 2 * j:2 * j + 2, :]
                    pg_, pgb_ = pG[j // 2]
                    for e_ in range(2):
                        h = 2 * j + e_
                        c0_ = (j % 2) * 256 + e_ * 128
                        P.op("pe", (lambda e, pg_=pg_, W0p=W0p, h=h, c0_=c0_: e.matmul(
                            pg_[:, c0_:c0_ + 128], lhsT=W0p, rhs=AY8[:, h, 0:128], start=True, stop=True)),
                            reads=[xbfg[g], ayb[h]], writes=[pgb_])
                for j in range(4):
                    jsl = slice(j * 128, (j + 1) * 128)
                    pblb = B("Pblk%d" % j)
                    qsb = B("Qs%d" % j)
                    g0b = B("G0T%d" % j)
                    pg_, pgb_ = pG[j // 2] if full else (None, None)
                    for e_ in range(2):
                        pbs = e_ * 64
                        c0_ = (j % 2) * 256 + e_ * 128
                        P.op("dve", (lambda e, pbs=pbs, j=j, i=i, pP=pP: e.scalar_tensor_tensor(
                            out=Pblk[pbs:pbs + 64, j, pbs:pbs + 64], in0=ident[pbs:pbs + 64, pbs:pbs + 64],
                            scalar=sc[pbs:pbs + 64, 3, j, i:i + 1], op0=ALU.mult,
                            in1=pP[pbs:pbs + 64, j * 128 + pbs:j * 128 + pbs + 64], op1=ALU.add)),
                            reads=[pPb, B("ident"), B("sc%d" % j)], writes=[pblb])
                        P.op("act", (lambda e, pbs=pbs, j=j, i=i, pQ=pQ: e.activation(
                            out=Qs[pbs:pbs + 64, j, :], in_=pQ[pbs:pbs + 64, j * 128 + pbs:j * 128 + pbs + 64], func=AF.Copy,
                            scale=sc[pbs:pbs + 64, 2, j, i:i + 1])), reads=[pQb, B("sc%d" % j)], writes=[qsb])
                        if full:
                            P.op("dve", (lambda e, pg_=pg_, pbs=pbs, j=j, c0_=c0_, isl=isl: e.tensor_tensor(
                                out=G0T[pbs:pbs + 64, j, :], in0=pg_[pbs:pbs + 64, c0_:c0_ + 128],
                                in1=rt0[pbs:pbs + 64, j, isl], op=ALU.add)), reads=[pgb_, B("rt0_%d" % j)], writes=[g0b])
                for j in range(4):
                    g = j // 2
                    pblb = B("Pblk%d" % j)
                    qsb = B("Qs%d" % j)
                    g0b = B("G0T%d" % j)
                    hbuf = B("H%d" % j)
                    for e_ in range(2 if full else 0):
                        pbs = e_ * 64
                        h = 2 * j + e_
                        yb_, ybb = YB[e_]
                        col = slice(j * 64, (j + 1) * 64)
                        P.op("pe", (lambda e, yb_=yb_, col=col, h=h, st_=ystart[e_]: e.matmul(
                            yb_[:, col], lhsT=AY8[:, h, 0:128], rhs=Xbf[:, 1, h, :], start=st_, stop=False,
                            skip_group_check=True)), reads=[ayb[h], xbfg[g]], writes=[ybb])
                        ystart[e_] = False
                        P.op("pe", (lambda e, yb_=yb_, col=col, h=h, i=i: e.matmul(
                            yb_[:, col], lhsT=AY8[:, h, 128:256], rhs=tokm[:, i, 3, h * 64:(h + 1) * 64],
                            start=False, stop=False, skip_group_check=True)),
                            reads=[ayb[h], B("tokm%d_3" % i)], writes=[ybb])
                        P.op("pe", (lambda e, yb_=yb_, col=col, pbs=pbs, j=j: e.matmul(
                            yb_[:, col], lhsT=G0T[pbs:pbs + 64, j, :], rhs=Hst[pbs:pbs + 64, j, :], start=False, stop=True,
                            skip_group_check=True)), reads=[g0b, hbuf], writes=[ybb])
                    pH, pHb = ps_next()
                    P.op("pe", (lambda e, pH=pH, j=j: e.matmul(pH[:, 0:64], lhsT=Pblk[:, j, :], rhs=Hst[:, j, :],
                                                               start=True, stop=True)),
                         reads=[pblb, hbuf], writes=[pHb])
                    P.op("dve", (lambda e, pH=pH, j=j, i=i: e.scalar_tensor_tensor(
                        out=Hst[:, j, :], in0=pH[:, 0:64], scalar=sc[:, 2, j, i:i + 1], op0=ALU.mult, in1=Qs[:, j, :],
                        op1=ALU.add)), reads=[pHb, B("sc%d" % j), qsb, hbuf], writes=[hbuf])
                    yield
                if not full:
                    continue
                ysb4 = Ysb[:, :].rearrange("p (j e d) -> p j e d", j=4, e=2)
                for e_ in range(2):
                    yb_, ybb = YB[e_]
                    P.op("act", (lambda e, yb_=yb_, e_=e_: e.activation(
                        out=ysb4[:, :, e_, :], in_=yb_[:, 0:256].rearrange("p (a b) -> p a b", a=4), func=AF.Copy)),
                        reads=[ybb], writes=[B("Ysb")])
                y3 = Ysb[:, :].rearrange("p (a b) -> p a b", b=64)
                P.op("dve", lambda e: e.tensor_reduce(out=gst[:, 0, :], in_=y3, op=ALU.add, axis=AX.X),
                     reads=[B("Ysb")], writes=[B("gst")])
                P.op("act", lambda e: e.activation(out=sqY[:], in_=Ysb[:], func=AF.Square), reads=[B("Ysb")], writes=[B("yn")])
                P.op("dve", lambda e: e.tensor_reduce(out=gst[:, 1, :], in_=sqY[:, :].rearrange("p (a b) -> p a b", b=64),
                                                      op=ALU.add, axis=AX.X), reads=[B("yn")], writes=[B("gst")])
                P.op("dve", lambda e: e.tensor_scalar(out=gst[:, 2, :], in0=gst[:, 0, :], scalar1=1.0 / 64, scalar2=None,
                                                      op0=ALU.mult), reads=[B("gst")], writes=[B("gst")])
                P.op("dve", lambda e: e.tensor_tensor(out=gst[:, 3, :], in0=gst[:, 2, :], in1=gst[:, 2, :], op=ALU.mult),
                     reads=[B("gst")], writes=[B("gst")])
                P.op("dve", lambda e: e.scalar_tensor_tensor(out=gst[:, 4, :], in0=gst[:, 1, :], scalar=1.0 / 64, op0=ALU.mult,
                                                             in1=gst[:, 3, :], op1=ALU.subtract),
                     reads=[B("gst")], writes=[B("gst")])
                P.op("act", lambda e: e.activation(out=gst[:, 4, :], in_=gst[:, 4, :], func=AF.Ln, bias=epsb[:, 1:2]),
                     reads=[B("gst"), B("epsb")], writes=[B("gst")])
                P.op("act", lambda e: e.activation(out=gst[:, 5, :], in_=gst[:, 4, :], func=AF.Exp, scale=-0.5),
                     reads=[B("gst")], writes=[B("gst")])
                yn3 = yn[:, :].rearrange("p (a b) -> p a b", b=64)
                P.op("dve", lambda e: e.tensor_tensor(out=yn3, in0=y3, in1=gst[:, 2, :].unsqueeze(2).broadcast_to([128, 8, 64]),
                                                      op=ALU.subtract), reads=[B("Ysb"), B("gst")], writes=[B("yn")])
                P.op("dve", lambda e: e.tensor_tensor(out=yn3, in0=yn3, in1=gst[:, 5, :].unsqueeze(2).broadcast_to([128, 8, 64]),
                                                      op=ALU.mult), reads=[B("yn"), B("gst")], writes=[B("yn")])
                pT_, pTb = ps_next()
                for j in range(4):
                    P.op("pe", (lambda e, pT_=pT_, j=j: e.transpose(pT_[:, j * 128:(j + 1) * 128], yn[:, j * 128:(j + 1) * 128],
                                                                   ident[:])), reads=[B("yn"), B("ident")], writes=[pTb])
                for j in range(4):
                    P.op("dve", (lambda e, pT_=pT_, j=j, isl=isl: e.tensor_scalar(
                        out=ynT[:, j, isl], in0=pT_[:, j * 128:(j + 1) * 128], scalar1=rwp[:, 4, j:j + 1],
                        scalar2=rwp[:, 5, j:j + 1], op0=ALU.mult, op1=ALU.add)),
                        reads=[pTb, B("rwp")], writes=[B("ynT%d" % j)])
                yield
            for j in range(4 if full else 0):
                pbn, pbnb = ps_next()
                P.op("pe", (lambda e, pbn=pbn, j=j: e.matmul(pbn[:, 0:NT], lhsT=bones[:], rhs=prod[:, j, :], start=True,
                                                             stop=True)), reads=[B("bones"), B("prod%d" % j)], writes=[pbnb])
                P.op("dve", (lambda e, pbn=pbn, j=j: e.tensor_tensor(out=T1[:, 18, :], in0=pbn[:, 0:NT], in1=zin(8 + j),
                                                                     op=ALU.mult)), reads=[pbnb, zb[8 + j]], writes=[tb[18]])
                P.op("pool", (lambda e, j=j: e.tensor_tensor(out=T1[:, 18, :], in0=T1[:, 18, :], in1=ynT[:, j, :], op=ALU.add)),
                     reads=[tb[18], B("ynT%d" % j)], writes=[tb[18]])
                P.op("pool", (lambda e, j=j: e.tensor_tensor(out=yrT[:, j, :], in0=T1[:, 18, :], in1=gT[:, j, :], op=ALU.mult)),
                     reads=[tb[18], B("gT%d" % j)], writes=[B("yrT%d" % j)])
                yield

        mrg = sb("mrg", [128, NKC, NT], BF16)
        mtmp = rtmp
        mb = [B("mrg%d" % c) for c in range(NKC)]

        def branch_chunk(wname, o, rhs_tile, rhs_bufs):
            w, wb = wload(wname, o, 4 * 128)
            pt, pbuf = ps_next()
            for kc in range(4):
                P.op("pe", (lambda e, pt=pt, w=w, kc=kc: e.matmul(
                    pt[:, 0:NT], lhsT=w[:, kc * 128:(kc + 1) * 128], rhs=rhs_tile[:, kc, :],
                    start=(kc == 0), stop=(kc == 3))), reads=[wb] + rhs_bufs, writes=[pbuf])
            return pt, pbuf

        def merge_out(st):
            hsel = st % 2
            yrb = [B("yrT%d" % c) for c in range(4)]
            for o in range(NKC):
                par = o % 2
                pa_, pab_ = branch_chunk("watt", o, yattT, [B("yattT")])
                P.op("dve", (lambda e, pa_=pa_, o=o, par=par: e.tensor_tensor(out=mtmp[:, par, 0, :], in0=pa_[:, 0:NT],
                                                                              in1=gates[:, o, :], op=ALU.mult)),
                     reads=[pab_, gb[o]], writes=[B("rtmp%d0" % par)])
                pr_, prb_ = branch_chunk("wrw", o, yrT, yrb)
                P.op("dve", (lambda e, pr_=pr_, o=o, par=par: e.tensor_tensor(out=mtmp[:, par, 1, :], in0=pr_[:, 0:NT],
                                                                              in1=gates[:, 8 + o, :], op=ALU.mult)),
                     reads=[prb_, gb[8 + o]], writes=[B("rtmp%d1" % par)])
                P.op("pool", (lambda e, o=o, par=par: e.tensor_tensor(out=mrg[:, o, :], in0=mtmp[:, par, 0, :],
                                                                      in1=mtmp[:, par, 1, :], op=ALU.add)),
                     reads=[B("rtmp%d0" % par), B("rtmp%d1" % par)], writes=[mb[o]])
            for c in range(NKC):
                w, wb = wload("wmix", c, NKC * 128)
                pt, pbuf = ps_next()
                for kc in range(NKC):
                    P.op("pe", (lambda e, pt=pt, w=w, kc=kc: e.matmul(
                        pt[:, 0:NT], lhsT=w[:, kc * 128:(kc + 1) * 128], rhs=mrg[:, kc, :],
                        start=(kc == 0), stop=(kc == NKC - 1))), reads=[wb, mb[kc]], writes=[pbuf])
                P.op("act", (lambda e, pt=pt, c=c: e.activation(out=fT[:, c, :], in_=pt[:, 0:NT], func=AF.Copy)),
                     reads=[pbuf], writes=[fb[c]])
            postnorm_add(3, False, hsel)

        outb = B("outT")

        def run_all(g):
            for _ in g:
                pass

        def interleave(ga, gb, ra, rb):
            if os.environ.get("KINT", "1") == "0":
                run_all(ga)
                run_all(gb)
                return
            la = lb = True
            while la or lb:
                for _ in range(ra):
                    if la:
                        try:
                            next(ga)
                        except StopIteration:
                            la = False
                for _ in range(rb):
                    if lb:
                        try:
                            next(gb)
                        except StopIteration:
                            lb = False

        def ffn1_front(st):
            hsel = st % 2
            ht, hbs = HT[hsel]
            t0 = st * NT
            P.dma("sp", "xin", (lambda e, t0=t0, ht=ht: e.dma_start(out=ht[:], in_=xT[:, :, t0:t0 + NT])),
                  writes=hbs)
            prenorm(0, hsel)
            yield
            yield from ffn("gu1", "d1")
            postnorm_add(1, True, hsel)
            yield

        def until(g, sentinel):
            for v in g:
                if v == sentinel:
                    return True
            return False

        def interleave2(ga, gb, ra, rb, stop_b=None):
            la = lb = True
            while la or lb:
                for _ in range(ra):
                    if la:
                        try:
                            next(ga)
                        except StopIteration:
                            la = False
                for _ in range(rb):
                    if lb:
                        try:
                            if next(gb) == stop_b and stop_b is not None:
                                lb = False
                        except StopIteration:
                            lb = False

        def chain(*gs):
            for g in gs:
                yield from g

        run_all(ffn1_front(0))
        inp_g = mixer_inproj(0, npre == 0)
        until(inp_g, "RWKV_IN_DONE") if npre == 0 else run_all(inp_g)
        for st in range(nsup):
            full = st >= npre
            nxt = st + 1
            nfull = nxt >= npre
            if not full:
                nin = mixer_inproj(nxt, nfull)

                def nxt_front(nin=nin, nxt=nxt):
                    yield from ffn1_front(nxt)
                    for v in nin:
                        if v == "RWKV_IN_DONE":
                            return
                        yield
                interleave2(rwkv(st, False), nxt_front(), 3, 1)
                inp_g = nin
                continue
            rg = rwkv(st, True)
            interleave2(chain(inp_g, attention(st)), rg, 1, 2, stop_b="PREP_DONE")
            if nxt < nsup:
                interleave2(ffn1_front(nxt), rg, 1, 3)
            else:
                run_all(rg)
            merge_out(st)
            prenorm(4, st % 2)
            f2 = ffn("gu2", "d2")
            until(f2, "UPDONE")
            if nxt < nsup:
                inp_g = mixer_inproj(nxt, True)
                interleave2(f2, inp_g, 1, 2, stop_b="RWKV_IN_DONE")
            else:
                run_all(f2)
            postnorm_add(5, True, st % 2)
            ht, hbs = HT[st % 2]
            o0 = (st - npre) * NT
            P.dma("sp", "xout", (lambda e, o0=o0, ht=ht: e.dma_start(out=outT[:, :, o0:o0 + NT], in_=ht[:])),
                  reads=hbs, writes=[outb])
        P.final_wait("sp", [outb])
        P.emit(block)
    return nc


def _tile_w(w, kdim, ncols_chunks):
    K_, N_ = w.shape
    kc = K_ // 128
    nj = N_ // 128
    t = w.reshape(kc, 128, nj, 128).transpose(2, 1, 0, 3)
    return np.ascontiguousarray(t.reshape(nj, 128, kc * 128))


def _prep_weights(inp):
    out = {}
    for i, (gu, dn) in enumerate((("ffn1_w_gate_up", "ffn1_w_down"), ("ffn2_w_gate_up", "ffn2_w_down"))):
        w = np.asarray(inp[gu][0])
        g = _tile_w(w[:, :DFF], D, NHC)
        u = _tile_w(w[:, DFF:], D, NHC)
        out["gu%d" % (i + 1)] = np.ascontiguousarray(np.concatenate([g, u], axis=2))
        out["d%d" % (i + 1)] = _tile_w(np.asarray(inp[dn][0]), DFF, NKC)
    win = np.asarray(inp["w_in"][0])
    def perm_heads(wc, nh):
        wc = wc.reshape(D, nh, 64)
        idx = np.concatenate([np.arange(8, 16), np.arange(0, 8), np.arange(16, 64)])
        return wc[:, :, idx].reshape(D, nh * 64)
    wq = win[:, 0:512]
    wk = win[:, 512:640]
    wv = win[:, 640:768]
    wk_sw = np.concatenate([wk[:, 64:], wk[:, :64]], axis=1)
    cols = [wq, perm_heads(wq, 8), wk, perm_heads(wk, 2), wk_sw, perm_heads(wk_sw, 2), wv,
            win[:, 768:768 + 1792], win[:, 2560:4608]]
    wext = np.concatenate(cols, axis=1)
    assert wext.shape[1] == NIN * 128
    out["win"] = _tile_w(wext, D, NIN)
    out["watt"] = _tile_w(np.asarray(inp["w_att_branch"][0]), 512, NKC)
    out["wrw"] = _tile_w(np.asarray(inp["w_rwkv_branch"][0]), 512, NKC)
    out["wmix"] = _tile_w(np.asarray(inp["w_mix_out"][0]), D, NKC)
    gl = ["ffn1_norm_pre", "ffn1_norm_post", "mix_norm_pre", "mix_norm_post", "ffn2_norm_pre", "ffn2_norm_post"]
    gains = np.stack([np.asarray(inp[g][0]).reshape(NKC, 128).T for g in gl], axis=1)
    out["gains"] = np.ascontiguousarray(gains.astype(np.float32))
    return out


def _prep_consts(inp, nmain, main0, special_masks):
    out = {}
    ntok = nmain * NT
    pos = (np.arange(ntok) + main0 * NT - FRONT).astype(np.float64)
    inv = 1.0 / (500000.0 ** (np.arange(8, dtype=np.float64) * (2.0 / 16.0)))
    C = np.ones((128, ntok), np.float64)
    S = np.zeros((128, ntok), np.float64)
    for r in range(128):
        d = r % 64
        if d < 16:
            ang = pos * inv[d % 8]
            C[r] = np.cos(ang)
            S[r] = -np.sin(ang) if d < 8 else np.sin(ang)
    cs = np.stack([C, S], axis=1).reshape(128, 2, nmain, NT).transpose(2, 0, 1, 3)
    out["rope"] = np.ascontiguousarray(cs.astype(np.float32))
    j = np.arange(128)[:, None]
    i = np.arange(128)[None, :]
    if special_masks:
        m = [(j <= i), (j > i), (j <= i) & (j >= 112), (j > i) & (j >= 112), np.zeros((128, 128), bool), (j < i)]
    else:
        m = [(j <= i), (j > i), (j <= i), (j > i), (j > i), (j < i)]
    mk = np.stack([np.tile(a.astype(np.float32), (1, 4)) for a in m], axis=1)
    out["masks"] = np.ascontiguousarray(mk)
    out["ident"] = np.eye(128, dtype=np.float32)
    bo = np.zeros((128, 128), np.float32)
    bo[:64, :64] = 1.0
    bo[64:, 64:] = 1.0
    out["bones"] = bo
    def pc(name, n):
        return np.asarray(inp[name][0]).reshape(n, 128).T
    kinds = ["rwkv_w0", "rwkv_a0", "rwkv_k_k", "rwkv_k_a", "rwkv_ln_w", "rwkv_ln_b"]
    rwp = [pc(k, 4) for k in kinds] + [np.asarray(inp["rwkv_r_k"][0]).reshape(4, 128).T, np.zeros((128, 4), np.float32)]
    out["rwp"] = np.ascontiguousarray(np.stack(rwp, axis=1).astype(np.float32))
    out["mu"] = np.ascontiguousarray(pc("rwkv_mu", 14).astype(np.float32))
    lw = np.zeros((128, 3, 512), np.float32)
    lw[:64, 0] = np.asarray(inp["rwkv_w2"][0])
    lw[64:, 1] = np.asarray(inp["rwkv_a2"][0])
    lw[:, 2] = np.asarray(inp["rwkv_g2"][0])
    out["loraw"] = lw
    out["sinks"] = np.ascontiguousarray(np.tile(np.asarray(inp["att_sinks"][0]).reshape(1, 8), (128, 1)).astype(np.float32))
    return out


def _padded_seq(x_b, meta, nsup):
    ntok = nsup * NT
    seq = np.zeros((ntok, D), np.float32)
    seq[FRONT:FRONT + NMETA] = meta
    n = min(x_b.shape[0], ntok - FRONT - NMETA)
    seq[FRONT + NMETA:FRONT + NMETA + n] = x_b[:n]
    return seq


def _fm(seq):
    return np.ascontiguousarray(seq.reshape(seq.shape[0], NKC, 128).transpose(2, 1, 0))


def kernel(**inp):
    x = np.asarray(inp["x"])
    meta = np.asarray(inp["meta_tokens"])
    wts = _prep_weights(inp)
    cA = _prep_consts(inp, NMAIN, 0, True)
    cB = _prep_consts(inp, NMAIN, NPRE, False)
    nc = build_program(NPRE, NMAIN)
    in_maps = []
    for c in range(8):
        b, half = c // 2, c % 2
        seq = _padded_seq(x[b], meta, NSUP_FULL)
        m = dict(wts)
        if half == 0:
            m.update(cA)
            m["xT"] = _fm(np.concatenate([np.zeros((NPRE * NT, D), np.float32), seq[:NMAIN * NT]], axis=0))
        else:
            m.update(cB)
            m["xT"] = _fm(seq)
        in_maps.append(m)
    res = run_bass_kernel_spmd(nc, in_maps, core_ids=list(range(8)))
    outs = []
    for b in range(4):
        oA = res.results[2 * b]["outT"].transpose(2, 1, 0).reshape(-1, D)
        oB = res.results[2 * b + 1]["outT"].transpose(2, 1, 0).reshape(-1, D)
        full = np.concatenate([oA, oB[(NMAIN - NPRE) * NT:]], axis=0)
        outs.append(full[FRONT + NMETA:FRONT + NMETA + SEQ])
    return np.stack(outs, axis=0).astype(np.float32)
```

```python
import os
import numpy as np
import concourse.bass as bass
import concourse.mybir as mybir
from concourse.bass_utils import run_bass_kernel_spmd

F32 = mybir.dt.float32
BF16 = mybir.dt.bfloat16
AF = mybir.ActivationFunctionType
ALU = mybir.AluOpType
AX = mybir.AxisListType

D = 1024
DFF = 2816
NKC = 8
NHC = 22
SEQ = 8192
NMETA = 16
SUBS = 2
NT = 128 * SUBS
FRONT = 240
NSUP_FULL = 33
NPRE = 16
NMAIN = 17
EPS = 1e-6
GN_EPS = 64e-5
CDEC = float(np.exp(-0.5))
WSLOT = 22 * 128
NWSLOT = 4

IQ, IQP, IKA, IKPA, IKB, IKPB, IV, IRW, IGA, IGR = 0, 4, 8, 9, 10, 11, 12, 13, 27, 35
NIN = 43


class Buf:
    __slots__ = ("name", "w", "r")

    def __init__(self, name):
        self.name = name
        self.w = None
        self.r = []


class Prog:
    def __init__(self, nc, ctx):
        self.nc = nc
        self.ctx = ctx
        self.engs = {"pe": nc.tensor, "act": nc.scalar, "dve": nc.vector, "pool": nc.gpsimd, "sp": nc.sync}
        self.sems = {}
        self.cnt = {}
        self.recs = {k: [] for k in self.engs}
        self.seen = {k: {} for k in self.engs}
        for k in self.engs:
            self.sems[k] = ctx.enter_context(nc.semaphore("s_" + k))
            self.cnt[k] = 0
        self.ndma = 0

    def dma_sem(self, name):
        key = "dma_" + name
        if key not in self.sems:
            self.sems[key] = self.ctx.enter_context(self.nc.semaphore(key))
            self.cnt[key] = 0
        return key

    def _deps(self, eng, reads, writes):
        need = {}
        for b in reads:
            if b.w is not None:
                k, v = b.w
                need[k] = max(need.get(k, 0), v)
        for b in writes:
            if b.w is not None:
                k, v = b.w
                need[k] = max(need.get(k, 0), v)
            for (k, v) in b.r:
                need[k] = max(need.get(k, 0), v)
        waits = []
        seen = self.seen[eng]
        for k, v in need.items():
            if k == eng and eng == "pe":
                continue
            if seen.get(k, 0) < v:
                seen[k] = v
                waits.append((k, v))
        return waits

    def op(self, eng, fn, reads=(), writes=()):
        waits = self._deps(eng, reads, writes)
        self.cnt[eng] += 1
        v = self.cnt[eng]
        self.recs[eng].append((waits, fn, eng, 1))
        for b in reads:
            b.r.append((eng, v))
        for b in writes:
            b.w = (eng, v)
            b.r = []

    def dma(self, eng, semname, fn, reads=(), writes=()):
        key = self.dma_sem(semname)
        waits = self._deps(eng, reads, writes)
        self.cnt[key] += 16
        v = self.cnt[key]
        self.recs[eng].append((waits, fn, key, 16))
        for b in reads:
            b.r.append((key, v))
        for b in writes:
            b.w = (key, v)
            b.r = []

    def final_wait(self, eng, bufs):
        waits = self._deps(eng, bufs, bufs)
        self.recs[eng].append((waits, None, None, 0))

    def emit(self, block):
        prog = self

        def run(name, e):
            for waits, fn, key, inc in prog.recs[name]:
                for (k, v) in waits:
                    e.wait_ge(prog.sems[k], v)
                if fn is not None:
                    fn(e).then_inc(prog.sems[key], inc)

        @block.tensor
        def _(e):
            run("pe", e)

        @block.scalar
        def _(e):
            run("act", e)

        @block.vector
        def _(e):
            run("dve", e)

        @block.gpsimd
        def _(e):
            run("pool", e)

        @block.sync
        def _(e):
            run("sp", e)


class K:
    def __init__(self, nc, ctx, nsup, stage, dbg):
        self.nc = nc
        self.ctx = ctx
        self.P = Prog(nc, ctx)
        self.nsup = nsup
        self.stage = stage
        self.dbg = dbg
        self.bufs = {}
        self.wslot_i = 0
        self.ps_i = 0
        self.ps_ring = [0, 1, 2, 3, 4, 5]

    def sb(self, name, shape, dt):
        return self.ctx.enter_context(self.nc.sbuf_tensor(name, shape, dt))

    def B(self, name):
        if name not in self.bufs:
            self.bufs[name] = Buf(name)
        return self.bufs[name]


def build_program(npre=NPRE, nmain=NMAIN, stage=99, dbg=False):
    from contextlib import ExitStack
    nc = bass.Bass("TRN2", target_bir_lowering=False)
    nsup = npre + nmain
    ntok = nsup * NT

    def din(name, shape, dt=F32):
        return nc.dram_tensor(name, shape, dt, kind="ExternalInput").ap()

    xT = din("xT", [128, NKC, ntok])
    outT = nc.dram_tensor("outT", [128, NKC, nmain * NT], F32, kind="ExternalOutput").ap()
    wsrc = {
        "gu1": din("gu1", [NHC, 128, 2 * NKC * 128]),
        "d1": din("d1", [NKC, 128, NHC * 128]),
        "gu2": din("gu2", [NHC, 128, 2 * NKC * 128]),
        "d2": din("d2", [NKC, 128, NHC * 128]),
        "win": din("win", [NIN, 128, NKC * 128]),
        "watt": din("watt", [NKC, 128, 4 * 128]),
        "wrw": din("wrw", [NKC, 128, 4 * 128]),
        "wmix": din("wmix", [NKC, 128, NKC * 128]),
    }
    gains = din("gains", [128, 6, NKC])
    rope = din("rope", [nmain, 128, 2, NT])
    masks = din("masks", [128, 6, 512])
    ident_in = din("ident", [128, 128])
    sinks_in = din("sinks", [128, 8])
    rwp_in = din("rwp", [128, 8, 4])
    mu_in = din("mu", [128, 14])
    lora_in_d = din("loraw", [128, 3, 512])
    bones_in = din("bones", [128, 128])
    wscr = {}
    for k, ap in wsrc.items():
        shp = list(ap.shape)
        wscr[k] = nc.dram_tensor("scr_" + k, shp, BF16, kind="Internal").ap()

    with ExitStack() as ctx:
        kb = K(nc, ctx, nsup, stage, dbg)
        kb.npre = npre
        P = kb.P
        B = kb.B
        sb = kb.sb
        block = ctx.enter_context(nc.Block())

        hT = sb("hT", [128, NKC, NT], F32)
        hT2 = sb("hT2", [128, NKC, NT], F32)
        xn = sb("xn", [128, NKC, NT], BF16)
        hid = sb("hid", [128, NHC, NT], BF16)
        fT = sb("fT", [128, NKC, NT], F32)
        sq = sb("sq", [128, NKC, NT], BF16)
        rstd = sb("rstd", [128, NT], F32)
        sgt = sb("sgt", [128, 2, NT], F32)
        gn = sb("gn", [128, 6, NKC], F32)
        gnh = sb("gnh", [128, 6, NKC], F32)
        ones_bf = sb("ones_bf", [128, 128], BF16)
        wring = sb("wring", [128, NWSLOT, WSLOT], BF16)
        psum = [ctx.enter_context(nc.psum_tensor("ps%d" % i, [128, 512], F32)) for i in range(8)]

        def ps_next():
            ring = kb.ps_ring
            i = ring[kb.ps_i % len(ring)]
            kb.ps_i += 1
            return psum[i], B("ps%d" % i)

        for k in wsrc:
            s, d = wsrc[k], wscr[k]
            n0 = s.shape[0]
            for i in range(n0):
                P.dma("pool", "cv_" + k, (lambda e, s=s, d=d, i=i: e.dma_start(out=d[i], in_=s[i])),
                      writes=[B("scr_" + k)] if i == n0 - 1 else [])
            B("scr_" + k).w = ("dma_cv_" + k, P.cnt["dma_cv_" + k])

        P.dma("sp", "c0", lambda e: e.dma_start(out=gn[:], in_=gains[:, :, :]), writes=[B("gn")])
        P.op("pool", lambda e: e.memset(ones_bf[:], 1.0), writes=[B("ones")])
        P.op("dve", lambda e: e.tensor_scalar(out=gnh[:], in0=gn[:], scalar1=0.5, scalar2=None, op0=ALU.mult),
             reads=[B("gn")], writes=[B("gnh")])

        def wload(name, idx, nel):
            slot = kb.wslot_i % NWSLOT
            kb.wslot_i += 1
            bslot = B("wslot%d" % slot)
            src = wscr[name]
            P.dma("sp", "w%d" % slot,
                  (lambda e, slot=slot, src=src, idx=idx, nel=nel:
                   e.dma_start(out=wring[:, slot, 0:nel], in_=src[idx])),
                  reads=[B("scr_" + name)], writes=[bslot])
            return wring[:, slot, :], bslot

        def rmsnorm_stats(src_tile, src_bufs, from_psum=None):
            for c in range(NKC):
                eng = ("act", "dve", "pool")[c % 3]
                if eng == "act":
                    P.op("act", (lambda e, c=c: e.activation(out=sq[:, c, :], in_=src_tile[:, c, :], func=AF.Square)),
                         reads=[src_bufs[c]], writes=[B("sq%d" % c)])
                else:
                    P.op(eng, (lambda e, c=c: e.tensor_tensor(out=sq[:, c, :], in0=src_tile[:, c, :], in1=src_tile[:, c, :],
                                                              op=ALU.mult)),
                         reads=[src_bufs[c]], writes=[B("sq%d" % c)])
            pt, pb = ps_next()
            for c in range(NKC):
                P.op("pe", (lambda e, c=c, pt=pt: e.matmul(pt[:, 0:NT], lhsT=ones_bf[:], rhs=sq[:, c, :],
                                                           start=(c == 0), stop=(c == NKC - 1))),
                     reads=[B("ones"), B("sq%d" % c)], writes=[pb])
            P.op("act", (lambda e, pt=pt: e.activation(out=rstd[:], in_=pt[:, 0:NT], func=AF.Ln,
                                                       scale=1.0 / D, bias=epsb[:, 0:1])),
                 reads=[pb, B("epsb")], writes=[B("rstd")])
            P.op("act", lambda e: e.activation(out=rstd[:], in_=rstd[:], func=AF.Exp, scale=-0.5),
                 reads=[B("rstd")], writes=[B("rstd")])

        epsb = sb("epsb", [128, 2], F32)
        P.op("pool", lambda e: e.memset(epsb[:, 0:1], EPS), writes=[B("epsb")])
        P.op("pool", lambda e: e.memset(epsb[:, 1:2], GN_EPS), writes=[B("epsb")])

        hb = [B("hT%d" % c) for c in range(NKC)]
        hb2 = [B("hT2_%d" % c) for c in range(NKC)]
        HT = [(hT, hb), (hT2, hb2)]
        fb = [B("fT%d" % c) for c in range(NKC)]
        xb = [B("xn%d" % c) for c in range(NKC)]

        def prenorm(gi, hsel=0):
            hT, hb = HT[hsel]
            rmsnorm_stats(hT, hb)
            for c in range(NKC):
                P.op("dve", (lambda e, c=c, hT=hT: e.scalar_tensor_tensor(
                    out=xn[:, c, :], in0=hT[:, c, :], scalar=gn[:, gi, c:c + 1], op0=ALU.mult,
                    in1=rstd[:], op1=ALU.mult)),
                    reads=[hb[c], B("gn"), B("rstd")], writes=[xb[c]])

        def postnorm_add(gi, half, hsel=0):
            hT, hb = HT[hsel]
            rmsnorm_stats(fT, fb)
            g = gnh if half else gn
            for c in range(NKC):
                P.op("dve", (lambda e, c=c: e.scalar_tensor_tensor(
                    out=fT[:, c, :], in0=fT[:, c, :], scalar=g[:, gi, c:c + 1], op0=ALU.mult,
                    in1=rstd[:], op1=ALU.mult)),
                    reads=[fb[c], B("gnh"), B("gn"), B("rstd")], writes=[fb[c]])
                P.op("pool", (lambda e, c=c, hT=hT: e.tensor_tensor(out=hT[:, c, :], in0=hT[:, c, :], in1=fT[:, c, :],
                                                             op=ALU.add)),
                     reads=[fb[c], hb[c]], writes=[hb[c]])

        def ffn(wgu, wd):
            hidb = [B("hid%d" % j) for j in range(NHC)]
            for j in range(NHC):
                w, wb = wload(wgu, j, 2 * NKC * 128)
                pg, pgb = ps_next()
                pu, pub = ps_next()
                for g, (pt, pbuf) in enumerate(((pg, pgb), (pu, pub))):
                    for kc in range(NKC):
                        off = (g * NKC + kc) * 128
                        P.op("pe", (lambda e, pt=pt, w=w, off=off, kc=kc: e.matmul(
                            pt[:, 0:NT], lhsT=w[:, off:off + 128], rhs=xn[:, kc, :],
                            start=(kc == 0), stop=(kc == NKC - 1))),
                            reads=[wb, xb[kc]], writes=[pbuf])
                s = j % 2
                P.op("act", (lambda e, pg=pg, s=s: e.activation(out=sgt[:, s, :], in_=pg[:, 0:NT], func=AF.Silu)),
                     reads=[pgb], writes=[B("sgt%d" % s)])
                P.op("dve", (lambda e, pu=pu, s=s, j=j: e.tensor_tensor(out=hid[:, j, :], in0=sgt[:, s, :],
                                                                       in1=pu[:, 0:NT], op=ALU.mult)),
                     reads=[B("sgt%d" % s), pub], writes=[hidb[j]])
                yield
            for c in range(NKC):
                w, wb = wload(wd, c, NHC * 128)
                pt, pbuf = ps_next()
                for kc in range(NHC):
                    P.op("pe", (lambda e, pt=pt, w=w, kc=kc: e.matmul(
                        pt[:, 0:NT], lhsT=w[:, kc * 128:(kc + 1) * 128], rhs=hid[:, kc, :],
                        start=(kc == 0), stop=(kc == NHC - 1))),
                        reads=[wb, hidb[kc]], writes=[pbuf])
                P.op("act", (lambda e, pt=pt, c=c: e.activation(out=fT[:, c, :], in_=pt[:, 0:NT], func=AF.Copy)),
                     reads=[pbuf], writes=[fb[c]])
                yield

        T1s = sb("T1", [128, 19, NT], F32)
        ropet = sb("ropet", [128, 2, NT], F32)
        rtmp = sb("rtmp", [128, 2, 2, NT], F32)
        qrot = sb("qrot", [128, 4, NT], BF16)
        kT = [sb("kTa", [128, (1 + SUBS) * 128], BF16), sb("kTb", [128, (1 + SUBS) * 128], BF16)]
        vtok = sb("vtok", [128, 1 + SUBS, 2, 65], BF16)
        zr = sb("zr", [128, 14, NT + 1], F32)
        gates = sb("gates", [128, 16, NT], BF16)
        PT = sb("PT", [128, 16, 128], BF16)
        maskb = sb("maskb", [128, 6, 128], BF16)
        esink = sb("esink", [128, 8], F32)
        den = sb("den", [128, 2, 8], F32)
        yatt = sb("yatt", [128, 512], F32)
        yattT = sb("yattT", [128, 4, NT], BF16)
        ident = sb("ident_sb", [128, 128], F32)

        stg = T1s[:, 0:4, :].rearrange("p (a b) c -> p a (b c)", a=2)
        kb.stg_i = 0

        def load_cast(dst_ap, src_ap, dst_buf):
            k = kb.stg_i % 2
            kb.stg_i += 1
            P.dma("sp", "stg%d" % k, (lambda e: e.dma_start(out=stg[:, k, :], in_=src_ap)), writes=[B("stg%d" % k)])
            P.op("dve", (lambda e: e.tensor_copy(out=dst_ap, in_=stg[:, k, :])), reads=[B("stg%d" % k)],
                 writes=[dst_buf])

        for mv in range(6):
            k_ = kb.stg_i % 2
            kb.stg_i += 1
            P.dma("sp", "stg%d" % k_, (lambda e, k_=k_, mv=mv: e.dma_start(out=stg[:, k_, 0:128], in_=masks[:, mv, 0:128])),
                  writes=[B("stg%d" % k_)])
            P.op("dve", (lambda e, k_=k_, mv=mv: e.tensor_copy(out=maskb[:, mv, :], in_=stg[:, k_, 0:128])),
                 reads=[B("stg%d" % k_)], writes=[B("maskb")])
        P.dma("sp", "c2", lambda e: e.dma_start(out=ident[:], in_=ident_in[:, :]), writes=[B("ident")])
        P.dma("sp", "c3", lambda e: e.dma_start(out=esink[:], in_=sinks_in[:, :]), writes=[B("esink")])
        P.op("act", lambda e: e.activation(out=esink[:], in_=esink[:], func=AF.Exp), reads=[B("esink")],
             writes=[B("esink")])
        kslot = [[B("kT%d_%d" % (a, sl)) for sl in range(1 + SUBS)] for a in range(2)]
        vslot = [B("v_%d" % sl) for sl in range(1 + SUBS)]
        P.op("pool", lambda e: e.memset(kT[0][:], 0.0), writes=kslot[0])
        P.op("pool", lambda e: e.memset(kT[1][:], 0.0), writes=kslot[1])
        P.op("pool", lambda e: e.memset(vtok[:], 0.0), writes=vslot)
        P.op("pool", lambda e: e.memset(vtok[:, :, :, 64:65], 1.0), writes=vslot)
        P.op("pool", lambda e: e.memset(zr[:, :, 0:1], 0.0), writes=[B("zrp%d" % c) for c in range(14)])
        qb = [B("qrot%d" % c) for c in range(4)]
        zb = [B("zr%d" % c) for c in range(14)]
        gb = [B("gate%d" % c) for c in range(16)]

        def inproj_chunk(ci):
            w, wb = wload("win", ci, NKC * 128)
            pt, pbuf = ps_next()
            for kc in range(NKC):
                P.op("pe", (lambda e, pt=pt, w=w, kc=kc: e.matmul(
                    pt[:, 0:NT], lhsT=w[:, kc * 128:(kc + 1) * 128], rhs=xn[:, kc, :],
                    start=(kc == 0), stop=(kc == NKC - 1))),
                    reads=[wb, xb[kc]], writes=[pbuf])
            return pt, pbuf

        def rope_pair(ci, cpi, out_ap, out_bufs, par):
            p1, b1 = inproj_chunk(ci)
            p2, b2 = inproj_chunk(cpi)
            P.op("dve", (lambda e, p1=p1, par=par: e.tensor_tensor(out=rtmp[:, par, 0, :], in0=p1[:, 0:NT],
                                                                   in1=ropet[:, 0, :], op=ALU.mult)),
                 reads=[b1, B("ropet")], writes=[B("rtmp%d0" % par)])
            P.op("dve", (lambda e, p2=p2, par=par: e.tensor_tensor(out=rtmp[:, par, 1, :], in0=p2[:, 0:NT],
                                                                   in1=ropet[:, 1, :], op=ALU.mult)),
                 reads=[b2, B("ropet")], writes=[B("rtmp%d1" % par)])
            P.op("pool", (lambda e, par=par: e.tensor_tensor(out=out_ap, in0=rtmp[:, par, 0, :],
                                                             in1=rtmp[:, par, 1, :], op=ALU.add)),
                 reads=[B("rtmp%d0" % par), B("rtmp%d1" % par)], writes=out_bufs)

        def mask_ids(gt):
            if gt == 1:
                return 2, 4
            if gt == 2:
                return 0, 3
            return 0, 1

        def mixer_inproj(st, full=True):
            prenorm(2, st % 2)
            yield
            if not full:
                for c in range(13):
                    pt, pbuf = inproj_chunk(IRW + c)
                    P.op("act", (lambda e, pt=pt, c=c: e.activation(out=zr[:, c, 1:NT + 1], in_=pt[:, 0:NT], func=AF.Copy)),
                         reads=[pbuf], writes=[zb[c]])
                    yield
                return
            ml = st - kb.npre
            P.dma("sp", "rope", (lambda e: e.dma_start(out=ropet[:], in_=rope[ml])), writes=[B("ropet")])
            for a in range(2):
                P.op("pool", (lambda e, a=a: e.tensor_copy(out=kT[a][:, 0:128], in_=kT[a][:, SUBS * 128:(SUBS + 1) * 128])),
                     reads=[kslot[a][SUBS]], writes=[kslot[a][0]])
            P.op("pool", lambda e: e.tensor_copy(out=vtok[:, 0, :, 0:64], in_=vtok[:, SUBS, :, 0:64]),
                 reads=[vslot[SUBS]], writes=[vslot[0]])
            for c in range(14):
                pt, pbuf = inproj_chunk(IRW + c)
                P.op("act", (lambda e, pt=pt, c=c: e.activation(out=zr[:, c, 1:NT + 1], in_=pt[:, 0:NT], func=AF.Copy)),
                     reads=[pbuf], writes=[zb[c]])
                yield
            yield "RWKV_IN_DONE"
            for c in range(4):
                rope_pair(IQ + c, IQP + c, qrot[:, c, :], [qb[c]], c % 2)
                yield
            rope_pair(IKA, IKPA, kT[0][:, 128:(1 + SUBS) * 128], kslot[0][1:], 0)
            yield
            rope_pair(IKB, IKPB, kT[1][:, 128:(1 + SUBS) * 128], kslot[1][1:], 1)
            yield
            wv, wvb = wload("win", IV, NKC * 128)
            for i in range(SUBS):
                pt, pbuf = ps_next()
                for kc in range(NKC):
                    P.op("pe", (lambda e, pt=pt, kc=kc, i=i: e.matmul(
                        pt[:, 0:128], lhsT=xn[:, kc, i * 128:(i + 1) * 128], rhs=wv[:, kc * 128:(kc + 1) * 128],
                        start=(kc == 0), stop=(kc == NKC - 1))),
                        reads=[wvb, xb[kc]], writes=[pbuf])
                P.op("act", (lambda e, pt=pt, i=i: e.activation(
                    out=vtok[:, 1 + i, :, 0:64], in_=pt[:, 0:128].rearrange("p (a b) -> p a b", a=2), func=AF.Copy)),
                    reads=[pbuf], writes=[vslot[1 + i]])
            yield
            for c in range(16):
                pt, pbuf = inproj_chunk(IGA + c)
                P.op("act", (lambda e, pt=pt, c=c: e.activation(out=gates[:, c, :], in_=pt[:, 0:NT], func=AF.Sigmoid)),
                     reads=[pbuf], writes=[gb[c]])
                yield

        def attention(st):
            CUT = 99
            for i in range(SUBS):
                mc, mp = mask_ids((st - kb.npre) * SUBS + i)
                banks = [ps_next() for _ in range(4)]
                for cp in range(2):
                    slot = 1 + i - cp
                    for h in range(8):
                        pbs = (h % 2) * 64
                        g = h // 4
                        a = 0 if g == (h % 2) else 1
                        pt, pbuf = banks[cp * 2 + h % 2]
                        hh = h // 2
                        P.op("pe", (lambda e, pt=pt, a=a, pbs=pbs, slot=slot, h=h, hh=hh, i=i: e.matmul(
                            pt[:, hh * 128:(hh + 1) * 128],
                            lhsT=kT[a][pbs:pbs + 64, slot * 128:(slot + 1) * 128],
                            rhs=qrot[pbs:pbs + 64, h // 2, i * 128:(i + 1) * 128], start=True, stop=True)),
                            reads=[kslot[a][slot], qb[h // 2]], writes=[pbuf])
                KSUB = 99
                for bi in range(4):
                    if KSUB <= 0:
                        break
                    pt, pbuf = banks[bi]
                    mi = mc if bi < 2 else mp
                    P.op("act", (lambda e, pt=pt, bi=bi: e.activation(
                        out=PT[:, bi * 4:(bi + 1) * 4, :], in_=pt[:, :].rearrange("p (a b) -> p a b", a=4),
                        func=AF.Exp, scale=0.125)),
                        reads=[pbuf], writes=[B("PT%d" % bi)])
                    if KSUB <= 1:
                        continue
                    P.op("pool", (lambda e, bi=bi, mi=mi: e.tensor_tensor(
                        out=PT[:, bi * 4:(bi + 1) * 4, :], in0=PT[:, bi * 4:(bi + 1) * 4, :],
                        in1=maskb[:, mi:mi + 1, :].broadcast_to([128, 4, 128]), op=ALU.mult)),
                        reads=[B("PT%d" % bi), B("maskb")], writes=[B("PT%d" % bi)])
                if CUT <= 2:
                    continue
                yield
                obanks = [ps_next() for _ in range(2)]
                for h in range(8):
                    g = h // 4
                    pt, pbuf = obanks[h // 4]
                    hh = h % 4
                    for cp in range(2):
                        slot = 1 + i - cp
                        P.op("pe", (lambda e, pt=pt, hh=hh, cp=cp, h=h, slot=slot, g=g: e.matmul(
                            pt[:, hh * 65:(hh + 1) * 65], lhsT=PT[:, cp * 8 + (h % 2) * 4 + h // 2, :],
                            rhs=vtok[:, slot, g, :], start=(cp == 0), stop=(cp == 1))),
                            reads=[B("PT%d" % (cp * 2 + h % 2)), vslot[slot]], writes=[pbuf])
                for hb2 in range(2):
                    pt, pbuf = obanks[hb2]
                    o3 = pt[:, 0:260].rearrange("p (a b) -> p a b", b=65)
                    P.op("dve", (lambda e, o3=o3, hb2=hb2: e.tensor_tensor(
                        out=den[:, 0, hb2 * 4:(hb2 + 1) * 4].unsqueeze(2), in0=o3[:, :, 64:65],
                        in1=esink[:, hb2 * 4:(hb2 + 1) * 4].unsqueeze(2), op=ALU.add)),
                        reads=[pbuf, B("esink")], writes=[B("den%d" % hb2)])
                    P.op("dve", (lambda e, hb2=hb2: e.reciprocal(out=den[:, 1, hb2 * 4:(hb2 + 1) * 4],
                                                                in_=den[:, 0, hb2 * 4:(hb2 + 1) * 4])),
                         reads=[B("den%d" % hb2)], writes=[B("den%d" % hb2)])
                    P.op("dve", (lambda e, o3=o3, hb2=hb2: e.tensor_tensor(
                        out=yatt[:, hb2 * 256:(hb2 + 1) * 256].rearrange("p (a b) -> p a b", b=64),
                        in0=o3[:, :, 0:64],
                        in1=den[:, 1, hb2 * 4:(hb2 + 1) * 4].unsqueeze(2).broadcast_to([128, 4, 64]), op=ALU.mult)),
                        reads=[pbuf, B("den%d" % hb2)], writes=[B("yatt%d" % hb2)])
                if CUT <= 3:
                    continue
                pt, pbuf = ps_next()
                for kc in range(4):
                    P.op("pe", (lambda e, pt=pt, kc=kc: e.transpose(pt[:, kc * 128:(kc + 1) * 128],
                                                                    yatt[:, kc * 128:(kc + 1) * 128], ident[:])),
                         reads=[B("yatt%d" % (kc // 2)), B("ident")], writes=[pbuf])
                P.op("act", (lambda e, pt=pt, i=i: e.activation(
                    out=yattT[:, :, i * 128:(i + 1) * 128], in_=pt[:, :].rearrange("p (a b) -> p a b", a=4),
                    func=AF.Copy)),
                    reads=[pbuf], writes=[B("yattT")])
                yield

        rwp = sb("rwp_sb", [128, 9, 4], F32)
        mus = sb("mus", [128, 14], F32)
        loraw = sb("loraw_sb", [128, 3, 512], BF16)
        bones = sb("bones_sb", [128, 128], BF16)
        identb = sb("identb", [128, 128], BF16)
        ones_f = sb("ones_f", [128, 128], F32)
        mX = sb("mX", [128, 384], BF16)
        mY = sb("mY", [128, 256], BF16)
        lin = sb("lin", [128, NT], BF16)
        sgx = sb("sgx", [128, NT], BF16)
        dz = sgt
        T1 = T1s
        gT = sb("gT", [128, 4, NT], BF16)
        ksq = sb("ksq", [128, NT], BF16)
        gg = sb("gg", [128, 2, 2, SUBS, 129], F32)
        sc = sb("sc", [128, 4, 4, SUBS], F32)
        opb = sb("opb", [128, 5, 4, NT], BF16)
        rt0 = sb("rt0", [128, 4, NT], F32)
        prod = sb("prod", [128, 4, NT], BF16)
        tokm = sb("tokm", [128, SUBS, 4, 512], BF16)
        AX8 = sb("AX8", [128, 8, 384], BF16)
        AY8 = sb("AY8", [128, 8, 256], BF16)
        SQ8 = sb("SQ8", [128, 2, 8, 2, 128], BF16)
        Xbf = sb("Xbf", [128, 2, 8, 64], BF16)
        Pblk = sb("Pblk", [128, 4, 128], F32)
        Qs = sb("Qs", [128, 4, 64], F32)
        G0T = sb("G0T", [128, 4, 128], F32)
        Hst = sb("Hst", [128, 4, 64], F32)
        Ysb = sb("Ysb", [128, 512], F32)
        yn = sb("yn", [128, 512], F32)
        sqY = yn
        gst = sb("gst", [128, 6, 8], F32)
        ynT = sb("ynT", [128, 4, NT], F32)
        yrT = sb("yrT", [128, 4, NT], BF16)

        P.dma("sp", "c4", lambda e: e.dma_start(out=rwp[:, 0:8, :], in_=rwp_in[:, :, :]), writes=[B("rwp")])
        P.dma("sp", "c5", lambda e: e.dma_start(out=mus[:], in_=mu_in[:, :]), writes=[B("mus")])
        for q3 in range(3):
            load_cast(loraw[:, q3, :], lora_in_d[:, q3, :], B("loraw"))
        P.dma("sp", "stgb", lambda e: e.dma_start(out=ones_f[:], in_=bones_in[:, :]), writes=[B("ones_f")])
        P.op("dve", lambda e: e.tensor_copy(out=bones[:], in_=ones_f[:]), reads=[B("ones_f")], writes=[B("bones")])
        P.op("pool", lambda e: e.memset(ones_f[:], 1.0), reads=[B("bones")], writes=[B("ones_f")])
        P.op("dve", lambda e: e.tensor_copy(out=identb[:], in_=ident[:]), reads=[B("ident")], writes=[B("identb")])
        P.op("dve", lambda e: e.tensor_scalar(out=rwp[:, 7, :], in0=rwp[:, 3, :], scalar1=-1.0, scalar2=1.0,
                                              op0=ALU.mult, op1=ALU.add), reads=[B("rwp")], writes=[B("rwp")])
        P.op("dve", lambda e: e.tensor_scalar(out=rwp[:, 8, :], in0=rwp[:, 2, :], scalar1=-1.0, scalar2=None, op0=ALU.mult),
             reads=[B("rwp")], writes=[B("rwp")])
        P.op("pool", lambda e: e.tensor_copy(out=mX[:, 0:128], in_=maskb[:, 5, 0:128]), reads=[B("maskb")], writes=[B("mX")])
        P.op("pool", lambda e: e.tensor_copy(out=mX[:, 128:256], in_=maskb[:, 1, 0:128]), reads=[B("maskb")], writes=[B("mX")])
        P.op("pool", lambda e: e.tensor_copy(out=mX[:, 256:384], in_=maskb[:, 5, 0:128]), reads=[B("maskb")], writes=[B("mX")])
        P.op("pool", lambda e: e.tensor_copy(out=mY[:, 0:128], in_=maskb[:, 0, 0:128]), reads=[B("maskb")], writes=[B("mY")])
        P.op("pool", lambda e: e.tensor_copy(out=mY[:, 128:256], in_=maskb[:, 0, 0:128]), reads=[B("maskb")], writes=[B("mY")])
        P.op("pool", lambda e: e.memset(Hst[:], 0.0), writes=[B("H%d" % j) for j in range(4)])
        P.op("pool", lambda e: e.memset(Pblk[:], 0.0), writes=[B("Pblk%d" % j) for j in range(4)])
        P.op("pool", lambda e: e.memset(gg[:], 1.0), writes=[B("gg0"), B("gg1")])
        YB = [(psum[6], B("ps6")), (psum[7], B("ps7"))]
        tb = [B("T1_%d" % q) for q in range(19)]
        P.op("pool", lambda e: e.memset(T1[:, 18, 0:1], 0.0), writes=[B("stg0"), B("stg1")] + tb)
        opbuf = [[B("opb%d_%d" % (k_, j)) for j in range(4)] for k_ in range(5)]

        def rwkv(st, full=True):
            zin = lambda c: zr[:, c, 1:NT + 1]
            for c in range(14 if full else 13):
                par = c % 2
                P.op("pool", (lambda e, c=c, par=par: e.tensor_tensor(out=dz[:, par, :], in0=zr[:, c, 0:NT],
                                                                      in1=zr[:, c, 1:NT + 1], op=ALU.subtract)),
                     reads=[zb[c], B("zrp%d" % c)], writes=[B("sgt%d" % par)])
                P.op("pool", (lambda e, c=c: e.tensor_copy(out=zr[:, c, 0:1], in_=zr[:, c, NT:NT + 1])),
                     reads=[zb[c], B("sgt%d" % par)], writes=[B("zrp%d" % c)])
                P.op("dve", (lambda e, c=c, par=par: e.scalar_tensor_tensor(
                    out=zr[:, c, 1:NT + 1], in0=dz[:, par, :], scalar=mus[:, c:c + 1], op0=ALU.mult,
                    in1=zr[:, c, 1:NT + 1], op1=ALU.add)),
                    reads=[B("sgt%d" % par), B("mus"), zb[c]], writes=[zb[c]])
                if c % 2 == 1:
                    yield
            P.op("act", lambda e: e.activation(out=lin[0:64, :], in_=zr[0:64, 12, 1:NT + 1], func=AF.Tanh),
                 reads=[zb[12]], writes=[B("lin0")])
            P.op("act", lambda e: e.activation(out=lin[64:128, :], in_=zr[64:128, 12, 1:NT + 1], func=AF.Copy),
                 reads=[zb[12]], writes=[B("lin1")])
            if full:
                P.op("act", lambda e: e.activation(out=sgx[:], in_=zr[:, 13, 1:NT + 1], func=AF.Sigmoid),
                     reads=[zb[13]], writes=[B("sgx")])
            def prep_j(j, T1, tb):
                pw, pwb = ps_next()
                P.op("pe", (lambda e, pw=pw, j=j: e.matmul(pw[:, 0:NT], lhsT=loraw[0:64, 0, j * 128:(j + 1) * 128],
                                                           rhs=lin[0:64, :], start=True, stop=True)),
                     reads=[B("loraw"), B("lin0")], writes=[pwb])
                pa, pab = ps_next()
                P.op("pe", (lambda e, pa=pa, j=j: e.matmul(pa[:, 0:NT], lhsT=loraw[64:128, 1, j * 128:(j + 1) * 128],
                                                           rhs=lin[64:128, :], start=True, stop=True)),
                     reads=[B("loraw"), B("lin1")], writes=[pab])
                if full:
                    pg, pgb = ps_next()
                    P.op("pe", (lambda e, pg=pg, j=j: e.matmul(pg[:, 0:NT], lhsT=loraw[:, 2, j * 128:(j + 1) * 128],
                                                               rhs=sgx[:], start=True, stop=True)),
                         reads=[B("loraw"), B("sgx")], writes=[pgb])
                P.op("act", (lambda e, pw=pw, j=j: e.activation(out=T1[:, 0, :], in_=pw[:, 0:NT], func=AF.Sigmoid,
                                                                bias=rwp[:, 0, j:j + 1])),
                     reads=[pwb, B("rwp")], writes=[tb[0]])
                P.op("act", (lambda e, pa=pa, j=j: e.activation(out=T1[:, 1, :], in_=pa[:, 0:NT], func=AF.Sigmoid,
                                                                bias=rwp[:, 1, j:j + 1])),
                     reads=[pab, B("rwp")], writes=[tb[1]])
                if full:
                    P.op("act", (lambda e, pg=pg, j=j: e.activation(out=gT[:, j, :], in_=pg[:, 0:NT], func=AF.Copy)),
                         reads=[pgb], writes=[B("gT%d" % j)])
                yield
                P.op("act", (lambda e, j=j: e.activation(out=ksq[:], in_=zin(4 + j), func=AF.Square,
                                                         scale=rwp[:, 2, j:j + 1])),
                     reads=[zb[4 + j], B("rwp")], writes=[B("ksq")])
                pss, pssb = ps_next()
                P.op("pe", (lambda e, pss=pss: e.matmul(pss[:, 0:NT], lhsT=bones[:], rhs=ksq[:], start=True, stop=True)),
                     reads=[B("bones"), B("ksq")], writes=[pssb])
                P.op("dve", (lambda e, pss=pss: e.tensor_scalar(out=T1[:, 2, :], in0=pss[:, 0:NT], scalar1=1e-24,
                                                                scalar2=None, op0=ALU.max)),
                     reads=[pssb], writes=[tb[2]])
                P.op("act", lambda e: e.activation(out=T1[:, 2, :], in_=T1[:, 2, :], func=AF.Ln), reads=[tb[2]], writes=[tb[2]])
                P.op("act", lambda e: e.activation(out=T1[:, 2, :], in_=T1[:, 2, :], func=AF.Exp, scale=-0.5),
                     reads=[tb[2]], writes=[tb[2]])
                P.op("dve", (lambda e, j=j: e.scalar_tensor_tensor(out=T1[:, 3, :], in0=zin(4 + j),
                                                                   scalar=rwp[:, 8, j:j + 1], op0=ALU.mult,
                                                                   in1=T1[:, 2, :], op1=ALU.mult)),
                     reads=[zb[4 + j], B("rwp"), tb[2]], writes=[tb[3]])
                P.op("dve", (lambda e, j=j: e.tensor_scalar(out=T1[:, 4, :], in0=T1[:, 1, :], scalar1=rwp[:, 3, j:j + 1],
                                                            scalar2=rwp[:, 7, j:j + 1], op0=ALU.mult, op1=ALU.add)),
                     reads=[tb[1], B("rwp")], writes=[tb[4]])
                P.op("pool", (lambda e, j=j: e.tensor_tensor(out=T1[:, 5, :], in0=zin(4 + j), in1=T1[:, 4, :], op=ALU.mult)),
                     reads=[zb[4 + j], tb[4]], writes=[tb[5]])
                P.op("dve", lambda e: e.scalar_tensor_tensor(out=T1[:, 6, :], in0=T1[:, 3, :], scalar=-1.0, op0=ALU.mult,
                                                             in1=T1[:, 1, :], op1=ALU.mult),
                     reads=[tb[3], tb[1]], writes=[tb[6]])
                if full:
                    P.op("dve", (lambda e, j=j: e.scalar_tensor_tensor(out=prod[:, j, :], in0=zin(j), scalar=rwp[:, 6, j:j + 1],
                                                                       op0=ALU.mult, in1=T1[:, 5, :], op1=ALU.mult)),
                         reads=[zb[j], B("rwp"), tb[5]], writes=[B("prod%d" % j)])
                yield
                gsel = j % 2
                ggb = B("gg%d" % gsel)
                scb = B("sc%d" % j)
                P.op("act", lambda e: e.activation(out=T1[:, 7, :], in_=T1[:, 0, :], func=AF.Exp, scale=-CDEC),
                     reads=[tb[0]], writes=[tb[7]])
                P.op("act", lambda e: e.activation(out=T1[:, 8, :], in_=T1[:, 0, :], func=AF.Exp, scale=CDEC),
                     reads=[tb[0]], writes=[tb[8]])
                for i in range(SUBS):
                    for q_, slot in ((0, 7), (1, 8)):
                        P.op("dve", (lambda e, i=i, q_=q_, slot=slot: e.tensor_tensor_scan(
                            out=gg[:, gsel, q_, i, 1:129], data0=T1[:, slot, i * 128:(i + 1) * 128], data1=ones_f[:, 0:128],
                            initial=1.0, op0=ALU.mult, op1=ALU.mult)),
                            reads=[tb[slot], B("ones_f")], writes=[ggb])
                P.op("dve", (lambda e: e.tensor_tensor(out=sc[:, 2, j, :], in0=gg[:, gsel, 0, :, 128], in1=gg[:, gsel, 1, :, 64],
                                                       op=ALU.mult)), reads=[ggb], writes=[scb])
                P.op("dve", (lambda e: e.tensor_copy(out=sc[:, 3, j, :], in_=gg[:, gsel, 0, :, 64])), reads=[ggb], writes=[scb])
                yield
                for i in range(SUBS):
                    isl = slice(i * 128, (i + 1) * 128)
                    gmid = gg[:, gsel, 0, i, 64:65]
                    imid = gg[:, gsel, 1, i, 64:65]
                    gprev = gg[:, gsel, 0, i, 0:128]
                    gcur = gg[:, gsel, 0, i, 1:129]
                    icur = gg[:, gsel, 1, i, 1:129]
                    P.op("dve", (lambda e, isl=isl, imid=imid, gprev=gprev: e.scalar_tensor_tensor(
                        out=opb[:, 0, j, isl], in0=T1[:, 3, isl], scalar=imid, op0=ALU.mult, in1=gprev, op1=ALU.mult)),
                        reads=[tb[3], ggb], writes=[opbuf[0][j]])
                    P.op("pool", (lambda e, isl=isl, gprev=gprev: e.tensor_tensor(out=opb[:, 1, j, isl], in0=T1[:, 3, isl],
                                                                                  in1=gprev, op=ALU.mult)),
                         reads=[tb[3], ggb], writes=[opbuf[1][j]])
                    P.op("dve", (lambda e, isl=isl, gmid=gmid, icur=icur: e.scalar_tensor_tensor(
                        out=opb[:, 2, j, isl], in0=T1[:, 6, isl], scalar=gmid, op0=ALU.mult, in1=icur, op1=ALU.mult)),
                        reads=[tb[6], ggb], writes=[opbuf[2][j]])
                    P.op("dve", (lambda e, isl=isl, gmid=gmid, icur=icur: e.scalar_tensor_tensor(
                        out=opb[:, 3, j, isl], in0=T1[:, 5, isl], scalar=gmid, op0=ALU.mult, in1=icur, op1=ALU.mult)),
                        reads=[tb[5], ggb], writes=[opbuf[3][j]])
                    yield
                    if not full:
                        continue
                    P.op("dve", (lambda e, isl=isl, i=i, imid=imid, gcur=gcur: e.scalar_tensor_tensor(
                        out=opb[:, 4, j, isl], in0=zr[:, j, 1 + i * 128:1 + (i + 1) * 128], scalar=imid, op0=ALU.mult,
                        in1=gcur, op1=ALU.mult)), reads=[zb[j], ggb], writes=[opbuf[4][j]])
                    P.op("pool", (lambda e, isl=isl, i=i, gcur=gcur: e.tensor_tensor(
                        out=rt0[:, j, isl], in0=zr[:, j, 1 + i * 128:1 + (i + 1) * 128], in1=gcur, op=ALU.mult)),
                        reads=[zb[j], ggb], writes=[B("rt0_%d" % j)])
            for j in range(4):
                jo = 9 * (j % 2)
                yield from prep_j(j, T1s[:, jo:jo + 9, :], tb[jo:jo + 9])
            for i in range(SUBS):
                isl = slice(i * 128, (i + 1) * 128)
                pA, pAb = ps_next()
                pAv = pA[:, :].bitcast(BF16)
                for j in range(4):
                    P.op("pe", (lambda e, pAv=pAv, j=j, isl=isl: e.transpose(pAv[:, j * 128:(j + 1) * 128],
                                                                             opb[:, 1, j, isl], identb[:])),
                         reads=[opbuf[1][j], B("identb")], writes=[pAb])
                    P.op("pe", (lambda e, pAv=pAv, j=j, isl=isl: e.transpose(pAv[:, 512 + j * 128:512 + (j + 1) * 128],
                                                                             opb[:, 2, j, isl], identb[:])),
                         reads=[opbuf[2][j], B("identb")], writes=[pAb])
                P.op("act", (lambda e, pAv=pAv, i=i: e.activation(out=tokm[:, i, 0:2, :],
                                                                  in_=pAv.rearrange("p (a b) -> p a b", a=2), func=AF.Copy)),
                     reads=[pAb], writes=[B("tokm%d_0" % i), B("tokm%d_1" % i)])
                yield
                pB_, pBb = ps_next()
                pBv = pB_[:, :].bitcast(BF16)
                for j in range(4):
                    P.op("pe", (lambda e, pBv=pBv, j=j, isl=isl: e.transpose(pBv[:, j * 128:(j + 1) * 128],
                                                                             opb[:, 3, j, isl], identb[:])),
                         reads=[opbuf[3][j], B("identb")], writes=[pBb])
                P.op("act", (lambda e, pBv=pBv, i=i: e.activation(out=tokm[:, i, 2, :], in_=pBv[:, 0:512], func=AF.Copy)),
                     reads=[pBb], writes=[B("tokm%d_2" % i)])
                pV, pVb = ps_next()
                for j in range(4):
                    P.op("pe", (lambda e, pV=pV, j=j, i=i: e.transpose(pV[:, j * 128:(j + 1) * 128],
                                                                       zr[:, 8 + j, 1 + i * 128:1 + (i + 1) * 128], ident[:])),
                         reads=[zb[8 + j], B("ident")], writes=[pVb])
                P.op("act", (lambda e, pV=pV, i=i: e.activation(out=tokm[:, i, 3, :], in_=pV[:, :], func=AF.Copy)),
                     reads=[pVb], writes=[B("tokm%d_3" % i)])
                yield
            yield "PREP_DONE"
            for i in range(SUBS):
                isl = slice(i * 128, (i + 1) * 128)
                ystart = [True, True]
                axb = [B("AX8_%d" % h) for h in range(8)]
                ayb = [B("AY8_%d" % h) for h in range(8)]
                x32g = [B("X32g%d" % g) for g in range(2)]
                xbfg = [B("Xbfg%d" % g) for g in range(2)]
                for h in range(8):
                    j, e_ = h // 2, h % 2
                    pbs = e_ * 64
                    at_ = opb[pbs:pbs + 64, 0, j, isl]
                    bt_ = opb[pbs:pbs + 64, 2, j, isl]
                    kt_ = opb[pbs:pbs + 64, 3, j, isl]
                    rt_ = opb[pbs:pbs + 64, 4, j, isl]
                    rds = [opbuf[k_][j] for k_ in (0, 2, 3, 4)]
                    bx, bxb = ps_next()
                    if full:
                        by, byb = ps_next()
                        rds = [opbuf[k_][j] for k_ in (0, 2, 3, 4)]
                    else:
                        rds = [opbuf[k_][j] for k_ in (0, 2, 3)]
                    for (dst, l_, r_) in ((bx[:, 0:128], bt_, at_), (bx[:, 128:256], at_, bt_), (bx[:, 256:384], kt_, at_)):
                        P.op("pe", (lambda e, dst=dst, l_=l_, r_=r_: e.matmul(dst, lhsT=l_, rhs=r_, start=True, stop=True)),
                             reads=rds, writes=[bxb])
                    P.op("dve", (lambda e, bx=bx, h=h: e.tensor_tensor(out=AX8[:, h, :], in0=bx[:, 0:384], in1=mX[:],
                                                                       op=ALU.mult)), reads=[bxb, B("mX")], writes=[axb[h]])
                    if full:
                        for (dst, l_, r_) in ((by[:, 0:128], bt_, rt_), (by[:, 128:256], kt_, rt_)):
                            P.op("pe", (lambda e, dst=dst, l_=l_, r_=r_: e.matmul(dst, lhsT=l_, rhs=r_, start=True, stop=True)),
                                 reads=rds, writes=[byb])
                        P.op("dve", (lambda e, by=by, h=h: e.tensor_tensor(out=AY8[:, h, :], in0=by[:, 0:256], in1=mY[:],
                                                                           op=ALU.mult)), reads=[byb, B("mY")], writes=[ayb[h]])
                    yield
                kb.ps_ring = [0, 1, 2, 3]
                XB = [(psum[4], B("ps4")), (psum[5], B("ps5"))]
                casteng = ["act", "dve"]

                def xcast(g):
                    xb_, xbb = XB[g]
                    src = xb_[:, :].rearrange("p (h a d) -> p a h d", h=4, a=2)
                    dst = Xbf[:, :, g * 4:(g + 1) * 4, :]
                    if casteng[g] == "act":
                        P.op("act", (lambda e: e.activation(out=dst, in_=src, func=AF.Copy)), reads=[xbb], writes=[xbfg[g]])
                    else:
                        P.op("dve", (lambda e: e.tensor_copy(out=dst, in_=src)), reads=[xbb], writes=[xbfg[g]])

                for g in range(2):
                    xb_, xbb = XB[g]
                    for hh in range(4):
                        h = g * 4 + hh
                        P.op("pe", (lambda e, xb_=xb_, hh=hh, h=h, i=i, st_=(hh == 0): e.matmul(
                            xb_[:, hh * 128:hh * 128 + 64], lhsT=identb[:], rhs=tokm[:, i, 0, h * 64:(h + 1) * 64],
                            start=st_, stop=False, skip_group_check=True)),
                            reads=[B("identb"), B("tokm%d_0" % i)], writes=[xbb])
                        P.op("pe", (lambda e, xb_=xb_, hh=hh, h=h, i=i: e.matmul(
                            xb_[:, hh * 128 + 64:hh * 128 + 128], lhsT=AX8[:, h, 256:384],
                            rhs=tokm[:, i, 3, h * 64:(h + 1) * 64], start=False, stop=False, skip_group_check=True)),
                            reads=[axb[h], B("tokm%d_3" % i)], writes=[xbb])
                    xcast(g)
                    yield
                curA = [AX8[:, h, 128:256] for h in range(8)]
                curAT = [AX8[:, h, 0:128] for h in range(8)]
                curb = [[axb[h]] for h in range(8)]
                pp = 0
                for lvl in range(7):
                    for g in range(2):
                        xb_, xbb = XB[g]
                        for hh in range(4):
                            h = g * 4 + hh
                            P.op("pe", (lambda e, xb_=xb_, hh=hh, h=h, lt=curAT[h]: e.matmul(
                                xb_[:, hh * 128:(hh + 1) * 128].rearrange("p (a d) -> p a d", a=2), lhsT=lt,
                                rhs=Xbf[:, :, h, :], start=False, stop=(lvl == 6), skip_group_check=True)),
                                reads=curb[h] + [xbfg[g]], writes=[xbb])
                        xcast(g)
                        yield
                    if lvl < 6:
                        for hp in range(4):
                            pq, pqb = ps_next()
                            sqb = B("SQ8_%d_%d" % (pp, hp))
                            for q_ in range(2):
                                h = hp * 2 + q_
                                P.op("pe", (lambda e, pq=pq, q_=q_, la=curAT[h], ra=curA[h]: e.matmul(
                                    pq[:, q_ * 256:q_ * 256 + 128], lhsT=la, rhs=ra, start=True, stop=True)),
                                    reads=curb[h], writes=[pqb])
                                P.op("pe", (lambda e, pq=pq, q_=q_, la=curA[h], ra=curAT[h]: e.matmul(
                                    pq[:, q_ * 256 + 128:q_ * 256 + 256], lhsT=la, rhs=ra, start=True, stop=True)),
                                    reads=curb[h], writes=[pqb])
                            eng = "act" if hp % 2 == 0 else "dve"
                            if eng == "act":
                                P.op("act", (lambda e, pq=pq, hp=hp, pp=pp: e.activation(
                                    out=SQ8[:, pp, hp * 2:hp * 2 + 2, :, :],
                                    in_=pq[:, :].rearrange("p (q a c) -> p q a c", q=2, a=2), func=AF.Copy)),
                                    reads=[pqb], writes=[sqb])
                            else:
                                P.op("dve", (lambda e, pq=pq, hp=hp, pp=pp: e.tensor_copy(
                                    out=SQ8[:, pp, hp * 2:hp * 2 + 2, :, :],
                                    in_=pq[:, :].rearrange("p (q a c) -> p q a c", q=2, a=2))),
                                    reads=[pqb], writes=[sqb])
                            for q_ in range(2):
                                h = hp * 2 + q_
                                curA[h] = SQ8[:, pp, h, 0, :]
                                curAT[h] = SQ8[:, pp, h, 1, :]
                                curb[h] = [sqb]
                            if hp % 2 == 1:
                                yield
                        pp ^= 1
                kb.ps_ring = [0, 1, 2, 3, 4, 5]
                pP, pPb = ps_next()
                pQ, pQb = ps_next()
                pG = [ps_next(), ps_next()] if full else [None, None]
                for j in range(4):
                    g = j // 2
                    jsl = slice(j * 128, (j + 1) * 128)
                    W0p = Xbf[:, 0, 2 * j:2 * j + 2, :]
                    P.op("pe", (lambda e, W0p=W0p, i=i, jsl=jsl, pP=pP: e.matmul(pP[:, jsl], lhsT=W0p, rhs=tokm[:, i, 1, jsl],
                                                                        start=True, stop=True)),
                         reads=[xbfg[g], B("tokm%d_1" % i)], writes=[pPb])
                for j in range(4):
                    g = j // 2
                    jsl = slice(j * 128, (j + 1) * 128)
                    Uvp = Xbf[:, 1, 2 * j:2 * j + 2, :]
                    P.op("pe", (lambda e, Uvp=Uvp, i=i, jsl=jsl, pQ=pQ: e.matmul(pQ[:, jsl], lhsT=tokm[:, i, 1, jsl], rhs=Uvp,
                                                                        start=True, stop=False)),
                         reads=[xbfg[g], B("tokm%d_1" % i)], writes=[pQb])
                    P.op("pe", (lambda e, i=i, jsl=jsl, pQ=pQ: e.matmul(pQ[:, jsl], lhsT=tokm[:, i, 2, jsl], rhs=tokm[:, i, 3, jsl],
                                                               start=False, stop=True)),
                         reads=[B("tokm%d_2" % i), B("tokm%d_3" % i)], writes=[pQb])
                for j in range(4 if full else 0):
                    g = j // 2
                    W0p = Xbf[:, 0, 2 * j:2 * j + 2, :]
                    pg_, pgb_ = pG[j // 2]
                    for e_ in range(2):
                        h = 2 * j + e_
                        c0_ = (j % 2) * 256 + e_ * 128
                        P.op("pe", (lambda e, pg_=pg_, W0p=W0p, h=h, c0_=c0_: e.matmul(
                            pg_[:, c0_:c0_ + 128], lhsT=W0p, rhs=AY8[:, h, 0:128], start=True, stop=True)),
                            reads=[xbfg[g], ayb[h]], writes=[pgb_])
                for j in range(4):
                    jsl = slice(j * 128, (j + 1) * 128)
                    pblb = B("Pblk%d" % j)
                    qsb = B("Qs%d" % j)
                    g0b = B("G0T%d" % j)
                    pg_, pgb_ = pG[j // 2] if full else (None, None)
                    for e_ in range(2):
                        pbs = e_ * 64
                        c0_ = (j % 2) * 256 + e_ * 128
                        P.op("dve", (lambda e, pbs=pbs, j=j, i=i, pP=pP: e.scalar_tensor_tensor(
                            out=Pblk[pbs:pbs + 64, j, pbs:pbs + 64], in0=ident[pbs:pbs + 64, pbs:pbs + 64],
                            scalar=sc[pbs:pbs + 64, 3, j, i:i + 1], op0=ALU.mult,
                            in1=pP[pbs:pbs + 64, j * 128 + pbs:j * 128 + pbs + 64], op1=ALU.add)),
                            reads=[pPb, B("ident"), B("sc%d" % j)], writes=[pblb])
                        P.op("act", (lambda e, pbs=pbs, j=j, i=i, pQ=pQ: e.activation(
                            out=Qs[pbs:pbs + 64, j, :], in_=pQ[pbs:pbs + 64, j * 128 + pbs:j * 128 + pbs + 64], func=AF.Copy,
                            scale=sc[pbs:pbs + 64, 2, j, i:i + 1])), reads=[pQb, B("sc%d" % j)], writes=[qsb])
                        if full:
                            P.op("dve", (lambda e, pg_=pg_, pbs=pbs, j=j, c0_=c0_, isl=isl: e.tensor_tensor(
                                out=G0T[pbs:pbs + 64, j, :], in0=pg_[pbs:pbs + 64, c0_:c0_ + 128],
                                in1=rt0[pbs:pbs + 64, j, isl], op=ALU.add)), reads=[pgb_, B("rt0_%d" % j)], writes=[g0b])
                for j in range(4):
                    g = j // 2
                    pblb = B("Pblk%d" % j)
                    qsb = B("Qs%d" % j)
                    g0b = B("G0T%d" % j)
                    hbuf = B("H%d" % j)
                    for e_ in range(2 if full else 0):
                        pbs = e_ * 64
                        h = 2 * j + e_
                        yb_, ybb = YB[e_]
                        col = slice(j * 64, (j + 1) * 64)
                        P.op("pe", (lambda e, yb_=yb_, col=col, h=h, st_=ystart[e_]: e.matmul(
                            yb_[:, col], lhsT=AY8[:, h, 0:128], rhs=Xbf[:, 1, h, :], start=st_, stop=False,
                            skip_group_check=True)), reads=[ayb[h], xbfg[g]], writes=[ybb])
                        ystart[e_] = False
                        P.op("pe", (lambda e, yb_=yb_, col=col, h=h, i=i: e.matmul(
                            yb_[:, col], lhsT=AY8[:, h, 128:256], rhs=tokm[:, i, 3, h * 64:(h + 1) * 64],
                            start=False, stop=False, skip_group_check=True)),
                            reads=[ayb[h], B("tokm%d_3" % i)], writes=[ybb])
                        P.op("pe", (lambda e, yb_=yb_, col=col, pbs=pbs, j=j: e.matmul(
                            yb_[:, col], lhsT=G0T[pbs:pbs + 64, j, :], rhs=Hst[pbs:pbs + 64, j, :], start=False, stop=True,
                            skip_group_check=True)), reads=[g0b, hbuf], writes=[ybb])
                    pH, pHb = ps_next()
                    P.op("pe", (lambda e, pH=pH, j=j: e.matmul(pH[:, 0:64], lhsT=Pblk[:, j, :], rhs=Hst[:, j, :],
                                                               start=True, stop=True)),
                         reads=[pblb, hbuf], writes=[pHb])
                    P.op("dve", (lambda e, pH=pH, j=j, i=i: e.scalar_tensor_tensor(
                        out=Hst[:, j, :], in0=pH[:, 0:64], scalar=sc[:, 2, j, i:i + 1], op0=ALU.mult, in1=Qs[:, j, :],
                        op1=ALU.add)), reads=[pHb, B("sc%d" % j), qsb, hbuf], writes=[hbuf])
                    yield
                if not full:
                    continue
                ysb4 = Ysb[:, :].rearrange("p (j e d) -> p j e d", j=4, e=2)
                for e_ in range(2):
                    yb_, ybb = YB[e_]
                    P.op("act", (lambda e, yb_=yb_, e_=e_: e.activation(
                        out=ysb4[:, :, e_, :], in_=yb_[:, 0:256].rearrange("p (a b) -> p a b", a=4), func=AF.Copy)),
                        reads=[ybb], writes=[B("Ysb")])
                y3 = Ysb[:, :].rearrange("p (a b) -> p a b", b=64)
                P.op("dve", lambda e: e.tensor_reduce(out=gst[:, 0, :], in_=y3, op=ALU.add, axis=AX.X),
                     reads=[B("Ysb")], writes=[B("gst")])
                P.op("act", lambda e: e.activation(out=sqY[:], in_=Ysb[:], func=AF.Square), reads=[B("Ysb")], writes=[B("yn")])
                P.op("dve", lambda e: e.tensor_reduce(out=gst[:, 1, :], in_=sqY[:, :].rearrange("p (a b) -> p a b", b=64),
                                                      op=ALU.add, axis=AX.X), reads=[B("yn")], writes=[B("gst")])
                P.op("dve", lambda e: e.tensor_scalar(out=gst[:, 2, :], in0=gst[:, 0, :], scalar1=1.0 / 64, scalar2=None,
                                                      op0=ALU.mult), reads=[B("gst")], writes=[B("gst")])
                P.op("dve", lambda e: e.tensor_tensor(out=gst[:, 3, :], in0=gst[:, 2, :], in1=gst[:, 2, :], op=ALU.mult),
                     reads=[B("gst")], writes=[B("gst")])
                P.op("dve", lambda e: e.scalar_tensor_tensor(out=gst[:, 4, :], in0=gst[:, 1, :], scalar=1.0 / 64, op0=ALU.mult,
                                                             in1=gst[:, 3, :], op1=ALU.subtract),
                     reads=[B("gst")], writes=[B("gst")])
                P.op("act", lambda e: e.activation(out=gst[:, 4, :], in_=gst[:, 4, :], func=AF.Ln, bias=epsb[:, 1:2]),
                     reads=[B("gst"), B("epsb")], writes=[B("gst")])
                P.op("act", lambda e: e.activation(out=gst[:, 5, :], in_=gst[:, 4, :], func=AF.Exp, scale=-0.5),
                     reads=[B("gst")], writes=[B("gst")])
                yn3 = yn[:, :].rearrange("p (a b) -> p a b", b=64)
                P.op("dve", lambda e: e.tensor_tensor(out=yn3, in0=y3, in1=gst[:, 2, :].unsqueeze(2).broadcast_to([128, 8, 64]),
                                                      op=ALU.subtract), reads=[B("Ysb"), B("gst")], writes=[B("yn")])
                P.op("dve", lambda e: e.tensor_tensor(out=yn3, in0=yn3, in1=gst[:, 5, :].unsqueeze(2).broadcast_to([128, 8, 64]),
                                                      op=ALU.mult), reads=[B("yn"), B("gst")], writes=[B("yn")])
                pT_, pTb = ps_next()
                for j in range(4):
                    P.op("pe", (lambda e, pT_=pT_, j=j: e.transpose(pT_[:, j * 128:(j + 1) * 128], yn[:, j * 128:(j + 1) * 128],
                                                                   ident[:])), reads=[B("yn"), B("ident")], writes=[pTb])
                for j in range(4):
                    P.op("dve", (lambda e, pT_=pT_, j=j, isl=isl: e.tensor_scalar(
                        out=ynT[:, j, isl], in0=pT_[:, j * 128:(j + 1) * 128], scalar1=rwp[:, 4, j:j + 1],
                        scalar2=rwp[:, 5, j:j + 1], op0=ALU.mult, op1=ALU.add)),
                        reads=[pTb, B("rwp")], writes=[B("ynT%d" % j)])
                yield
            for j in range(4 if full else 0):
                pbn, pbnb = ps_next()
                P.op("pe", (lambda e, pbn=pbn, j=j: e.matmul(pbn[:, 0:NT], lhsT=bones[:], rhs=prod[:, j, :], start=True,
                                                             stop=True)), reads=[B("bones"), B("prod%d" % j)], writes=[pbnb])
                P.op("dve", (lambda e, pbn=pbn, j=j: e.tensor_tensor(out=T1[:, 18, :], in0=pbn[:, 0:NT], in1=zin(8 + j),
                                                                     op=ALU.mult)), reads=[pbnb, zb[8 + j]], writes=[tb[18]])
                P.op("pool", (lambda e, j=j: e.tensor_tensor(out=T1[:, 18, :], in0=T1[:, 18, :], in1=ynT[:, j, :], op=ALU.add)),
                     reads=[tb[18], B("ynT%d" % j)], writes=[tb[18]])
                P.op("pool", (lambda e, j=j: e.tensor_tensor(out=yrT[:, j, :], in0=T1[:, 18, :], in1=gT[:, j, :], op=ALU.mult)),
                     reads=[tb[18], B("gT%d" % j)], writes=[B("yrT%d" % j)])
                yield

        mrg = sb("mrg", [128, NKC, NT], BF16)
        mtmp = rtmp
        mb = [B("mrg%d" % c) for c in range(NKC)]

        def branch_chunk(wname, o, rhs_tile, rhs_bufs):
            w, wb = wload(wname, o, 4 * 128)
            pt, pbuf = ps_next()
            for kc in range(4):
                P.op("pe", (lambda e, pt=pt, w=w, kc=kc: e.matmul(
                    pt[:, 0:NT], lhsT=w[:, kc * 128:(kc + 1) * 128], rhs=rhs_tile[:, kc, :],
                    start=(kc == 0), stop=(kc == 3))), reads=[wb] + rhs_bufs, writes=[pbuf])
            return pt, pbuf

        def merge_out(st):
            hsel = st % 2
            yrb = [B("yrT%d" % c) for c in range(4)]
            for o in range(NKC):
                par = o % 2
                pa_, pab_ = branch_chunk("watt", o, yattT, [B("yattT")])
                P.op("dve", (lambda e, pa_=pa_, o=o, par=par: e.tensor_tensor(out=mtmp[:, par, 0, :], in0=pa_[:, 0:NT],
                                                                              in1=gates[:, o, :], op=ALU.mult)),
                     reads=[pab_, gb[o]], writes=[B("rtmp%d0" % par)])
                pr_, prb_ = branch_chunk("wrw", o, yrT, yrb)
                P.op("dve", (lambda e, pr_=pr_, o=o, par=par: e.tensor_tensor(out=mtmp[:, par, 1, :], in0=pr_[:, 0:NT],
                                                                              in1=gates[:, 8 + o, :], op=ALU.mult)),
                     reads=[prb_, gb[8 + o]], writes=[B("rtmp%d1" % par)])
                P.op("pool", (lambda e, o=o, par=par: e.tensor_tensor(out=mrg[:, o, :], in0=mtmp[:, par, 0, :],
                                                                      in1=mtmp[:, par, 1, :], op=ALU.add)),
                     reads=[B("rtmp%d0" % par), B("rtmp%d1" % par)], writes=[mb[o]])
            for c in range(NKC):
                w, wb = wload("wmix", c, NKC * 128)
                pt, pbuf = ps_next()
                for kc in range(NKC):
                    P.op("pe", (lambda e, pt=pt, w=w, kc=kc: e.matmul(
                        pt[:, 0:NT], lhsT=w[:, kc * 128:(kc + 1) * 128], rhs=mrg[:, kc, :],
                        start=(kc == 0), stop=(kc == NKC - 1))), reads=[wb, mb[kc]], writes=[pbuf])
                P.op("act", (lambda e, pt=pt, c=c: e.activation(out=fT[:, c, :], in_=pt[:, 0:NT], func=AF.Copy)),
                     reads=[pbuf], writes=[fb[c]])
            postnorm_add(3, False, hsel)

        outb = B("outT")

        def run_all(g):
            for _ in g:
                pass

        def interleave(ga, gb, ra, rb):
            if os.environ.get("KINT", "1") == "0":
                run_all(ga)
                run_all(gb)
                return
            la = lb = True
            while la or lb:
                for _ in range(ra):
                    if la:
                        try:
                            next(ga)
                        except StopIteration:
                            la = False
                for _ in range(rb):
                    if lb:
                        try:
                            next(gb)
                        except StopIteration:
                            lb = False

        def ffn1_front(st):
            hsel = st % 2
            ht, hbs = HT[hsel]
            t0 = st * NT
            P.dma("sp", "xin", (lambda e, t0=t0, ht=ht: e.dma_start(out=ht[:], in_=xT[:, :, t0:t0 + NT])),
                  writes=hbs)
            prenorm(0, hsel)
            yield
            yield from ffn("gu1", "d1")
            postnorm_add(1, True, hsel)
            yield

        def until(g, sentinel):
            for v in g:
                if v == sentinel:
                    return True
            return False

        def interleave2(ga, gb, ra, rb, stop_b=None):
            la = lb = True
            while la or lb:
                for _ in range(ra):
                    if la:
                        try:
                            next(ga)
                        except StopIteration:
                            la = False
                for _ in range(rb):
                    if lb:
                        try:
                            if next(gb) == stop_b and stop_b is not None:
                                lb = False
                        except StopIteration:
                            lb = False

        def chain(*gs):
            for g in gs:
                yield from g

        run_all(ffn1_front(0))
        inp_g = mixer_inproj(0, npre == 0)
        until(inp_g, "RWKV_IN_DONE") if npre == 0 else run_all(inp_g)
        for st in range(nsup):
            full = st >= npre
            nxt = st + 1
            nfull = nxt >= npre
            if not full:
                nin = mixer_inproj(nxt, nfull)

                def nxt_front(nin=nin, nxt=nxt):
                    yield from ffn1_front(nxt)
                    for v in nin:
                        if v == "RWKV_IN_DONE":
                            return
                        yield
                interleave2(rwkv(st, False), nxt_front(), 3, 1)
                inp_g = nin
                continue
            rg = rwkv(st, True)
            interleave2(chain(inp_g, attention(st)), rg, 1, 2, stop_b="PREP_DONE")
            if nxt < nsup:
                interleave2(ffn1_front(nxt), rg, 1, 3)
            else:
                run_all(rg)
            merge_out(st)
            prenorm(4, st % 2)
            run_all(ffn("gu2", "d2"))
            postnorm_add(5, True, st % 2)
            ht, hbs = HT[st % 2]
            o0 = (st - npre) * NT
            P.dma("sp", "xout", (lambda e, o0=o0, ht=ht: e.dma_start(out=outT[:, :, o0:o0 + NT], in_=ht[:])),
                  reads=hbs, writes=[outb])
            if nxt < nsup:
                inp_g = mixer_inproj(nxt, True)
                until(inp_g, "RWKV_IN_DONE")
        P.final_wait("sp", [outb])
        P.emit(block)
    return nc


def _tile_w(w, kdim, ncols_chunks):
    K_, N_ = w.shape
    kc = K_ // 128
    nj = N_ // 128
    t = w.reshape(kc, 128, nj, 128).transpose(2, 1, 0, 3)
    return np.ascontiguousarray(t.reshape(nj, 128, kc * 128))


def _prep_weights(inp):
    out = {}
    for i, (gu, dn) in enumerate((("ffn1_w_gate_up", "ffn1_w_down"), ("ffn2_w_gate_up", "ffn2_w_down"))):
        w = np.asarray(inp[gu][0])
        g = _tile_w(w[:, :DFF], D, NHC)
        u = _tile_w(w[:, DFF:], D, NHC)
        out["gu%d" % (i + 1)] = np.ascontiguousarray(np.concatenate([g, u], axis=2))
        out["d%d" % (i + 1)] = _tile_w(np.asarray(inp[dn][0]), DFF, NKC)
    win = np.asarray(inp["w_in"][0])
    def perm_heads(wc, nh):
        wc = wc.reshape(D, nh, 64)
        idx = np.concatenate([np.arange(8, 16), np.arange(0, 8), np.arange(16, 64)])
        return wc[:, :, idx].reshape(D, nh * 64)
    wq = win[:, 0:512]
    wk = win[:, 512:640]
    wv = win[:, 640:768]
    wk_sw = np.concatenate([wk[:, 64:], wk[:, :64]], axis=1)
    cols = [wq, perm_heads(wq, 8), wk, perm_heads(wk, 2), wk_sw, perm_heads(wk_sw, 2), wv,
            win[:, 768:768 + 1792], win[:, 2560:4608]]
    wext = np.concatenate(cols, axis=1)
    assert wext.shape[1] == NIN * 128
    out["win"] = _tile_w(wext, D, NIN)
    out["watt"] = _tile_w(np.asarray(inp["w_att_branch"][0]), 512, NKC)
    out["wrw"] = _tile_w(np.asarray(inp["w_rwkv_branch"][0]), 512, NKC)
    out["wmix"] = _tile_w(np.asarray(inp["w_mix_out"][0]), D, NKC)
    gl = ["ffn1_norm_pre", "ffn1_norm_post", "mix_norm_pre", "mix_norm_post", "ffn2_norm_pre", "ffn2_norm_post"]
    gains = np.stack([np.asarray(inp[g][0]).reshape(NKC, 128).T for g in gl], axis=1)
    out["gains"] = np.ascontiguousarray(gains.astype(np.float32))
    return out


def _prep_consts(inp, nmain, main0, special_masks):
    out = {}
    ntok = nmain * NT
    pos = (np.arange(ntok) + main0 * NT - FRONT).astype(np.float64)
    inv = 1.0 / (500000.0 ** (np.arange(8, dtype=np.float64) * (2.0 / 16.0)))
    C = np.ones((128, ntok), np.float64)
    S = np.zeros((128, ntok), np.float64)
    for r in range(128):
        d = r % 64
        if d < 16:
            ang = pos * inv[d % 8]
            C[r] = np.cos(ang)
            S[r] = -np.sin(ang) if d < 8 else np.sin(ang)
    cs = np.stack([C, S], axis=1).reshape(128, 2, nmain, NT).transpose(2, 0, 1, 3)
    out["rope"] = np.ascontiguousarray(cs.astype(np.float32))
    j = np.arange(128)[:, None]
    i = np.arange(128)[None, :]
    if special_masks:
        m = [(j <= i), (j > i), (j <= i) & (j >= 112), (j > i) & (j >= 112), np.zeros((128, 128), bool), (j < i)]
    else:
        m = [(j <= i), (j > i), (j <= i), (j > i), (j > i), (j < i)]
    mk = np.stack([np.tile(a.astype(np.float32), (1, 4)) for a in m], axis=1)
    out["masks"] = np.ascontiguousarray(mk)
    out["ident"] = np.eye(128, dtype=np.float32)
    bo = np.zeros((128, 128), np.float32)
    bo[:64, :64] = 1.0
    bo[64:, 64:] = 1.0
    out["bones"] = bo
    def pc(name, n):
        return np.asarray(inp[name][0]).reshape(n, 128).T
    kinds = ["rwkv_w0", "rwkv_a0", "rwkv_k_k", "rwkv_k_a", "rwkv_ln_w", "rwkv_ln_b"]
    rwp = [pc(k, 4) for k in kinds] + [np.asarray(inp["rwkv_r_k"][0]).reshape(4, 128).T, np.zeros((128, 4), np.float32)]
    out["rwp"] = np.ascontiguousarray(np.stack(rwp, axis=1).astype(np.float32))
    out["mu"] = np.ascontiguousarray(pc("rwkv_mu", 14).astype(np.float32))
    lw = np.zeros((128, 3, 512), np.float32)
    lw[:64, 0] = np.asarray(inp["rwkv_w2"][0])
    lw[64:, 1] = np.asarray(inp["rwkv_a2"][0])
    lw[:, 2] = np.asarray(inp["rwkv_g2"][0])
    out["loraw"] = lw
    out["sinks"] = np.ascontiguousarray(np.tile(np.asarray(inp["att_sinks"][0]).reshape(1, 8), (128, 1)).astype(np.float32))
    return out


def _padded_seq(x_b, meta, nsup):
    ntok = nsup * NT
    seq = np.zeros((ntok, D), np.float32)
    seq[FRONT:FRONT + NMETA] = meta
    n = min(x_b.shape[0], ntok - FRONT - NMETA)
    seq[FRONT + NMETA:FRONT + NMETA + n] = x_b[:n]
    return seq


def _fm(seq):
    return np.ascontiguousarray(seq.reshape(seq.shape[0], NKC, 128).transpose(2, 1, 0))


def kernel(**inp):
    x = np.asarray(inp["x"])
    meta = np.asarray(inp["meta_tokens"])
    wts = _prep_weights(inp)
    cA = _prep_consts(inp, NMAIN, 0, True)
    cB = _prep_consts(inp, NMAIN, NPRE, False)
    nc = build_program(NPRE, NMAIN)
    in_maps = []
    for c in range(8):
        b, half = c // 2, c % 2
        seq = _padded_seq(x[b], meta, NSUP_FULL)
        m = dict(wts)
        if half == 0:
            m.update(cA)
            m["xT"] = _fm(np.concatenate([np.zeros((NPRE * NT, D), np.float32), seq[:NMAIN * NT]], axis=0))
        else:
            m.update(cB)
            m["xT"] = _fm(seq)
        in_maps.append(m)
    res = run_bass_kernel_spmd(nc, in_maps, core_ids=list(range(8)))
    outs = []
    for b in range(4):
        oA = res.results[2 * b]["outT"].transpose(2, 1, 0).reshape(-1, D)
        oB = res.results[2 * b + 1]["outT"].transpose(2, 1, 0).reshape(-1, D)
        full = np.concatenate([oA, oB[(NMAIN - NPRE) * NT:]], axis=0)
        outs.append(full[FRONT + NMETA:FRONT + NMETA + SEQ])
    return np.stack(outs, axis=0).astype(np.float32)
```

```python
import os
import numpy as np
import concourse.bass as bass
import concourse.mybir as mybir
from concourse.bass_utils import run_bass_kernel_spmd

F32 = mybir.dt.float32
BF16 = mybir.dt.bfloat16
AF = mybir.ActivationFunctionType
ALU = mybir.AluOpType
AX = mybir.AxisListType

D = 1024
DFF = 2816
NKC = 8
NHC = 22
SEQ = 8192
NMETA = 16
SUBS = 2
NT = 128 * SUBS
FRONT = 240
NSUP_FULL = 33
NPRE = 16
NMAIN = 17
EPS = 1e-6
GN_EPS = 64e-5
CDEC = float(np.exp(-0.5))
WSLOT = 22 * 128
NWSLOT = 4

IQ, IQP, IKA, IKPA, IKB, IKPB, IV, IRW, IGA, IGR = 0, 4, 8, 9, 10, 11, 12, 13, 27, 35
NIN = 43


class Buf:
    __slots__ = ("name", "w", "r")

    def __init__(self, name):
        self.name = name
        self.w = None
        self.r = []


class Prog:
    def __init__(self, nc, ctx):
        self.nc = nc
        self.ctx = ctx
        self.engs = {"pe": nc.tensor, "act": nc.scalar, "dve": nc.vector, "pool": nc.gpsimd, "sp": nc.sync}
        self.sems = {}
        self.cnt = {}
        self.recs = {k: [] for k in self.engs}
        self.seen = {k: {} for k in self.engs}
        for k in self.engs:
            self.sems[k] = ctx.enter_context(nc.semaphore("s_" + k))
            self.cnt[k] = 0
        self.ndma = 0

    def dma_sem(self, name):
        key = "dma_" + name
        if key not in self.sems:
            self.sems[key] = self.ctx.enter_context(self.nc.semaphore(key))
            self.cnt[key] = 0
        return key

    def _deps(self, eng, reads, writes):
        need = {}
        for b in reads:
            if b.w is not None:
                k, v = b.w
                need[k] = max(need.get(k, 0), v)
        for b in writes:
            if b.w is not None:
                k, v = b.w
                need[k] = max(need.get(k, 0), v)
            for (k, v) in b.r:
                need[k] = max(need.get(k, 0), v)
        waits = []
        seen = self.seen[eng]
        for k, v in need.items():
            if k == eng and eng == "pe":
                continue
            if seen.get(k, 0) < v:
                seen[k] = v
                waits.append((k, v))
        return waits

    def op(self, eng, fn, reads=(), writes=()):
        waits = self._deps(eng, reads, writes)
        self.cnt[eng] += 1
        v = self.cnt[eng]
        self.recs[eng].append((waits, fn, eng, 1))
        for b in reads:
            b.r.append((eng, v))
        for b in writes:
            b.w = (eng, v)
            b.r = []

    def dma(self, eng, semname, fn, reads=(), writes=()):
        key = self.dma_sem(semname)
        waits = self._deps(eng, reads, writes)
        self.cnt[key] += 16
        v = self.cnt[key]
        self.recs[eng].append((waits, fn, key, 16))
        for b in reads:
            b.r.append((key, v))
        for b in writes:
            b.w = (key, v)
            b.r = []

    def final_wait(self, eng, bufs):
        waits = self._deps(eng, bufs, bufs)
        self.recs[eng].append((waits, None, None, 0))

    def emit(self, block):
        prog = self

        def run(name, e):
            for waits, fn, key, inc in prog.recs[name]:
                for (k, v) in waits:
                    e.wait_ge(prog.sems[k], v)
                if fn is not None:
                    fn(e).then_inc(prog.sems[key], inc)

        @block.tensor
        def _(e):
            run("pe", e)

        @block.scalar
        def _(e):
            run("act", e)

        @block.vector
        def _(e):
            run("dve", e)

        @block.gpsimd
        def _(e):
            run("pool", e)

        @block.sync
        def _(e):
            run("sp", e)


class K:
    def __init__(self, nc, ctx, nsup, stage, dbg):
        self.nc = nc
        self.ctx = ctx
        self.P = Prog(nc, ctx)
        self.nsup = nsup
        self.stage = stage
        self.dbg = dbg
        self.bufs = {}
        self.wslot_i = 0
        self.ps_i = 0
        self.ps_ring = [0, 1, 2, 3, 4, 5]

    def sb(self, name, shape, dt):
        return self.ctx.enter_context(self.nc.sbuf_tensor(name, shape, dt))

    def B(self, name):
        if name not in self.bufs:
            self.bufs[name] = Buf(name)
        return self.bufs[name]


def build_program(npre=NPRE, nmain=NMAIN, stage=99, dbg=False):
    from contextlib import ExitStack
    nc = bass.Bass("TRN2", target_bir_lowering=False)
    nsup = npre + nmain
    ntok = nsup * NT

    def din(name, shape, dt=F32):
        return nc.dram_tensor(name, shape, dt, kind="ExternalInput").ap()

    xT = din("xT", [128, NKC, ntok])
    outT = nc.dram_tensor("outT", [128, NKC, nmain * NT], F32, kind="ExternalOutput").ap()
    wsrc = {
        "gu1": din("gu1", [NHC, 128, 2 * NKC * 128]),
        "d1": din("d1", [NKC, 128, NHC * 128]),
        "gu2": din("gu2", [NHC, 128, 2 * NKC * 128]),
        "d2": din("d2", [NKC, 128, NHC * 128]),
        "win": din("win", [NIN, 128, NKC * 128]),
        "watt": din("watt", [NKC, 128, 4 * 128]),
        "wrw": din("wrw", [NKC, 128, 4 * 128]),
        "wmix": din("wmix", [NKC, 128, NKC * 128]),
    }
    gains = din("gains", [128, 6, NKC])
    rope = din("rope", [nmain, 128, 2, NT])
    masks = din("masks", [128, 6, 512])
    ident_in = din("ident", [128, 128])
    sinks_in = din("sinks", [128, 8])
    rwp_in = din("rwp", [128, 8, 4])
    mu_in = din("mu", [128, 14])
    lora_in_d = din("loraw", [128, 3, 512])
    bones_in = din("bones", [128, 128])
    wscr = {}
    for k, ap in wsrc.items():
        shp = list(ap.shape)
        wscr[k] = nc.dram_tensor("scr_" + k, shp, BF16, kind="Internal").ap()

    with ExitStack() as ctx:
        kb = K(nc, ctx, nsup, stage, dbg)
        kb.npre = npre
        P = kb.P
        B = kb.B
        sb = kb.sb
        block = ctx.enter_context(nc.Block())

        hT = sb("hT", [128, NKC, NT], F32)
        hT2 = sb("hT2", [128, NKC, NT], F32)
        xn = sb("xn", [128, NKC, NT], BF16)
        hid = sb("hid", [128, NHC, NT], BF16)
        fT = sb("fT", [128, NKC, NT], F32)
        sq = sb("sq", [128, NKC, NT], BF16)
        rstd = sb("rstd", [128, NT], F32)
        sgt = sb("sgt", [128, 2, NT], F32)
        gn = sb("gn", [128, 6, NKC], F32)
        gnh = sb("gnh", [128, 6, NKC], F32)
        ones_bf = sb("ones_bf", [128, 128], BF16)
        wring = sb("wring", [128, NWSLOT, WSLOT], BF16)
        psum = [ctx.enter_context(nc.psum_tensor("ps%d" % i, [128, 512], F32)) for i in range(8)]

        def ps_next():
            ring = kb.ps_ring
            i = ring[kb.ps_i % len(ring)]
            kb.ps_i += 1
            return psum[i], B("ps%d" % i)

        for k in wsrc:
            s, d = wsrc[k], wscr[k]
            n0 = s.shape[0]
            for i in range(n0):
                P.dma("pool", "cv_" + k, (lambda e, s=s, d=d, i=i: e.dma_start(out=d[i], in_=s[i])),
                      writes=[B("scr_" + k)] if i == n0 - 1 else [])
            B("scr_" + k).w = ("dma_cv_" + k, P.cnt["dma_cv_" + k])

        P.dma("sp", "c0", lambda e: e.dma_start(out=gn[:], in_=gains[:, :, :]), writes=[B("gn")])
        P.op("pool", lambda e: e.memset(ones_bf[:], 1.0), writes=[B("ones")])
        P.op("dve", lambda e: e.tensor_scalar(out=gnh[:], in0=gn[:], scalar1=0.5, scalar2=None, op0=ALU.mult),
             reads=[B("gn")], writes=[B("gnh")])

        def wload(name, idx, nel):
            slot = kb.wslot_i % NWSLOT
            kb.wslot_i += 1
            bslot = B("wslot%d" % slot)
            src = wscr[name]
            P.dma("sp", "w%d" % slot,
                  (lambda e, slot=slot, src=src, idx=idx, nel=nel:
                   e.dma_start(out=wring[:, slot, 0:nel], in_=src[idx])),
                  reads=[B("scr_" + name)], writes=[bslot])
            return wring[:, slot, :], bslot

        def rmsnorm_stats(src_tile, src_bufs, from_psum=None):
            for c in range(NKC):
                eng = ("act", "dve", "pool")[c % 3]
                if eng == "act":
                    P.op("act", (lambda e, c=c: e.activation(out=sq[:, c, :], in_=src_tile[:, c, :], func=AF.Square)),
                         reads=[src_bufs[c]], writes=[B("sq%d" % c)])
                else:
                    P.op(eng, (lambda e, c=c: e.tensor_tensor(out=sq[:, c, :], in0=src_tile[:, c, :], in1=src_tile[:, c, :],
                                                              op=ALU.mult)),
                         reads=[src_bufs[c]], writes=[B("sq%d" % c)])
            pt, pb = ps_next()
            for c in range(NKC):
                P.op("pe", (lambda e, c=c, pt=pt: e.matmul(pt[:, 0:NT], lhsT=ones_bf[:], rhs=sq[:, c, :],
                                                           start=(c == 0), stop=(c == NKC - 1))),
                     reads=[B("ones"), B("sq%d" % c)], writes=[pb])
            P.op("act", (lambda e, pt=pt: e.activation(out=rstd[:], in_=pt[:, 0:NT], func=AF.Ln,
                                                       scale=1.0 / D, bias=epsb[:, 0:1])),
                 reads=[pb, B("epsb")], writes=[B("rstd")])
            P.op("act", lambda e: e.activation(out=rstd[:], in_=rstd[:], func=AF.Exp, scale=-0.5),
                 reads=[B("rstd")], writes=[B("rstd")])

        epsb = sb("epsb", [128, 2], F32)
        P.op("pool", lambda e: e.memset(epsb[:, 0:1], EPS), writes=[B("epsb")])
        P.op("pool", lambda e: e.memset(epsb[:, 1:2], GN_EPS), writes=[B("epsb")])

        hb = [B("hT%d" % c) for c in range(NKC)]
        hb2 = [B("hT2_%d" % c) for c in range(NKC)]
        HT = [(hT, hb), (hT2, hb2)]
        fb = [B("fT%d" % c) for c in range(NKC)]
        xb = [B("xn%d" % c) for c in range(NKC)]

        def prenorm(gi, hsel=0):
            hT, hb = HT[hsel]
            rmsnorm_stats(hT, hb)
            for c in range(NKC):
                P.op("dve", (lambda e, c=c, hT=hT: e.scalar_tensor_tensor(
                    out=xn[:, c, :], in0=hT[:, c, :], scalar=gn[:, gi, c:c + 1], op0=ALU.mult,
                    in1=rstd[:], op1=ALU.mult)),
                    reads=[hb[c], B("gn"), B("rstd")], writes=[xb[c]])

        def postnorm_add(gi, half, hsel=0):
            hT, hb = HT[hsel]
            rmsnorm_stats(fT, fb)
            g = gnh if half else gn
            for c in range(NKC):
                P.op("dve", (lambda e, c=c: e.scalar_tensor_tensor(
                    out=fT[:, c, :], in0=fT[:, c, :], scalar=g[:, gi, c:c + 1], op0=ALU.mult,
                    in1=rstd[:], op1=ALU.mult)),
                    reads=[fb[c], B("gnh"), B("gn"), B("rstd")], writes=[fb[c]])
                P.op("pool", (lambda e, c=c, hT=hT: e.tensor_tensor(out=hT[:, c, :], in0=hT[:, c, :], in1=fT[:, c, :],
                                                             op=ALU.add)),
                     reads=[fb[c], hb[c]], writes=[hb[c]])

        def ffn(wgu, wd):
            hidb = [B("hid%d" % j) for j in range(NHC)]
            for j in range(NHC):
                w, wb = wload(wgu, j, 2 * NKC * 128)
                pg, pgb = ps_next()
                pu, pub = ps_next()
                for g, (pt, pbuf) in enumerate(((pg, pgb), (pu, pub))):
                    for kc in range(NKC):
                        off = (g * NKC + kc) * 128
                        P.op("pe", (lambda e, pt=pt, w=w, off=off, kc=kc: e.matmul(
                            pt[:, 0:NT], lhsT=w[:, off:off + 128], rhs=xn[:, kc, :],
                            start=(kc == 0), stop=(kc == NKC - 1))),
                            reads=[wb, xb[kc]], writes=[pbuf])
                s = j % 2
                P.op("act", (lambda e, pg=pg, s=s: e.activation(out=sgt[:, s, :], in_=pg[:, 0:NT], func=AF.Silu)),
                     reads=[pgb], writes=[B("sgt%d" % s)])
                P.op("dve", (lambda e, pu=pu, s=s, j=j: e.tensor_tensor(out=hid[:, j, :], in0=sgt[:, s, :],
                                                                       in1=pu[:, 0:NT], op=ALU.mult)),
                     reads=[B("sgt%d" % s), pub], writes=[hidb[j]])
                yield
            for c in range(NKC):
                w, wb = wload(wd, c, NHC * 128)
                pt, pbuf = ps_next()
                for kc in range(NHC):
                    P.op("pe", (lambda e, pt=pt, w=w, kc=kc: e.matmul(
                        pt[:, 0:NT], lhsT=w[:, kc * 128:(kc + 1) * 128], rhs=hid[:, kc, :],
                        start=(kc == 0), stop=(kc == NHC - 1))),
                        reads=[wb, hidb[kc]], writes=[pbuf])
                P.op("act", (lambda e, pt=pt, c=c: e.activation(out=fT[:, c, :], in_=pt[:, 0:NT], func=AF.Copy)),
                     reads=[pbuf], writes=[fb[c]])
                yield

        T1s = sb("T1", [128, 19, NT], F32)
        ropet = sb("ropet", [128, 2, NT], F32)
        rtmp = sb("rtmp", [128, 2, 2, NT], F32)
        qrot = sb("qrot", [128, 4, NT], BF16)
        kT = [sb("kTa", [128, (1 + SUBS) * 128], BF16), sb("kTb", [128, (1 + SUBS) * 128], BF16)]
        vtok = sb("vtok", [128, 1 + SUBS, 2, 65], BF16)
        zr = sb("zr", [128, 14, NT + 1], F32)
        gates = sb("gates", [128, 16, NT], BF16)
        PT = sb("PT", [128, 16, 128], BF16)
        maskb = sb("maskb", [128, 6, 128], BF16)
        esink = sb("esink", [128, 8], F32)
        den = sb("den", [128, 2, 8], F32)
        yatt = sb("yatt", [128, 512], F32)
        yattT = sb("yattT", [128, 4, NT], BF16)
        ident = sb("ident_sb", [128, 128], F32)

        stg = T1s[:, 0:4, :].rearrange("p (a b) c -> p a (b c)", a=2)
        kb.stg_i = 0

        def load_cast(dst_ap, src_ap, dst_buf):
            k = kb.stg_i % 2
            kb.stg_i += 1
            P.dma("sp", "stg%d" % k, (lambda e: e.dma_start(out=stg[:, k, :], in_=src_ap)), writes=[B("stg%d" % k)])
            P.op("dve", (lambda e: e.tensor_copy(out=dst_ap, in_=stg[:, k, :])), reads=[B("stg%d" % k)],
                 writes=[dst_buf])

        for mv in range(6):
            k_ = kb.stg_i % 2
            kb.stg_i += 1
            P.dma("sp", "stg%d" % k_, (lambda e, k_=k_, mv=mv: e.dma_start(out=stg[:, k_, 0:128], in_=masks[:, mv, 0:128])),
                  writes=[B("stg%d" % k_)])
            P.op("dve", (lambda e, k_=k_, mv=mv: e.tensor_copy(out=maskb[:, mv, :], in_=stg[:, k_, 0:128])),
                 reads=[B("stg%d" % k_)], writes=[B("maskb")])
        P.dma("sp", "c2", lambda e: e.dma_start(out=ident[:], in_=ident_in[:, :]), writes=[B("ident")])
        P.dma("sp", "c3", lambda e: e.dma_start(out=esink[:], in_=sinks_in[:, :]), writes=[B("esink")])
        P.op("act", lambda e: e.activation(out=esink[:], in_=esink[:], func=AF.Exp), reads=[B("esink")],
             writes=[B("esink")])
        kslot = [[B("kT%d_%d" % (a, sl)) for sl in range(1 + SUBS)] for a in range(2)]
        vslot = [B("v_%d" % sl) for sl in range(1 + SUBS)]
        P.op("pool", lambda e: e.memset(kT[0][:], 0.0), writes=kslot[0])
        P.op("pool", lambda e: e.memset(kT[1][:], 0.0), writes=kslot[1])
        P.op("pool", lambda e: e.memset(vtok[:], 0.0), writes=vslot)
        P.op("pool", lambda e: e.memset(vtok[:, :, :, 64:65], 1.0), writes=vslot)
        P.op("pool", lambda e: e.memset(zr[:, :, 0:1], 0.0), writes=[B("zrp%d" % c) for c in range(14)])
        qb = [B("qrot%d" % c) for c in range(4)]
        zb = [B("zr%d" % c) for c in range(14)]
        gb = [B("gate%d" % c) for c in range(16)]

        def inproj_chunk(ci):
            w, wb = wload("win", ci, NKC * 128)
            pt, pbuf = ps_next()
            for kc in range(NKC):
                P.op("pe", (lambda e, pt=pt, w=w, kc=kc: e.matmul(
                    pt[:, 0:NT], lhsT=w[:, kc * 128:(kc + 1) * 128], rhs=xn[:, kc, :],
                    start=(kc == 0), stop=(kc == NKC - 1))),
                    reads=[wb, xb[kc]], writes=[pbuf])
            return pt, pbuf

        def rope_pair(ci, cpi, out_ap, out_bufs, par):
            p1, b1 = inproj_chunk(ci)
            p2, b2 = inproj_chunk(cpi)
            P.op("dve", (lambda e, p1=p1, par=par: e.tensor_tensor(out=rtmp[:, par, 0, :], in0=p1[:, 0:NT],
                                                                   in1=ropet[:, 0, :], op=ALU.mult)),
                 reads=[b1, B("ropet")], writes=[B("rtmp%d0" % par)])
            P.op("dve", (lambda e, p2=p2, par=par: e.tensor_tensor(out=rtmp[:, par, 1, :], in0=p2[:, 0:NT],
                                                                   in1=ropet[:, 1, :], op=ALU.mult)),
                 reads=[b2, B("ropet")], writes=[B("rtmp%d1" % par)])
            P.op("pool", (lambda e, par=par: e.tensor_tensor(out=out_ap, in0=rtmp[:, par, 0, :],
                                                             in1=rtmp[:, par, 1, :], op=ALU.add)),
                 reads=[B("rtmp%d0" % par), B("rtmp%d1" % par)], writes=out_bufs)

        def mask_ids(gt):
            if gt == 1:
                return 2, 4
            if gt == 2:
                return 0, 3
            return 0, 1

        def mixer_inproj(st, full=True):
            prenorm(2, st % 2)
            yield
            if not full:
                for c in range(13):
                    pt, pbuf = inproj_chunk(IRW + c)
                    P.op("act", (lambda e, pt=pt, c=c: e.activation(out=zr[:, c, 1:NT + 1], in_=pt[:, 0:NT], func=AF.Copy)),
                         reads=[pbuf], writes=[zb[c]])
                    yield
                return
            ml = st - kb.npre
            P.dma("sp", "rope", (lambda e: e.dma_start(out=ropet[:], in_=rope[ml])), writes=[B("ropet")])
            for a in range(2):
                P.op("pool", (lambda e, a=a: e.tensor_copy(out=kT[a][:, 0:128], in_=kT[a][:, SUBS * 128:(SUBS + 1) * 128])),
                     reads=[kslot[a][SUBS]], writes=[kslot[a][0]])
            P.op("pool", lambda e: e.tensor_copy(out=vtok[:, 0, :, 0:64], in_=vtok[:, SUBS, :, 0:64]),
                 reads=[vslot[SUBS]], writes=[vslot[0]])
            for c in range(14):
                pt, pbuf = inproj_chunk(IRW + c)
                P.op("act", (lambda e, pt=pt, c=c: e.activation(out=zr[:, c, 1:NT + 1], in_=pt[:, 0:NT], func=AF.Copy)),
                     reads=[pbuf], writes=[zb[c]])
                yield
            yield "RWKV_IN_DONE"
            for c in range(4):
                rope_pair(IQ + c, IQP + c, qrot[:, c, :], [qb[c]], c % 2)
                yield
            rope_pair(IKA, IKPA, kT[0][:, 128:(1 + SUBS) * 128], kslot[0][1:], 0)
            yield
            rope_pair(IKB, IKPB, kT[1][:, 128:(1 + SUBS) * 128], kslot[1][1:], 1)
            yield
            wv, wvb = wload("win", IV, NKC * 128)
            for i in range(SUBS):
                pt, pbuf = ps_next()
                for kc in range(NKC):
                    P.op("pe", (lambda e, pt=pt, kc=kc, i=i: e.matmul(
                        pt[:, 0:128], lhsT=xn[:, kc, i * 128:(i + 1) * 128], rhs=wv[:, kc * 128:(kc + 1) * 128],
                        start=(kc == 0), stop=(kc == NKC - 1))),
                        reads=[wvb, xb[kc]], writes=[pbuf])
                P.op("act", (lambda e, pt=pt, i=i: e.activation(
                    out=vtok[:, 1 + i, :, 0:64], in_=pt[:, 0:128].rearrange("p (a b) -> p a b", a=2), func=AF.Copy)),
                    reads=[pbuf], writes=[vslot[1 + i]])
            yield
            for c in range(16):
                pt, pbuf = inproj_chunk(IGA + c)
                P.op("dve", (lambda e, pt=pt, c=c: e.tensor_copy(out=gates[:, c, :], in_=pt[:, 0:NT])),
                     reads=[pbuf], writes=[gb[c]])
                yield

        def attention(st):
            CUT = 99
            for i in range(SUBS):
                mc, mp = mask_ids((st - kb.npre) * SUBS + i)
                banks = [ps_next() for _ in range(4)]
                for cp in range(2):
                    slot = 1 + i - cp
                    for h in range(8):
                        pbs = (h % 2) * 64
                        g = h // 4
                        a = 0 if g == (h % 2) else 1
                        pt, pbuf = banks[cp * 2 + h % 2]
                        hh = h // 2
                        P.op("pe", (lambda e, pt=pt, a=a, pbs=pbs, slot=slot, h=h, hh=hh, i=i: e.matmul(
                            pt[:, hh * 128:(hh + 1) * 128],
                            lhsT=kT[a][pbs:pbs + 64, slot * 128:(slot + 1) * 128],
                            rhs=qrot[pbs:pbs + 64, h // 2, i * 128:(i + 1) * 128], start=True, stop=True)),
                            reads=[kslot[a][slot], qb[h // 2]], writes=[pbuf])
                KSUB = 99
                for bi in range(4):
                    if KSUB <= 0:
                        break
                    pt, pbuf = banks[bi]
                    mi = mc if bi < 2 else mp
                    P.op("act", (lambda e, pt=pt, bi=bi: e.activation(
                        out=PT[:, bi * 4:(bi + 1) * 4, :], in_=pt[:, :].rearrange("p (a b) -> p a b", a=4),
                        func=AF.Exp, scale=0.125)),
                        reads=[pbuf], writes=[B("PT%d" % bi)])
                    if KSUB <= 1:
                        continue
                    P.op("pool", (lambda e, bi=bi, mi=mi: e.tensor_tensor(
                        out=PT[:, bi * 4:(bi + 1) * 4, :], in0=PT[:, bi * 4:(bi + 1) * 4, :],
                        in1=maskb[:, mi:mi + 1, :].broadcast_to([128, 4, 128]), op=ALU.mult)),
                        reads=[B("PT%d" % bi), B("maskb")], writes=[B("PT%d" % bi)])
                if CUT <= 2:
                    continue
                yield
                obanks = [ps_next() for _ in range(2)]
                for h in range(8):
                    g = h // 4
                    pt, pbuf = obanks[h // 4]
                    hh = h % 4
                    for cp in range(2):
                        slot = 1 + i - cp
                        P.op("pe", (lambda e, pt=pt, hh=hh, cp=cp, h=h, slot=slot, g=g: e.matmul(
                            pt[:, hh * 65:(hh + 1) * 65], lhsT=PT[:, cp * 8 + (h % 2) * 4 + h // 2, :],
                            rhs=vtok[:, slot, g, :], start=(cp == 0), stop=(cp == 1))),
                            reads=[B("PT%d" % (cp * 2 + h % 2)), vslot[slot]], writes=[pbuf])
                for hb2 in range(2):
                    pt, pbuf = obanks[hb2]
                    o3 = pt[:, 0:260].rearrange("p (a b) -> p a b", b=65)
                    P.op("dve", (lambda e, o3=o3, hb2=hb2: e.tensor_tensor(
                        out=den[:, 0, hb2 * 4:(hb2 + 1) * 4].unsqueeze(2), in0=o3[:, :, 64:65],
                        in1=esink[:, hb2 * 4:(hb2 + 1) * 4].unsqueeze(2), op=ALU.add)),
                        reads=[pbuf, B("esink")], writes=[B("den%d" % hb2)])
                    P.op("dve", (lambda e, hb2=hb2: e.reciprocal(out=den[:, 1, hb2 * 4:(hb2 + 1) * 4],
                                                                in_=den[:, 0, hb2 * 4:(hb2 + 1) * 4])),
                         reads=[B("den%d" % hb2)], writes=[B("den%d" % hb2)])
                    P.op("dve", (lambda e, o3=o3, hb2=hb2: e.tensor_tensor(
                        out=yatt[:, hb2 * 256:(hb2 + 1) * 256].rearrange("p (a b) -> p a b", b=64),
                        in0=o3[:, :, 0:64],
                        in1=den[:, 1, hb2 * 4:(hb2 + 1) * 4].unsqueeze(2).broadcast_to([128, 4, 64]), op=ALU.mult)),
                        reads=[pbuf, B("den%d" % hb2)], writes=[B("yatt%d" % hb2)])
                if CUT <= 3:
                    continue
                pt, pbuf = ps_next()
                for kc in range(4):
                    P.op("pe", (lambda e, pt=pt, kc=kc: e.transpose(pt[:, kc * 128:(kc + 1) * 128],
                                                                    yatt[:, kc * 128:(kc + 1) * 128], ident[:])),
                         reads=[B("yatt%d" % (kc // 2)), B("ident")], writes=[pbuf])
                P.op("act", (lambda e, pt=pt, i=i: e.activation(
                    out=yattT[:, :, i * 128:(i + 1) * 128], in_=pt[:, :].rearrange("p (a b) -> p a b", a=4),
                    func=AF.Copy)),
                    reads=[pbuf], writes=[B("yattT")])
                yield

        rwp = sb("rwp_sb", [128, 9, 4], F32)
        mus = sb("mus", [128, 14], F32)
        loraw = sb("loraw_sb", [128, 3, 512], BF16)
        bones = sb("bones_sb", [128, 128], BF16)
        identb = sb("identb", [128, 128], BF16)
        ones_f = sb("ones_f", [128, 128], F32)
        mX = sb("mX", [128, 384], BF16)
        mY = sb("mY", [128, 256], BF16)
        lin = sb("lin", [128, NT], BF16)
        sgx = sb("sgx", [128, NT], BF16)
        dz = sgt
        T1 = T1s
        gT = sb("gT", [128, 4, NT], BF16)
        ksq = sb("ksq", [128, NT], BF16)
        gg = sb("gg", [128, 2, 2, SUBS, 129], F32)
        sc = sb("sc", [128, 4, 4, SUBS], F32)
        opb = sb("opb", [128, 5, 4, NT], BF16)
        rt0 = sb("rt0", [128, 4, NT], F32)
        prod = sb("prod", [128, 4, NT], BF16)
        tokm = sb("tokm", [128, SUBS, 4, 512], BF16)
        AX8 = sb("AX8", [128, 8, 384], BF16)
        AY8 = sb("AY8", [128, 8, 256], BF16)
        SQ8 = sb("SQ8", [128, 2, 8, 2, 128], BF16)
        Xbf = sb("Xbf", [128, 2, 8, 64], BF16)
        Pblk = sb("Pblk", [128, 4, 128], F32)
        Qs = sb("Qs", [128, 4, 64], F32)
        G0T = sb("G0T", [128, 4, 128], F32)
        Hst = sb("Hst", [128, 4, 64], F32)
        Ysb = sb("Ysb", [128, 512], F32)
        yn = sb("yn", [128, 512], F32)
        sqY = yn
        gst = sb("gst", [128, 6, 8], F32)
        ynT = sb("ynT", [128, 4, NT], F32)
        yrT = sb("yrT", [128, 4, NT], BF16)

        P.dma("sp", "c4", lambda e: e.dma_start(out=rwp[:, 0:8, :], in_=rwp_in[:, :, :]), writes=[B("rwp")])
        P.dma("sp", "c5", lambda e: e.dma_start(out=mus[:], in_=mu_in[:, :]), writes=[B("mus")])
        for q3 in range(3):
            load_cast(loraw[:, q3, :], lora_in_d[:, q3, :], B("loraw"))
        P.dma("sp", "stgb", lambda e: e.dma_start(out=ones_f[:], in_=bones_in[:, :]), writes=[B("ones_f")])
        P.op("dve", lambda e: e.tensor_copy(out=bones[:], in_=ones_f[:]), reads=[B("ones_f")], writes=[B("bones")])
        P.op("pool", lambda e: e.memset(ones_f[:], 1.0), reads=[B("bones")], writes=[B("ones_f")])
        P.op("dve", lambda e: e.tensor_copy(out=identb[:], in_=ident[:]), reads=[B("ident")], writes=[B("identb")])
        P.op("dve", lambda e: e.tensor_scalar(out=rwp[:, 7, :], in0=rwp[:, 3, :], scalar1=-1.0, scalar2=1.0,
                                              op0=ALU.mult, op1=ALU.add), reads=[B("rwp")], writes=[B("rwp")])
        P.op("dve", lambda e: e.tensor_scalar(out=rwp[:, 8, :], in0=rwp[:, 2, :], scalar1=-1.0, scalar2=None, op0=ALU.mult),
             reads=[B("rwp")], writes=[B("rwp")])
        P.op("pool", lambda e: e.tensor_copy(out=mX[:, 0:128], in_=maskb[:, 5, 0:128]), reads=[B("maskb")], writes=[B("mX")])
        P.op("pool", lambda e: e.tensor_copy(out=mX[:, 128:256], in_=maskb[:, 1, 0:128]), reads=[B("maskb")], writes=[B("mX")])
        P.op("pool", lambda e: e.tensor_copy(out=mX[:, 256:384], in_=maskb[:, 5, 0:128]), reads=[B("maskb")], writes=[B("mX")])
        P.op("pool", lambda e: e.tensor_copy(out=mY[:, 0:128], in_=maskb[:, 0, 0:128]), reads=[B("maskb")], writes=[B("mY")])
        P.op("pool", lambda e: e.tensor_copy(out=mY[:, 128:256], in_=maskb[:, 0, 0:128]), reads=[B("maskb")], writes=[B("mY")])
        P.op("pool", lambda e: e.memset(Hst[:], 0.0), writes=[B("H%d" % j) for j in range(4)])
        P.op("pool", lambda e: e.memset(Pblk[:], 0.0), writes=[B("Pblk%d" % j) for j in range(4)])
        P.op("pool", lambda e: e.memset(gg[:], 1.0), writes=[B("gg0"), B("gg1")])
        YB = [(psum[6], B("ps6")), (psum[7], B("ps7"))]
        tb = [B("T1_%d" % q) for q in range(19)]
        P.op("pool", lambda e: e.memset(T1[:, 18, 0:1], 0.0), writes=[B("stg0"), B("stg1")] + tb)
        opbuf = [[B("opb%d_%d" % (k_, j)) for j in range(4)] for k_ in range(5)]

        def rwkv(st, full=True):
            zin = lambda c: zr[:, c, 1:NT + 1]
            for c in range(14 if full else 13):
                par = c % 2
                P.op("pool", (lambda e, c=c, par=par: e.tensor_tensor(out=dz[:, par, :], in0=zr[:, c, 0:NT],
                                                                      in1=zr[:, c, 1:NT + 1], op=ALU.subtract)),
                     reads=[zb[c], B("zrp%d" % c)], writes=[B("sgt%d" % par)])
                P.op("pool", (lambda e, c=c: e.tensor_copy(out=zr[:, c, 0:1], in_=zr[:, c, NT:NT + 1])),
                     reads=[zb[c], B("sgt%d" % par)], writes=[B("zrp%d" % c)])
                P.op("dve", (lambda e, c=c, par=par: e.scalar_tensor_tensor(
                    out=zr[:, c, 1:NT + 1], in0=dz[:, par, :], scalar=mus[:, c:c + 1], op0=ALU.mult,
                    in1=zr[:, c, 1:NT + 1], op1=ALU.add)),
                    reads=[B("sgt%d" % par), B("mus"), zb[c]], writes=[zb[c]])
                if c % 2 == 1:
                    yield
            P.op("act", lambda e: e.activation(out=lin[0:64, :], in_=zr[0:64, 12, 1:NT + 1], func=AF.Tanh),
                 reads=[zb[12]], writes=[B("lin0")])
            P.op("pool", lambda e: e.tensor_copy(out=lin[64:128, :], in_=zr[64:128, 12, 1:NT + 1]),
                 reads=[zb[12]], writes=[B("lin1")])
            if full:
                P.op("act", lambda e: e.activation(out=sgx[:], in_=zr[:, 13, 1:NT + 1], func=AF.Sigmoid),
                     reads=[zb[13]], writes=[B("sgx")])
            def prep_j(j, T1, tb):
                pw, pwb = ps_next()
                P.op("pe", (lambda e, pw=pw, j=j: e.matmul(pw[:, 0:NT], lhsT=loraw[0:64, 0, j * 128:(j + 1) * 128],
                                                           rhs=lin[0:64, :], start=True, stop=True)),
                     reads=[B("loraw"), B("lin0")], writes=[pwb])
                pa, pab = ps_next()
                P.op("pe", (lambda e, pa=pa, j=j: e.matmul(pa[:, 0:NT], lhsT=loraw[64:128, 1, j * 128:(j + 1) * 128],
                                                           rhs=lin[64:128, :], start=True, stop=True)),
                     reads=[B("loraw"), B("lin1")], writes=[pab])
                if full:
                    pg, pgb = ps_next()
                    P.op("pe", (lambda e, pg=pg, j=j: e.matmul(pg[:, 0:NT], lhsT=loraw[:, 2, j * 128:(j + 1) * 128],
                                                               rhs=sgx[:], start=True, stop=True)),
                         reads=[B("loraw"), B("sgx")], writes=[pgb])
                P.op("act", (lambda e, pw=pw, j=j: e.activation(out=T1[:, 0, :], in_=pw[:, 0:NT], func=AF.Sigmoid,
                                                                bias=rwp[:, 0, j:j + 1])),
                     reads=[pwb, B("rwp")], writes=[tb[0]])
                P.op("act", (lambda e, pa=pa, j=j: e.activation(out=T1[:, 1, :], in_=pa[:, 0:NT], func=AF.Sigmoid,
                                                                bias=rwp[:, 1, j:j + 1])),
                     reads=[pab, B("rwp")], writes=[tb[1]])
                if full:
                    P.op("dve", (lambda e, pg=pg, j=j: e.tensor_copy(out=gT[:, j, :], in_=pg[:, 0:NT])),
                         reads=[pgb], writes=[B("gT%d" % j)])
                yield
                P.op("act", (lambda e, j=j: e.activation(out=ksq[:], in_=zin(4 + j), func=AF.Square,
                                                         scale=rwp[:, 2, j:j + 1])),
                     reads=[zb[4 + j], B("rwp")], writes=[B("ksq")])
                pss, pssb = ps_next()
                P.op("pe", (lambda e, pss=pss: e.matmul(pss[:, 0:NT], lhsT=bones[:], rhs=ksq[:], start=True, stop=True)),
                     reads=[B("bones"), B("ksq")], writes=[pssb])
                P.op("dve", (lambda e, pss=pss: e.tensor_scalar(out=T1[:, 2, :], in0=pss[:, 0:NT], scalar1=1e-24,
                                                                scalar2=None, op0=ALU.max)),
                     reads=[pssb], writes=[tb[2]])
                P.op("act", lambda e: e.activation(out=T1[:, 2, :], in_=T1[:, 2, :], func=AF.Ln), reads=[tb[2]], writes=[tb[2]])
                P.op("act", lambda e: e.activation(out=T1[:, 2, :], in_=T1[:, 2, :], func=AF.Exp, scale=-0.5),
                     reads=[tb[2]], writes=[tb[2]])
                P.op("dve", (lambda e, j=j: e.scalar_tensor_tensor(out=T1[:, 3, :], in0=zin(4 + j),
                                                                   scalar=rwp[:, 8, j:j + 1], op0=ALU.mult,
                                                                   in1=T1[:, 2, :], op1=ALU.mult)),
                     reads=[zb[4 + j], B("rwp"), tb[2]], writes=[tb[3]])
                P.op("dve", (lambda e, j=j: e.tensor_scalar(out=T1[:, 4, :], in0=T1[:, 1, :], scalar1=rwp[:, 3, j:j + 1],
                                                            scalar2=rwp[:, 7, j:j + 1], op0=ALU.mult, op1=ALU.add)),
                     reads=[tb[1], B("rwp")], writes=[tb[4]])
                P.op("pool", (lambda e, j=j: e.tensor_tensor(out=T1[:, 5, :], in0=zin(4 + j), in1=T1[:, 4, :], op=ALU.mult)),
                     reads=[zb[4 + j], tb[4]], writes=[tb[5]])
                P.op("dve", lambda e: e.scalar_tensor_tensor(out=T1[:, 6, :], in0=T1[:, 3, :], scalar=-1.0, op0=ALU.mult,
                                                             in1=T1[:, 1, :], op1=ALU.mult),
                     reads=[tb[3], tb[1]], writes=[tb[6]])
                if full:
                    P.op("dve", (lambda e, j=j: e.scalar_tensor_tensor(out=prod[:, j, :], in0=zin(j), scalar=rwp[:, 6, j:j + 1],
                                                                       op0=ALU.mult, in1=T1[:, 5, :], op1=ALU.mult)),
                         reads=[zb[j], B("rwp"), tb[5]], writes=[B("prod%d" % j)])
                yield
                gsel = j % 2
                ggb = B("gg%d" % gsel)
                scb = B("sc%d" % j)
                P.op("act", lambda e: e.activation(out=T1[:, 7, :], in_=T1[:, 0, :], func=AF.Exp, scale=-CDEC),
                     reads=[tb[0]], writes=[tb[7]])
                P.op("act", lambda e: e.activation(out=T1[:, 8, :], in_=T1[:, 0, :], func=AF.Exp, scale=CDEC),
                     reads=[tb[0]], writes=[tb[8]])
                for i in range(SUBS):
                    for q_, slot in ((0, 7), (1, 8)):
                        P.op("dve", (lambda e, i=i, q_=q_, slot=slot: e.tensor_tensor_scan(
                            out=gg[:, gsel, q_, i, 1:129], data0=T1[:, slot, i * 128:(i + 1) * 128], data1=ones_f[:, 0:128],
                            initial=1.0, op0=ALU.mult, op1=ALU.mult)),
                            reads=[tb[slot], B("ones_f")], writes=[ggb])
                P.op("dve", (lambda e: e.tensor_tensor(out=sc[:, 2, j, :], in0=gg[:, gsel, 0, :, 128], in1=gg[:, gsel, 1, :, 64],
                                                       op=ALU.mult)), reads=[ggb], writes=[scb])
                P.op("dve", (lambda e: e.tensor_copy(out=sc[:, 3, j, :], in_=gg[:, gsel, 0, :, 64])), reads=[ggb], writes=[scb])
                yield
                for i in range(SUBS):
                    isl = slice(i * 128, (i + 1) * 128)
                    gmid = gg[:, gsel, 0, i, 64:65]
                    imid = gg[:, gsel, 1, i, 64:65]
                    gprev = gg[:, gsel, 0, i, 0:128]
                    gcur = gg[:, gsel, 0, i, 1:129]
                    icur = gg[:, gsel, 1, i, 1:129]
                    P.op("dve", (lambda e, isl=isl, imid=imid, gprev=gprev: e.scalar_tensor_tensor(
                        out=opb[:, 0, j, isl], in0=T1[:, 3, isl], scalar=imid, op0=ALU.mult, in1=gprev, op1=ALU.mult)),
                        reads=[tb[3], ggb], writes=[opbuf[0][j]])
                    P.op("pool", (lambda e, isl=isl, gprev=gprev: e.tensor_tensor(out=opb[:, 1, j, isl], in0=T1[:, 3, isl],
                                                                                  in1=gprev, op=ALU.mult)),
                         reads=[tb[3], ggb], writes=[opbuf[1][j]])
                    P.op("dve", (lambda e, isl=isl, gmid=gmid, icur=icur: e.scalar_tensor_tensor(
                        out=opb[:, 2, j, isl], in0=T1[:, 6, isl], scalar=gmid, op0=ALU.mult, in1=icur, op1=ALU.mult)),
                        reads=[tb[6], ggb], writes=[opbuf[2][j]])
                    P.op("dve", (lambda e, isl=isl, gmid=gmid, icur=icur: e.scalar_tensor_tensor(
                        out=opb[:, 3, j, isl], in0=T1[:, 5, isl], scalar=gmid, op0=ALU.mult, in1=icur, op1=ALU.mult)),
                        reads=[tb[5], ggb], writes=[opbuf[3][j]])
                    yield
                    if not full:
                        continue
                    P.op("dve", (lambda e, isl=isl, i=i, imid=imid, gcur=gcur: e.scalar_tensor_tensor(
                        out=opb[:, 4, j, isl], in0=zr[:, j, 1 + i * 128:1 + (i + 1) * 128], scalar=imid, op0=ALU.mult,
                        in1=gcur, op1=ALU.mult)), reads=[zb[j], ggb], writes=[opbuf[4][j]])
                    P.op("pool", (lambda e, isl=isl, i=i, gcur=gcur: e.tensor_tensor(
                        out=rt0[:, j, isl], in0=zr[:, j, 1 + i * 128:1 + (i + 1) * 128], in1=gcur, op=ALU.mult)),
                        reads=[zb[j], ggb], writes=[B("rt0_%d" % j)])
            for j in range(4):
                jo = 9 * (j % 2)
                yield from prep_j(j, T1s[:, jo:jo + 9, :], tb[jo:jo + 9])
            for i in range(SUBS):
                isl = slice(i * 128, (i + 1) * 128)
                pA, pAb = ps_next()
                pAv = pA[:, :].bitcast(BF16)
                for j in range(4):
                    P.op("pe", (lambda e, pAv=pAv, j=j, isl=isl: e.transpose(pAv[:, j * 128:(j + 1) * 128],
                                                                             opb[:, 1, j, isl], identb[:])),
                         reads=[opbuf[1][j], B("identb")], writes=[pAb])
                    P.op("pe", (lambda e, pAv=pAv, j=j, isl=isl: e.transpose(pAv[:, 512 + j * 128:512 + (j + 1) * 128],
                                                                             opb[:, 2, j, isl], identb[:])),
                         reads=[opbuf[2][j], B("identb")], writes=[pAb])
                P.op("act", (lambda e, pAv=pAv, i=i: e.activation(out=tokm[:, i, 0:2, :],
                                                                  in_=pAv.rearrange("p (a b) -> p a b", a=2), func=AF.Copy)),
                     reads=[pAb], writes=[B("tokm%d_0" % i), B("tokm%d_1" % i)])
                yield
                pB_, pBb = ps_next()
                pBv = pB_[:, :].bitcast(BF16)
                for j in range(4):
                    P.op("pe", (lambda e, pBv=pBv, j=j, isl=isl: e.transpose(pBv[:, j * 128:(j + 1) * 128],
                                                                             opb[:, 3, j, isl], identb[:])),
                         reads=[opbuf[3][j], B("identb")], writes=[pBb])
                P.op("act", (lambda e, pBv=pBv, i=i: e.activation(out=tokm[:, i, 2, :], in_=pBv[:, 0:512], func=AF.Copy)),
                     reads=[pBb], writes=[B("tokm%d_2" % i)])
                pV, pVb = ps_next()
                for j in range(4):
                    P.op("pe", (lambda e, pV=pV, j=j, i=i: e.transpose(pV[:, j * 128:(j + 1) * 128],
                                                                       zr[:, 8 + j, 1 + i * 128:1 + (i + 1) * 128], ident[:])),
                         reads=[zb[8 + j], B("ident")], writes=[pVb])
                P.op("act", (lambda e, pV=pV, i=i: e.activation(out=tokm[:, i, 3, :], in_=pV[:, :], func=AF.Copy)),
                     reads=[pVb], writes=[B("tokm%d_3" % i)])
                yield
            yield "PREP_DONE"
            for i in range(SUBS):
                isl = slice(i * 128, (i + 1) * 128)
                ystart = [True, True]
                axb = [B("AX8_%d" % h) for h in range(8)]
                ayb = [B("AY8_%d" % h) for h in range(8)]
                x32g = [B("X32g%d" % g) for g in range(2)]
                xbfg = [B("Xbfg%d" % g) for g in range(2)]
                for h in range(8):
                    j, e_ = h // 2, h % 2
                    pbs = e_ * 64
                    at_ = opb[pbs:pbs + 64, 0, j, isl]
                    bt_ = opb[pbs:pbs + 64, 2, j, isl]
                    kt_ = opb[pbs:pbs + 64, 3, j, isl]
                    rt_ = opb[pbs:pbs + 64, 4, j, isl]
                    rds = [opbuf[k_][j] for k_ in (0, 2, 3, 4)]
                    bx, bxb = ps_next()
                    if full:
                        by, byb = ps_next()
                        rds = [opbuf[k_][j] for k_ in (0, 2, 3, 4)]
                    else:
                        rds = [opbuf[k_][j] for k_ in (0, 2, 3)]
                    for (dst, l_, r_) in ((bx[:, 0:128], bt_, at_), (bx[:, 128:256], at_, bt_), (bx[:, 256:384], kt_, at_)):
                        P.op("pe", (lambda e, dst=dst, l_=l_, r_=r_: e.matmul(dst, lhsT=l_, rhs=r_, start=True, stop=True)),
                             reads=rds, writes=[bxb])
                    P.op("dve", (lambda e, bx=bx, h=h: e.tensor_tensor(out=AX8[:, h, :], in0=bx[:, 0:384], in1=mX[:],
                                                                       op=ALU.mult)), reads=[bxb, B("mX")], writes=[axb[h]])
                    if full:
                        for (dst, l_, r_) in ((by[:, 0:128], bt_, rt_), (by[:, 128:256], kt_, rt_)):
                            P.op("pe", (lambda e, dst=dst, l_=l_, r_=r_: e.matmul(dst, lhsT=l_, rhs=r_, start=True, stop=True)),
                                 reads=rds, writes=[byb])
                        P.op("dve", (lambda e, by=by, h=h: e.tensor_tensor(out=AY8[:, h, :], in0=by[:, 0:256], in1=mY[:],
                                                                           op=ALU.mult)), reads=[byb, B("mY")], writes=[ayb[h]])
                    yield
                kb.ps_ring = [0, 1, 2, 3]
                XB = [(psum[4], B("ps4")), (psum[5], B("ps5"))]
                casteng = ["act", "dve"]

                def xcast(g):
                    xb_, xbb = XB[g]
                    src = xb_[:, :].rearrange("p (h a d) -> p a h d", h=4, a=2)
                    dst = Xbf[:, :, g * 4:(g + 1) * 4, :]
                    if casteng[g] == "act":
                        P.op("act", (lambda e: e.activation(out=dst, in_=src, func=AF.Copy)), reads=[xbb], writes=[xbfg[g]])
                    else:
                        P.op("dve", (lambda e: e.tensor_copy(out=dst, in_=src)), reads=[xbb], writes=[xbfg[g]])

                for g in range(2):
                    xb_, xbb = XB[g]
                    for hh in range(4):
                        h = g * 4 + hh
                        P.op("pe", (lambda e, xb_=xb_, hh=hh, h=h, i=i, st_=(hh == 0): e.matmul(
                            xb_[:, hh * 128:hh * 128 + 64], lhsT=identb[:], rhs=tokm[:, i, 0, h * 64:(h + 1) * 64],
                            start=st_, stop=False, skip_group_check=True)),
                            reads=[B("identb"), B("tokm%d_0" % i)], writes=[xbb])
                        P.op("pe", (lambda e, xb_=xb_, hh=hh, h=h, i=i: e.matmul(
                            xb_[:, hh * 128 + 64:hh * 128 + 128], lhsT=AX8[:, h, 256:384],
                            rhs=tokm[:, i, 3, h * 64:(h + 1) * 64], start=False, stop=False, skip_group_check=True)),
                            reads=[axb[h], B("tokm%d_3" % i)], writes=[xbb])
                    xcast(g)
                    yield
                curA = [AX8[:, h, 128:256] for h in range(8)]
                curAT = [AX8[:, h, 0:128] for h in range(8)]
                curb = [[axb[h]] for h in range(8)]
                pp = 0
                for lvl in range(7):
                    for g in range(2):
                        xb_, xbb = XB[g]
                        for hh in range(4):
                            h = g * 4 + hh
                            P.op("pe", (lambda e, xb_=xb_, hh=hh, h=h, lt=curAT[h]: e.matmul(
                                xb_[:, hh * 128:(hh + 1) * 128].rearrange("p (a d) -> p a d", a=2), lhsT=lt,
                                rhs=Xbf[:, :, h, :], start=False, stop=(lvl == 6), skip_group_check=True)),
                                reads=curb[h] + [xbfg[g]], writes=[xbb])
                        xcast(g)
                        yield
                    if lvl < 6:
                        for hp in range(4):
                            pq, pqb = ps_next()
                            sqb = B("SQ8_%d_%d" % (pp, hp))
                            for q_ in range(2):
                                h = hp * 2 + q_
                                P.op("pe", (lambda e, pq=pq, q_=q_, la=curAT[h], ra=curA[h]: e.matmul(
                                    pq[:, q_ * 256:q_ * 256 + 128], lhsT=la, rhs=ra, start=True, stop=True)),
                                    reads=curb[h], writes=[pqb])
                                P.op("pe", (lambda e, pq=pq, q_=q_, la=curA[h], ra=curAT[h]: e.matmul(
                                    pq[:, q_ * 256 + 128:q_ * 256 + 256], lhsT=la, rhs=ra, start=True, stop=True)),
                                    reads=curb[h], writes=[pqb])
                            eng = "act" if hp % 2 == 0 else "dve"
                            if eng == "act":
                                P.op("act", (lambda e, pq=pq, hp=hp, pp=pp: e.activation(
                                    out=SQ8[:, pp, hp * 2:hp * 2 + 2, :, :],
                                    in_=pq[:, :].rearrange("p (q a c) -> p q a c", q=2, a=2), func=AF.Copy)),
                                    reads=[pqb], writes=[sqb])
                            else:
                                P.op("dve", (lambda e, pq=pq, hp=hp, pp=pp: e.tensor_copy(
                                    out=SQ8[:, pp, hp * 2:hp * 2 + 2, :, :],
                                    in_=pq[:, :].rearrange("p (q a c) -> p q a c", q=2, a=2))),
                                    reads=[pqb], writes=[sqb])
                            for q_ in range(2):
                                h = hp * 2 + q_
                                curA[h] = SQ8[:, pp, h, 0, :]
                                curAT[h] = SQ8[:, pp, h, 1, :]
                                curb[h] = [sqb]
                            if hp % 2 == 1:
                                yield
                        pp ^= 1
                kb.ps_ring = [0, 1, 2, 3, 4, 5]
                pP, pPb = ps_next()
                pQ, pQb = ps_next()
                pG = [ps_next(), ps_next()] if full else [None, None]
                for j in range(4):
                    g = j // 2
                    jsl = slice(j * 128, (j + 1) * 128)
                    W0p = Xbf[:, 0, 2 * j:2 * j + 2, :]
                    P.op("pe", (lambda e, W0p=W0p, i=i, jsl=jsl, pP=pP: e.matmul(pP[:, jsl], lhsT=W0p, rhs=tokm[:, i, 1, jsl],
                                                                        start=True, stop=True)),
                         reads=[xbfg[g], B("tokm%d_1" % i)], writes=[pPb])
                for j in range(4):
                    g = j // 2
                    jsl = slice(j * 128, (j + 1) * 128)
                    Uvp = Xbf[:, 1, 2 * j:2 * j + 2, :]
                    P.op("pe", (lambda e, Uvp=Uvp, i=i, jsl=jsl, pQ=pQ: e.matmul(pQ[:, jsl], lhsT=tokm[:, i, 1, jsl], rhs=Uvp,
                                                                        start=True, stop=False)),
                         reads=[xbfg[g], B("tokm%d_1" % i)], writes=[pQb])
                    P.op("pe", (lambda e, i=i, jsl=jsl, pQ=pQ: e.matmul(pQ[:, jsl], lhsT=tokm[:, i, 2, jsl], rhs=tokm[:, i, 3, jsl],
                                                               start=False, stop=True)),
                         reads=[B("tokm%d_2" % i), B("tokm%d_3" % i)], writes=[pQb])
                for j in range(4 if full else 0):
                    g = j // 2
                    W0p = Xbf[:, 0, 2 * j:2 * j + 2, :]
                    pg_, pgb_ = pG[j // 2]
                    for e_ in range(2):
                        h = 2 * j + e_
                        c0_ = (j % 2) * 256 + e_ * 128
                        P.op("pe", (lambda e, pg_=pg_, W0p=W0p, h=h, c0_=c0_: e.matmul(
                            pg_[:, c0_:c0_ + 128], lhsT=W0p, rhs=AY8[:, h, 0:128], start=True, stop=True)),
                            reads=[xbfg[g], ayb[h]], writes=[pgb_])
                for j in range(4):
                    jsl = slice(j * 128, (j + 1) * 128)
                    pblb = B("Pblk%d" % j)
                    qsb = B("Qs%d" % j)
                    g0b = B("G0T%d" % j)
                    pg_, pgb_ = pG[j // 2] if full else (None, None)
                    for e_ in range(2):
                        pbs = e_ * 64
                        c0_ = (j % 2) * 256 + e_ * 128
                        P.op("dve", (lambda e, pbs=pbs, j=j, i=i, pP=pP: e.scalar_tensor_tensor(
                            out=Pblk[pbs:pbs + 64, j, pbs:pbs + 64], in0=ident[pbs:pbs + 64, pbs:pbs + 64],
                            scalar=sc[pbs:pbs + 64, 3, j, i:i + 1], op0=ALU.mult,
                            in1=pP[pbs:pbs + 64, j * 128 + pbs:j * 128 + pbs + 64], op1=ALU.add)),
                            reads=[pPb, B("ident"), B("sc%d" % j)], writes=[pblb])
                        P.op("act", (lambda e, pbs=pbs, j=j, i=i, pQ=pQ: e.activation(
                            out=Qs[pbs:pbs + 64, j, :], in_=pQ[pbs:pbs + 64, j * 128 + pbs:j * 128 + pbs + 64], func=AF.Copy,
                            scale=sc[pbs:pbs + 64, 2, j, i:i + 1])), reads=[pQb, B("sc%d" % j)], writes=[qsb])
                        if full:
                            P.op("dve", (lambda e, pg_=pg_, pbs=pbs, j=j, c0_=c0_, isl=isl: e.tensor_tensor(
                                out=G0T[pbs:pbs + 64, j, :], in0=pg_[pbs:pbs + 64, c0_:c0_ + 128],
                                in1=rt0[pbs:pbs + 64, j, isl], op=ALU.add)), reads=[pgb_, B("rt0_%d" % j)], writes=[g0b])
                for j in range(4):
                    g = j // 2
                    pblb = B("Pblk%d" % j)
                    qsb = B("Qs%d" % j)
                    g0b = B("G0T%d" % j)
                    hbuf = B("H%d" % j)
                    for e_ in range(2 if full else 0):
                        pbs = e_ * 64
                        h = 2 * j + e_
                        yb_, ybb = YB[e_]
                        col = slice(j * 64, (j + 1) * 64)
                        P.op("pe", (lambda e, yb_=yb_, col=col, h=h, st_=ystart[e_]: e.matmul(
                            yb_[:, col], lhsT=AY8[:, h, 0:128], rhs=Xbf[:, 1, h, :], start=st_, stop=False,
                            skip_group_check=True)), reads=[ayb[h], xbfg[g]], writes=[ybb])
                        ystart[e_] = False
                        P.op("pe", (lambda e, yb_=yb_, col=col, h=h, i=i: e.matmul(
                            yb_[:, col], lhsT=AY8[:, h, 128:256], rhs=tokm[:, i, 3, h * 64:(h + 1) * 64],
                            start=False, stop=False, skip_group_check=True)),
                            reads=[ayb[h], B("tokm%d_3" % i)], writes=[ybb])
                        P.op("pe", (lambda e, yb_=yb_, col=col, pbs=pbs, j=j: e.matmul(
                            yb_[:, col], lhsT=G0T[pbs:pbs + 64, j, :], rhs=Hst[pbs:pbs + 64, j, :], start=False, stop=True,
                            skip_group_check=True)), reads=[g0b, hbuf], writes=[ybb])
                    pH, pHb = ps_next()
                    P.op("pe", (lambda e, pH=pH, j=j: e.matmul(pH[:, 0:64], lhsT=Pblk[:, j, :], rhs=Hst[:, j, :],
                                                               start=True, stop=True)),
                         reads=[pblb, hbuf], writes=[pHb])
                    P.op("dve", (lambda e, pH=pH, j=j, i=i: e.scalar_tensor_tensor(
                        out=Hst[:, j, :], in0=pH[:, 0:64], scalar=sc[:, 2, j, i:i + 1], op0=ALU.mult, in1=Qs[:, j, :],
                        op1=ALU.add)), reads=[pHb, B("sc%d" % j), qsb, hbuf], writes=[hbuf])
                    yield
                if not full:
                    continue
                ysb4 = Ysb[:, :].rearrange("p (j e d) -> p j e d", j=4, e=2)
                for e_ in range(2):
                    yb_, ybb = YB[e_]
                    P.op("act", (lambda e, yb_=yb_, e_=e_: e.activation(
                        out=ysb4[:, :, e_, :], in_=yb_[:, 0:256].rearrange("p (a b) -> p a b", a=4), func=AF.Copy)),
                        reads=[ybb], writes=[B("Ysb")])
                y3 = Ysb[:, :].rearrange("p (a b) -> p a b", b=64)
                P.op("dve", lambda e: e.tensor_reduce(out=gst[:, 0, :], in_=y3, op=ALU.add, axis=AX.X),
                     reads=[B("Ysb")], writes=[B("gst")])
                P.op("act", lambda e: e.activation(out=sqY[:], in_=Ysb[:], func=AF.Square), reads=[B("Ysb")], writes=[B("yn")])
                P.op("dve", lambda e: e.tensor_reduce(out=gst[:, 1, :], in_=sqY[:, :].rearrange("p (a b) -> p a b", b=64),
                                                      op=ALU.add, axis=AX.X), reads=[B("yn")], writes=[B("gst")])
                P.op("dve", lambda e: e.tensor_scalar(out=gst[:, 2, :], in0=gst[:, 0, :], scalar1=1.0 / 64, scalar2=None,
                                                      op0=ALU.mult), reads=[B("gst")], writes=[B("gst")])
                P.op("dve", lambda e: e.tensor_tensor(out=gst[:, 3, :], in0=gst[:, 2, :], in1=gst[:, 2, :], op=ALU.mult),
                     reads=[B("gst")], writes=[B("gst")])
                P.op("dve", lambda e: e.scalar_tensor_tensor(out=gst[:, 4, :], in0=gst[:, 1, :], scalar=1.0 / 64, op0=ALU.mult,
                                                             in1=gst[:, 3, :], op1=ALU.subtract),
                     reads=[B("gst")], writes=[B("gst")])
                P.op("act", lambda e: e.activation(out=gst[:, 4, :], in_=gst[:, 4, :], func=AF.Ln, bias=epsb[:, 1:2]),
                     reads=[B("gst"), B("epsb")], writes=[B("gst")])
                P.op("act", lambda e: e.activation(out=gst[:, 5, :], in_=gst[:, 4, :], func=AF.Exp, scale=-0.5),
                     reads=[B("gst")], writes=[B("gst")])
                yn3 = yn[:, :].rearrange("p (a b) -> p a b", b=64)
                P.op("dve", lambda e: e.tensor_tensor(out=yn3, in0=y3, in1=gst[:, 2, :].unsqueeze(2).broadcast_to([128, 8, 64]),
                                                      op=ALU.subtract), reads=[B("Ysb"), B("gst")], writes=[B("yn")])
                P.op("dve", lambda e: e.tensor_tensor(out=yn3, in0=yn3, in1=gst[:, 5, :].unsqueeze(2).broadcast_to([128, 8, 64]),
                                                      op=ALU.mult), reads=[B("yn"), B("gst")], writes=[B("yn")])
                pT_, pTb = ps_next()
                for j in range(4):
                    P.op("pe", (lambda e, pT_=pT_, j=j: e.transpose(pT_[:, j * 128:(j + 1) * 128], yn[:, j * 128:(j + 1) * 128],
                                                                   ident[:])), reads=[B("yn"), B("ident")], writes=[pTb])
                for j in range(4):
                    P.op("dve", (lambda e, pT_=pT_, j=j, isl=isl: e.tensor_scalar(
                        out=ynT[:, j, isl], in0=pT_[:, j * 128:(j + 1) * 128], scalar1=rwp[:, 4, j:j + 1],
                        scalar2=rwp[:, 5, j:j + 1], op0=ALU.mult, op1=ALU.add)),
                        reads=[pTb, B("rwp")], writes=[B("ynT%d" % j)])
                yield
            for j in range(4 if full else 0):
                pbn, pbnb = ps_next()
                P.op("pe", (lambda e, pbn=pbn, j=j: e.matmul(pbn[:, 0:NT], lhsT=bones[:], rhs=prod[:, j, :], start=True,
                                                             stop=True)), reads=[B("bones"), B("prod%d" % j)], writes=[pbnb])
                P.op("dve", (lambda e, pbn=pbn, j=j: e.tensor_tensor(out=T1[:, 18, :], in0=pbn[:, 0:NT], in1=zin(8 + j),
                                                                     op=ALU.mult)), reads=[pbnb, zb[8 + j]], writes=[tb[18]])
                P.op("pool", (lambda e, j=j: e.tensor_tensor(out=T1[:, 18, :], in0=T1[:, 18, :], in1=ynT[:, j, :], op=ALU.add)),
                     reads=[tb[18], B("ynT%d" % j)], writes=[tb[18]])
                P.op("pool", (lambda e, j=j: e.tensor_tensor(out=yrT[:, j, :], in0=T1[:, 18, :], in1=gT[:, j, :], op=ALU.mult)),
                     reads=[tb[18], B("gT%d" % j)], writes=[B("yrT%d" % j)])
                yield

        mrg = sb("mrg", [128, NKC, NT], BF16)
        mtmp = rtmp
        mb = [B("mrg%d" % c) for c in range(NKC)]

        def branch_chunk(wname, o, rhs_tile, rhs_bufs):
            w, wb = wload(wname, o, 4 * 128)
            pt, pbuf = ps_next()
            for kc in range(4):
                P.op("pe", (lambda e, pt=pt, w=w, kc=kc: e.matmul(
                    pt[:, 0:NT], lhsT=w[:, kc * 128:(kc + 1) * 128], rhs=rhs_tile[:, kc, :],
                    start=(kc == 0), stop=(kc == 3))), reads=[wb] + rhs_bufs, writes=[pbuf])
            return pt, pbuf

        def merge_out(st):
            hsel = st % 2
            for q_ in range(2):
                P.op("act", (lambda e, q_=q_: e.activation(out=gates[:, q_ * 8:(q_ + 1) * 8, :], in_=gates[:, q_ * 8:(q_ + 1) * 8, :],
                                                           func=AF.Sigmoid)),
                     reads=gb[q_ * 8:(q_ + 1) * 8], writes=gb[q_ * 8:(q_ + 1) * 8])
            yrb = [B("yrT%d" % c) for c in range(4)]
            for o in range(NKC):
                par = o % 2
                pa_, pab_ = branch_chunk("watt", o, yattT, [B("yattT")])
                P.op("dve", (lambda e, pa_=pa_, o=o, par=par: e.tensor_tensor(out=mtmp[:, par, 0, :], in0=pa_[:, 0:NT],
                                                                              in1=gates[:, o, :], op=ALU.mult)),
                     reads=[pab_, gb[o]], writes=[B("rtmp%d0" % par)])
                pr_, prb_ = branch_chunk("wrw", o, yrT, yrb)
                P.op("dve", (lambda e, pr_=pr_, o=o, par=par: e.tensor_tensor(out=mtmp[:, par, 1, :], in0=pr_[:, 0:NT],
                                                                              in1=gates[:, 8 + o, :], op=ALU.mult)),
                     reads=[prb_, gb[8 + o]], writes=[B("rtmp%d1" % par)])
                P.op("pool", (lambda e, o=o, par=par: e.tensor_tensor(out=mrg[:, o, :], in0=mtmp[:, par, 0, :],
                                                                      in1=mtmp[:, par, 1, :], op=ALU.add)),
                     reads=[B("rtmp%d0" % par), B("rtmp%d1" % par)], writes=[mb[o]])
            for c in range(NKC):
                w, wb = wload("wmix", c, NKC * 128)
                pt, pbuf = ps_next()
                for kc in range(NKC):
                    P.op("pe", (lambda e, pt=pt, w=w, kc=kc: e.matmul(
                        pt[:, 0:NT], lhsT=w[:, kc * 128:(kc + 1) * 128], rhs=mrg[:, kc, :],
                        start=(kc == 0), stop=(kc == NKC - 1))), reads=[wb, mb[kc]], writes=[pbuf])
                P.op("act", (lambda e, pt=pt, c=c: e.activation(out=fT[:, c, :], in_=pt[:, 0:NT], func=AF.Copy)),
                     reads=[pbuf], writes=[fb[c]])
            postnorm_add(3, False, hsel)

        outb = B("outT")

        def run_all(g):
            for _ in g:
                pass

        def interleave(ga, gb, ra, rb):
            if os.environ.get("KINT", "1") == "0":
                run_all(ga)
                run_all(gb)
                return
            la = lb = True
            while la or lb:
                for _ in range(ra):
                    if la:
                        try:
                            next(ga)
                        except StopIteration:
                            la = False
                for _ in range(rb):
                    if lb:
                        try:
                            next(gb)
                        except StopIteration:
                            lb = False

        def ffn1_front(st):
            hsel = st % 2
            ht, hbs = HT[hsel]
            t0 = st * NT
            P.dma("sp", "xin", (lambda e, t0=t0, ht=ht: e.dma_start(out=ht[:], in_=xT[:, :, t0:t0 + NT])),
                  writes=hbs)
            prenorm(0, hsel)
            yield
            yield from ffn("gu1", "d1")
            postnorm_add(1, True, hsel)
            yield

        def until(g, sentinel):
            for v in g:
                if v == sentinel:
                    return True
            return False

        def interleave2(ga, gb, ra, rb, stop_b=None):
            la = lb = True
            while la or lb:
                for _ in range(ra):
                    if la:
                        try:
                            next(ga)
                        except StopIteration:
                            la = False
                for _ in range(rb):
                    if lb:
                        try:
                            if next(gb) == stop_b and stop_b is not None:
                                lb = False
                        except StopIteration:
                            lb = False

        def chain(*gs):
            for g in gs:
                yield from g

        run_all(ffn1_front(0))
        inp_g = mixer_inproj(0, npre == 0)
        until(inp_g, "RWKV_IN_DONE") if npre == 0 else run_all(inp_g)
        for st in range(nsup):
            full = st >= npre
            nxt = st + 1
            nfull = nxt >= npre
            if not full:
                nin = mixer_inproj(nxt, nfull)

                def nxt_front(nin=nin, nxt=nxt):
                    yield from ffn1_front(nxt)
                    for v in nin:
                        if v == "RWKV_IN_DONE":
                            return
                        yield
                interleave2(rwkv(st, False), nxt_front(), 3, 1)
                inp_g = nin
                continue
            rg = rwkv(st, True)
            interleave2(chain(inp_g, attention(st)), rg, 1, 2, stop_b="PREP_DONE")
            if nxt < nsup:
                interleave2(ffn1_front(nxt), rg, 1, 3)
            else:
                run_all(rg)
            merge_out(st)
            prenorm(4, st % 2)
            run_all(ffn("gu2", "d2"))
            postnorm_add(5, True, st % 2)
            ht, hbs = HT[st % 2]
            o0 = (st - npre) * NT
            P.dma("sp", "xout", (lambda e, o0=o0, ht=ht: e.dma_start(out=outT[:, :, o0:o0 + NT], in_=ht[:])),
                  reads=hbs, writes=[outb])
            if nxt < nsup:
                inp_g = mixer_inproj(nxt, True)
                until(inp_g, "RWKV_IN_DONE")
        P.final_wait("sp", [outb])
        P.emit(block)
    return nc


def _tile_w(w, kdim, ncols_chunks):
    K_, N_ = w.shape
    kc = K_ // 128
    nj = N_ // 128
    t = w.reshape(kc, 128, nj, 128).transpose(2, 1, 0, 3)
    return np.ascontiguousarray(t.reshape(nj, 128, kc * 128))


def _prep_weights(inp):
    out = {}
    for i, (gu, dn) in enumerate((("ffn1_w_gate_up", "ffn1_w_down"), ("ffn2_w_gate_up", "ffn2_w_down"))):
        w = np.asarray(inp[gu][0])
        g = _tile_w(w[:, :DFF], D, NHC)
        u = _tile_w(w[:, DFF:], D, NHC)
        out["gu%d" % (i + 1)] = np.ascontiguousarray(np.concatenate([g, u], axis=2))
        out["d%d" % (i + 1)] = _tile_w(np.asarray(inp[dn][0]), DFF, NKC)
    win = np.asarray(inp["w_in"][0])
    def perm_heads(wc, nh):
        wc = wc.reshape(D, nh, 64)
        idx = np.concatenate([np.arange(8, 16), np.arange(0, 8), np.arange(16, 64)])
        return wc[:, :, idx].reshape(D, nh * 64)
    wq = win[:, 0:512]
    wk = win[:, 512:640]
    wv = win[:, 640:768]
    wk_sw = np.concatenate([wk[:, 64:], wk[:, :64]], axis=1)
    cols = [wq, perm_heads(wq, 8), wk, perm_heads(wk, 2), wk_sw, perm_heads(wk_sw, 2), wv,
            win[:, 768:768 + 1792], win[:, 2560:4608]]
    wext = np.concatenate(cols, axis=1)
    assert wext.shape[1] == NIN * 128
    out["win"] = _tile_w(wext, D, NIN)
    out["watt"] = _tile_w(np.asarray(inp["w_att_branch"][0]), 512, NKC)
    out["wrw"] = _tile_w(np.asarray(inp["w_rwkv_branch"][0]), 512, NKC)
    out["wmix"] = _tile_w(np.asarray(inp["w_mix_out"][0]), D, NKC)
    gl = ["ffn1_norm_pre", "ffn1_norm_post", "mix_norm_pre", "mix_norm_post", "ffn2_norm_pre", "ffn2_norm_post"]
    gains = np.stack([np.asarray(inp[g][0]).reshape(NKC, 128).T for g in gl], axis=1)
    out["gains"] = np.ascontiguousarray(gains.astype(np.float32))
    return out


def _prep_consts(inp, nmain, main0, special_masks):
    out = {}
    ntok = nmain * NT
    pos = (np.arange(ntok) + main0 * NT - FRONT).astype(np.float64)
    inv = 1.0 / (500000.0 ** (np.arange(8, dtype=np.float64) * (2.0 / 16.0)))
    C = np.ones((128, ntok), np.float64)
    S = np.zeros((128, ntok), np.float64)
    for r in range(128):
        d = r % 64
        if d < 16:
            ang = pos * inv[d % 8]
            C[r] = np.cos(ang)
            S[r] = -np.sin(ang) if d < 8 else np.sin(ang)
    cs = np.stack([C, S], axis=1).reshape(128, 2, nmain, NT).transpose(2, 0, 1, 3)
    out["rope"] = np.ascontiguousarray(cs.astype(np.float32))
    j = np.arange(128)[:, None]
    i = np.arange(128)[None, :]
    if special_masks:
        m = [(j <= i), (j > i), (j <= i) & (j >= 112), (j > i) & (j >= 112), np.zeros((128, 128), bool), (j < i)]
    else:
        m = [(j <= i), (j > i), (j <= i), (j > i), (j > i), (j < i)]
    mk = np.stack([np.tile(a.astype(np.float32), (1, 4)) for a in m], axis=1)
    out["masks"] = np.ascontiguousarray(mk)
    out["ident"] = np.eye(128, dtype=np.float32)
    bo = np.zeros((128, 128), np.float32)
    bo[:64, :64] = 1.0
    bo[64:, 64:] = 1.0
    out["bones"] = bo
    def pc(name, n):
        return np.asarray(inp[name][0]).reshape(n, 128).T
    kinds = ["rwkv_w0", "rwkv_a0", "rwkv_k_k", "rwkv_k_a", "rwkv_ln_w", "rwkv_ln_b"]
    rwp = [pc(k, 4) for k in kinds] + [np.asarray(inp["rwkv_r_k"][0]).reshape(4, 128).T, np.zeros((128, 4), np.float32)]
    out["rwp"] = np.ascontiguousarray(np.stack(rwp, axis=1).astype(np.float32))
    out["mu"] = np.ascontiguousarray(pc("rwkv_mu", 14).astype(np.float32))
    lw = np.zeros((128, 3, 512), np.float32)
    lw[:64, 0] = np.asarray(inp["rwkv_w2"][0])
    lw[64:, 1] = np.asarray(inp["rwkv_a2"][0])
    lw[:, 2] = np.asarray(inp["rwkv_g2"][0])
    out["loraw"] = lw
    out["sinks"] = np.ascontiguousarray(np.tile(np.asarray(inp["att_sinks"][0]).reshape(1, 8), (128, 1)).astype(np.float32))
    return out


def _padded_seq(x_b, meta, nsup):
    ntok = nsup * NT
    seq = np.zeros((ntok, D), np.float32)
    seq[FRONT:FRONT + NMETA] = meta
    n = min(x_b.shape[0], ntok - FRONT - NMETA)
    seq[FRONT + NMETA:FRONT + NMETA + n] = x_b[:n]
    return seq


def _fm(seq):
    return np.ascontiguousarray(seq.reshape(seq.shape[0], NKC, 128).transpose(2, 1, 0))


def kernel(**inp):
    x = np.asarray(inp["x"])
    meta = np.asarray(inp["meta_tokens"])
    wts = _prep_weights(inp)
    cA = _prep_consts(inp, NMAIN, 0, True)
    cB = _prep_consts(inp, NMAIN, NPRE, False)
    nc = build_program(NPRE, NMAIN)
    in_maps = []
    for c in range(8):
        b, half = c // 2, c % 2
        seq = _padded_seq(x[b], meta, NSUP_FULL)
        m = dict(wts)
        if half == 0:
            m.update(cA)
            m["xT"] = _fm(np.concatenate([np.zeros((NPRE * NT, D), np.float32), seq[:NMAIN * NT]], axis=0))
        else:
            m.update(cB)
            m["xT"] = _fm(seq)
        in_maps.append(m)
    res = run_bass_kernel_spmd(nc, in_maps, core_ids=list(range(8)))
    outs = []
    for b in range(4):
        oA = res.results[2 * b]["outT"].transpose(2, 1, 0).reshape(-1, D)
        oB = res.results[2 * b + 1]["outT"].transpose(2, 1, 0).reshape(-1, D)
        full = np.concatenate([oA, oB[(NMAIN - NPRE) * NT:]], axis=0)
        outs.append(full[FRONT + NMETA:FRONT + NMETA + SEQ])
    return np.stack(outs, axis=0).astype(np.float32)
```
